# Optimizing a Trainium2 kernel written in Bass

```python
import math
import jax, jax.numpy as jnp
from jax import lax
import numpy as np

D_MODEL = 1024
BATCH = 8
SEQ = 4096
DEPTH = 4
DEC_BATCH = 2
DEC_SEQ = 16384
PAST_LEN = 128

F_WIDTH = D_MODEL // 4
F_GROUPS = 4
F_GROUP = F_WIDTH // F_GROUPS
S_WIDTH = D_MODEL // 4
S_CH = 16
S_GROUPS = S_WIDTH // S_CH
S_STATE = 64
DT_MIN = 1e-3
DT_MAX = 1e-1
N_HEADS = 8
QK_NOPE = 64
QK_ROPE = 32
V_DIM = 64
Q_LORA = 256
KV_LORA = 128
A_WIDTH = N_HEADS * V_DIM
ROPE_THETA = 10000.0
Q_BLOCK = 128
N_BRANCH = 3
D_FF = (-((-8 * D_MODEL) // (3 * 256))) * 256
ALPHA = (2 * DEPTH) ** 0.25
BETA = (8 * DEPTH) ** -0.25
EPS = 1e-5

IN_COLS = F_WIDTH + S_WIDTH + Q_LORA + KV_LORA + QK_ROPE + N_BRANCH * D_MODEL
SPLIT_IDX = [F_WIDTH,
             F_WIDTH + S_WIDTH,
             F_WIDTH + S_WIDTH + Q_LORA,
             F_WIDTH + S_WIDTH + Q_LORA + KV_LORA,
             F_WIDTH + S_WIDTH + Q_LORA + KV_LORA + QK_ROPE]

kernel_name = "hybrid_fnet_s5_mla_encoder"


def _layer_norm(x, g, b):
    xf = x.astype(jnp.float32)
    mu = jnp.mean(xf, axis=-1, keepdims=True)
    var = jnp.mean(jnp.square(xf - mu), axis=-1, keepdims=True)
    return ((xf - mu) * lax.rsqrt(var + EPS) * g.astype(jnp.float32) + b.astype(jnp.float32)).astype(x.dtype)


def _rms_norm(x, g):
    xf = x.astype(jnp.float32)
    ms = jnp.mean(jnp.square(xf), axis=-1, keepdims=True)
    return (xf * lax.rsqrt(ms + EPS) * g.astype(jnp.float32)).astype(x.dtype)


def _rope_tables(L):
    pos = jnp.arange(L, dtype=jnp.float32)
    inv = ROPE_THETA ** (-jnp.arange(0, QK_ROPE, 2, dtype=jnp.float32) / QK_ROPE)
    ang = pos[:, None] * inv[None, :]
    return jnp.cos(ang), jnp.sin(ang)


def _apply_rope(t, cos, sin):
    half = QK_ROPE // 2
    t1 = t[..., :half].astype(jnp.float32)
    t2 = t[..., half:].astype(jnp.float32)
    return jnp.concatenate([t1 * cos - t2 * sin, t1 * sin + t2 * cos], axis=-1).astype(t.dtype)


def _fourier(h):
    Bt, L, _ = h.shape
    hg = h.astype(jnp.float32).reshape(Bt, L, F_GROUPS, F_GROUP)
    f = jnp.real(jnp.fft.fft2(hg, axes=(1, 3), norm="ortho"))
    return f.reshape(Bt, L, F_WIDTH).astype(h.dtype)


def _scan_combine(e1, e2):
    a1, b1 = e1
    a2, b2 = e2
    return a1 * a2, a2 * b1 + b2


def _s5(u, a_re, a_im, log_dt, b_re, b_im, c_re, c_im, d, w_glu):
    Bt, L, _ = u.shape
    uf = u.astype(jnp.float32)
    uc = uf.reshape(Bt, L, S_GROUPS, S_CH).astype(jnp.complex64)
    y = uf * d.astype(jnp.float32)
    for k in range(2):
        lam = lax.complex(a_re[k].astype(jnp.float32), a_im[k].astype(jnp.float32))
        dt = jnp.exp(log_dt[k].astype(jnp.float32))[:, None]
        abar = jnp.exp(lam * dt)
        bmat = lax.complex(b_re[k].astype(jnp.float32), b_im[k].astype(jnp.float32))
        bbar = ((abar - 1.0) / lam)[..., None] * bmat
        bu = jnp.einsum('blgc,gnc->blgn', uc, bbar)
        a_seq = jnp.broadcast_to(abar[None, None], (1, L, S_GROUPS, S_STATE))
        _, states = lax.associative_scan(_scan_combine, (a_seq, bu), axis=1, reverse=(k == 1))
        cmat = lax.complex(c_re[k].astype(jnp.float32), c_im[k].astype(jnp.float32))
        y = y + jnp.real(jnp.einsum('blgn,gcn->blgc', states, cmat)).reshape(Bt, L, S_WIDTH)
    g = jax.nn.gelu(y)
    out = g * jax.nn.sigmoid(g @ w_glu.astype(jnp.float32))
    return out.astype(u.dtype)


def _mla(c_q, c_kv, k_pe, q_norm_g, w_uq, kv_norm_g, w_ukv):
    Bt, L, _ = c_q.shape
    cos, sin = _rope_tables(L)
    q = (_rms_norm(c_q, q_norm_g) @ w_uq).reshape(Bt, L, N_HEADS, QK_NOPE + QK_ROPE)
    q_nope = q[..., :QK_NOPE]
    q_pe = _apply_rope(q[..., QK_NOPE:], cos[:, None, :], sin[:, None, :])
    kv = (_rms_norm(c_kv, kv_norm_g) @ w_ukv).reshape(Bt, L, N_HEADS, QK_NOPE + V_DIM)
    k_nope = kv[..., :QK_NOPE]
    v = kv[..., QK_NOPE:]
    k_pe = _apply_rope(k_pe, cos, sin)
    scale = (QK_NOPE + QK_ROPE) ** -0.5
    nb = L // Q_BLOCK
    qn_b = jnp.moveaxis(q_nope.reshape(Bt, nb, Q_BLOCK, N_HEADS, QK_NOPE), 1, 0)
    qp_b = jnp.moveaxis(q_pe.reshape(Bt, nb, Q_BLOCK, N_HEADS, QK_ROPE), 1, 0)

    def block(qs):
        qn, qp = qs
        s = jnp.einsum('bqhd,bkhd->bhqk', qn, k_nope) + jnp.einsum('bqhr,bkr->bhqk', qp, k_pe)
        p = jax.nn.softmax(s.astype(jnp.float32) * scale, axis=-1).astype(v.dtype)
        return jnp.einsum('bhqk,bkhd->bqhd', p, v)

    o = lax.map(block, (qn_b, qp_b))
    return jnp.moveaxis(o, 0, 1).reshape(Bt, L, A_WIDTH)


def _trunk(x, w_in, b_gate, q_norm_g, w_uq, kv_norm_g, w_ukv,
           s5_a_re, s5_a_im, s5_log_dt, s5_b_re, s5_b_im, s5_c_re, s5_c_im, s5_d, w_glu,
           w_br_f, w_br_s, w_br_a, w_o, ln1_g, ln1_b, w_gu, w_down, ln2_g, ln2_b):
    Bt, L, _ = x.shape
    for l in range(DEPTH):
        proj = x @ w_in[l]
        h_f, h_s, c_q, c_kv, k_pe, gate_logits = jnp.split(proj, SPLIT_IDX, axis=-1)
        br_f = _fourier(h_f) @ w_br_f[l]
        br_s = _s5(h_s, s5_a_re[l], s5_a_im[l], s5_log_dt[l], s5_b_re[l], s5_b_im[l],
                   s5_c_re[l], s5_c_im[l], s5_d[l], w_glu[l]) @ w_br_s[l]
        br_a = _mla(c_q, c_kv, k_pe, q_norm_g[l], w_uq[l], kv_norm_g[l], w_ukv[l]) @ w_br_a[l]
        gates = jax.nn.sigmoid(gate_logits.reshape(Bt, L, N_BRANCH, D_MODEL) + b_gate[l])
        merged = gates[:, :, 0] * br_f + gates[:, :, 1] * br_s + gates[:, :, 2] * br_a
        x = _layer_norm(ALPHA * x + merged @ w_o[l], ln1_g[l], ln1_b[l])
        gu = x @ w_gu[l]
        ffn = (jax.nn.silu(gu[..., :D_FF]) * gu[..., D_FF:]) @ w_down[l]
        x = _layer_norm(ALPHA * x + ffn, ln2_g[l], ln2_b[l])
    return x


def setup_inputs(seed: int = 0) -> dict:
    key = jax.random.key(seed)
    ks = jax.random.split(key, 32)
    f32 = jnp.float32

    def nrm(k, shape, scale):
        return jax.random.normal(k, shape, f32) * scale

    G, N, C = S_GROUPS, S_STATE, S_CH
    inp = {}
    inp["x_prompt"] = nrm(ks[0], (BATCH, SEQ, D_MODEL), 1.0)
    inp["x_sample"] = nrm(ks[1], (DEC_BATCH, DEC_SEQ, D_MODEL), 1.0)
    inp["w_in"] = nrm(ks[2], (DEPTH, D_MODEL, IN_COLS), D_MODEL ** -0.5)
    inp["b_gate"] = nrm(ks[3], (DEPTH, N_BRANCH, D_MODEL), 0.01)
    inp["q_norm_g"] = 1.0 + nrm(ks[4], (DEPTH, Q_LORA), 0.01)
    inp["w_uq"] = nrm(ks[5], (DEPTH, Q_LORA, N_HEADS * (QK_NOPE + QK_ROPE)), Q_LORA ** -0.5)
    inp["kv_norm_g"] = 1.0 + nrm(ks[6], (DEPTH, KV_LORA), 0.01)
    inp["w_ukv"] = nrm(ks[7], (DEPTH, KV_LORA, N_HEADS * (QK_NOPE + V_DIM)), KV_LORA ** -0.5)
    inp["s5_a_re"] = -0.5 * jnp.exp(nrm(ks[8], (DEPTH, 2, G, N), 0.02))
    inp["s5_a_im"] = math.pi * jnp.arange(N, dtype=f32) + nrm(ks[9], (DEPTH, 2, G, N), 0.02)
    inp["s5_log_dt"] = jax.random.uniform(ks[10], (DEPTH, 2, G), f32,
                                          minval=math.log(DT_MIN), maxval=math.log(DT_MAX))
    inp["s5_b_re"] = nrm(ks[11], (DEPTH, 2, G, N, C), (2 * C) ** -0.5)
    inp["s5_b_im"] = nrm(ks[12], (DEPTH, 2, G, N, C), (2 * C) ** -0.5)
    inp["s5_c_re"] = nrm(ks[13], (DEPTH, 2, G, C, N), N ** -0.5)
    inp["s5_c_im"] = nrm(ks[14], (DEPTH, 2, G, C, N), N ** -0.5)
    inp["s5_d"] = nrm(ks[15], (DEPTH, S_WIDTH), 1.0)
    inp["w_glu"] = nrm(ks[16], (DEPTH, S_WIDTH, S_WIDTH), S_WIDTH ** -0.5)
    inp["w_br_f"] = nrm(ks[17], (DEPTH, F_WIDTH, D_MODEL), F_WIDTH ** -0.5)
    inp["w_br_s"] = nrm(ks[18], (DEPTH, S_WIDTH, D_MODEL), S_WIDTH ** -0.5)
    inp["w_br_a"] = nrm(ks[19], (DEPTH, A_WIDTH, D_MODEL), A_WIDTH ** -0.5)
    inp["w_o"] = nrm(ks[20], (DEPTH, D_MODEL, D_MODEL), BETA * D_MODEL ** -0.5)
    inp["ln1_g"] = 1.0 + nrm(ks[21], (DEPTH, D_MODEL), 0.01)
    inp["ln1_b"] = nrm(ks[22], (DEPTH, D_MODEL), 0.01)
    inp["w_gu"] = nrm(ks[23], (DEPTH, D_MODEL, 2 * D_FF), D_MODEL ** -0.5)
    inp["w_down"] = nrm(ks[24], (DEPTH, D_FF, D_MODEL), BETA * D_FF ** -0.5)
    inp["ln2_g"] = 1.0 + nrm(ks[25], (DEPTH, D_MODEL), 0.01)
    inp["ln2_b"] = nrm(ks[26], (DEPTH, D_MODEL), 0.01)
    return inp


def reference(x_prompt, x_sample, w_in, b_gate, q_norm_g, w_uq, kv_norm_g, w_ukv,
              s5_a_re, s5_a_im, s5_log_dt, s5_b_re, s5_b_im, s5_c_re, s5_c_im, s5_d, w_glu,
              w_br_f, w_br_s, w_br_a, w_o, ln1_g, ln1_b, w_gu, w_down, ln2_g, ln2_b):
    y_prompt = _trunk(x_prompt, w_in, b_gate, q_norm_g, w_uq, kv_norm_g, w_ukv,
                      s5_a_re, s5_a_im, s5_log_dt, s5_b_re, s5_b_im, s5_c_re, s5_c_im, s5_d, w_glu,
                      w_br_f, w_br_s, w_br_a, w_o, ln1_g, ln1_b, w_gu, w_down, ln2_g, ln2_b)
    y_sample = _trunk(x_sample, w_in, b_gate, q_norm_g, w_uq, kv_norm_g, w_ukv,
                      s5_a_re, s5_a_im, s5_log_dt, s5_b_re, s5_b_im, s5_c_re, s5_c_im, s5_d, w_glu,
                      w_br_f, w_br_s, w_br_a, w_o, ln1_g, ln1_b, w_gu, w_down, ln2_g, ln2_b)
    return (y_prompt, y_sample)
```

```python
import contextlib
import math
import numpy as np
import concourse.bass as bass
import concourse.mybir as mybir
from concourse.bass_utils import run_bass_kernel_spmd

F32 = mybir.dt.float32
BF16 = mybir.dt.bfloat16
AF = mybir.ActivationFunctionType
ALU = mybir.AluOpType

D = 1024
H = 8
DFF = 2816
NFF = DFF // 128
ALPHA = (2 * 4) ** 0.25
EPS = 1e-5
PI = math.pi


class Dep:
    __slots__ = ("name", "w", "r", "sems", "wdma")

    def __init__(self, name=""):
        self.name = name
        self.w = None
        self.r = {}
        self.sems = {}
        self.wdma = None


class Tile:
    def __init__(self, t, name):
        self.t = t
        self.dep = Dep(name)
        self.name = name

    def __getitem__(self, key):
        return self.t[key]


class KB:
    ENG = ("pe", "act", "dve", "pool", "sp")

    def __init__(self):
        self.nc = bass.Bass("TRN2", target_bir_lowering=False)
        self.es = contextlib.ExitStack()
        self.streams = {e: [] for e in self.ENG}
        self.cnt = {e: 0 for e in self.ENG}
        self.seen = {e: {} for e in self.ENG}
        self.esem = {}
        for e in self.ENG:
            self.esem[e] = self.es.enter_context(self.nc.semaphore("s_" + e))
        self.nsem = 0
        self.dma_sems = []
        self.free_recs = {"sw": [], "hw": []}
        self.active = []
        self.semowner = {}
        self.uid = 0
        self.ninst = 0

    def sb(self, name, shape, dtype, stack=None):
        self.uid += 1
        st = stack if stack is not None else self.es
        t = st.enter_context(self.nc.sbuf_tensor(f"{name}_{self.uid}", list(shape), dtype))
        return Tile(t, name)

    def ps(self, name, shape, dtype=F32, stack=None):
        self.uid += 1
        st = stack if stack is not None else self.es
        t = st.enter_context(self.nc.psum_tensor(f"{name}_{self.uid}", list(shape), dtype))
        return Tile(t, name)

    def dram(self, name, shape, dtype, kind="Internal"):
        return self.nc.dram_tensor(name, list(shape), dtype, kind=kind)

    def _dsem(self, dep, q):
        if q not in dep.sems:
            if self.free_recs[q]:
                rec = self.free_recs[q].pop()
                rec[2] = False
            else:
                self.nsem += 1
                sem = self.es.enter_context(self.nc.semaphore(f"d{self.nsem}"))
                rec = [sem, 0, False]
                self.dma_sems.append(rec)
                self.semowner[id(sem)] = rec
            dep.sems[q] = rec
            self.active.append((dep, q))
        return dep.sems[q]

    def _need(self, eng, waits, ev):
        if ev is None:
            return
        sem, val = ev
        key = id(sem)
        ow = self.semowner.get(key)
        if ow is not None:
            ow[2] = True
        if self.seen[eng].get(key, 0) >= val:
            return
        if sem is self.esem.get(eng) and eng in ("pe", "sp"):
            return
        self.seen[eng][key] = val
        waits[key] = (sem, max(val, waits.get(key, (sem, 0))[1]))

    def _deps(self, eng, reads, writes, same_gen=None):
        waits = {}
        for d in reads:
            self._need(eng, waits, d.w)
        for d in writes:
            if not (same_gen is not None and d.wdma is same_gen):
                self._need(eng, waits, d.w)
            for key, (sem, val) in d.r.items():
                self._need(eng, waits, (sem, val))
        return list(waits.values())

    def _commit(self, ev, reads, writes):
        sem, val = ev
        k = id(sem)
        for d in reads:
            if d.r.get(k, (sem, 0))[1] < val:
                d.r[k] = (sem, val)
            d.wdma = None
        for d in writes:
            d.w = ev
            d.r = {}
            d.wdma = None

    @staticmethod
    def _d(x):
        out = []
        for v in x:
            if v is None:
                continue
            out.append(v.dep if isinstance(v, Tile) else v)
        return out

    def op(self, eng, fn, reads=(), writes=()):
        reads = self._d(reads)
        writes = self._d(writes)
        waits = self._deps(eng, reads, writes)
        self.cnt[eng] += 1
        ev = (self.esem[eng], self.cnt[eng])
        self.streams[eng].append((waits, fn, self.esem[eng], 1))
        self._commit(ev, reads, writes)
        self.ninst += 1
        return ev

    def dma(self, eng, out, in_, reads=(), writes=(), owner=None, **kw):
        reads = self._d(reads)
        writes = self._d(writes)
        owner = owner.dep if isinstance(owner, Tile) else owner
        rec = self._dsem(owner, "sw" if eng == "pool" else "hw")
        sem = rec[0]
        waits = self._deps(eng, reads, writes, same_gen=sem)
        if rec[2] and rec[1] > 0:
            w2 = {}
            self._need(eng, w2, (sem, rec[1]))
            if w2:
                waits = [x for x in waits if x[0] is not sem] + [(sem, rec[1])]
        rec[2] = False
        rec[1] += 16
        ev = (sem, rec[1])

        def fn(e, out=out, in_=in_, kw=kw):
            return e.dma_start(out=out, in_=in_, **kw)
        self.streams[eng].append((waits, fn, sem, 16))
        self._commit(ev, reads, writes)
        if owner in writes:
            owner.wdma = sem
        self.ninst += 1
        return ev

    def barrier(self):
        evs = [(self.esem[e], self.cnt[e]) for e in self.ENG if self.cnt[e] > 0]
        for rec in self.dma_sems:
            if rec[1] > 0:
                evs.append((rec[0], rec[1]))
        for e in self.ENG:
            waits = {}
            for ev in evs:
                self._need(e, waits, ev)
            if waits:
                self.streams[e].append((list(waits.values()), None, None, 0))

    def flush(self):
        self.barrier()
        nc = self.nc
        hmap = {"pe": "tensor", "act": "scalar", "dve": "vector", "pool": "gpsimd", "sp": "sync"}
        with nc.Block() as block:
            for e in self.ENG:
                stream = self.streams[e]

                def body(eng, stream=stream):
                    for waits, fn, sem, inc in stream:
                        for (s, v) in waits:
                            eng.wait_ge(s, v)
                        if fn is not None:
                            fn(eng).then_inc(sem, inc)
                getattr(block, hmap[e])(body)
        self.streams = {e: [] for e in self.ENG}
        for dep, q in self.active:
            self.free_recs[q].append(dep.sems.pop(q))
        self.active = []

    def finish(self):
        self.flush()
        self.es.close()
        return self.nc


def _consts_for_seq(L, N1, N2):
    c = {}
    n1 = np.arange(N1)[:, None]
    k1 = np.arange(N1)[None, :]
    ang = 2 * np.pi * n1 * k1 / N1
    c["FA"] = np.concatenate([np.cos(ang), -np.sin(ang)], axis=1).astype(np.float32)
    n2 = np.arange(N2)[:, None]
    ang = 2 * np.pi * n2 * np.arange(N1)[None, :] / L
    c["TW"] = np.stack([np.cos(ang), -np.sin(ang)], axis=1).astype(np.float32)
    k2 = np.arange(N2)[None, :]
    ang = 2 * np.pi * n2 * k2 / N2
    Cm, Sm = np.cos(ang), np.sin(ang)
    c["FC1"] = np.concatenate([Cm, -Sm], axis=1).astype(np.float32)
    c["FC2"] = np.concatenate([Sm, Cm], axis=1).astype(np.float32)
    pos = np.arange(L, dtype=np.float32)
    inv = (10000.0 ** (-np.arange(0, 32, 2, dtype=np.float32) / 32)).astype(np.float32)
    a = (pos[None, :] * inv[:, None]).astype(np.float32)
    cos2 = np.concatenate([np.cos(a), np.cos(a)], axis=0)
    sin2 = np.concatenate([-np.sin(a), np.sin(a)], axis=0)
    scale = 96 ** -0.5
    c["ROPE"] = np.stack([np.stack([cos2 * scale, sin2 * scale]), np.stack([cos2, sin2])]).astype(np.float32)
    return c


def _global_consts():
    c = {}
    c["ONES"] = np.ones((128, 128), np.float32)
    j = np.arange(64)
    ang = 2 * np.pi * j[:, None] * j[None, :] / 64
    C64 = np.zeros((128, 128), np.float32)
    S64 = np.zeros((128, 128), np.float32)
    for b in range(2):
        C64[b * 64:(b + 1) * 64, b * 64:(b + 1) * 64] = np.cos(ang)
        S64[b * 64:(b + 1) * 64, b * 64:(b + 1) * 64] = np.sin(ang)
    c["CS64"] = np.stack([C64, S64], axis=1).astype(np.float32)
    I2 = np.zeros((128, 64), np.float32)
    I2[np.arange(128), np.arange(128) % 64] = 1.0
    c["I2"] = I2
    I16 = np.zeros((16, 16), np.float32)
    I16[np.arange(16), np.arange(16)] = 1.0
    c["I16"] = I16
    m = np.zeros((128, 4), np.float32)
    for p in range(128):
        m[p, (p % 64) // 16] = 1.0
    c["EMASK"] = m
    return c


def _stage_params(inp, depth):
    o = {}
    f = lambda a: np.ascontiguousarray(np.asarray(a, dtype=np.float32))
    w_in = f(inp["w_in"])
    o["w_in"] = w_in
    kpe = w_in[:, :, 896:928]
    o["w_kpesw"] = f(np.concatenate([kpe[:, :, 16:32], kpe[:, :, 0:16]], axis=2))
    w_uq = f(inp["w_uq"]).reshape(depth, 256, H, 96)
    o["w_uq"] = f(w_uq.reshape(depth, 256, 768))
    sw = np.concatenate([w_uq[..., 0:64], w_uq[..., 80:96], w_uq[..., 64:80]], axis=-1)
    o["w_uq_sw"] = f(sw.reshape(depth, 256, 768))
    w_ukv = f(inp["w_ukv"]).reshape(depth, 128, H, 128)
    o["w_uk"] = f(w_ukv[..., 0:64].reshape(depth, 128, 512))
    o["w_uv"] = f(w_ukv[..., 64:128].reshape(depth, 128, 512))
    for n in ("w_glu", "w_br_f", "w_br_s", "w_br_a", "w_o", "w_gu", "w_down"):
        o[n] = f(inp[n])
    def pk(v, kc):
        return f(v).reshape(depth, kc, 128).transpose(0, 2, 1)
    bg = f(inp["b_gate"]).reshape(depth, 3 * 8, 128).transpose(0, 2, 1)
    o["vecs"] = f(np.concatenate([pk(inp["ln1_g"], 8), pk(inp["ln1_b"], 8), pk(inp["ln2_g"], 8), pk(inp["ln2_b"], 8),
                                  bg, pk(inp["q_norm_g"], 2), pk(inp["kv_norm_g"], 1)], axis=2))
    are, aim, ldt = f(inp["s5_a_re"]), f(inp["s5_a_im"]), f(inp["s5_log_dt"])
    def layA(a):
        t = a.reshape(depth, 32, 64).transpose(0, 2, 1)
        return np.concatenate([t, t], axis=1)
    ldt_full = np.broadcast_to(ldt[..., None], (depth, 2, 16, 64))
    o["s5A_a"] = f(np.stack([layA(are), layA(aim), layA(ldt_full)], axis=2))
    def layA3(b):
        t = b.reshape(depth, 32, 64, 16).transpose(0, 2, 1, 3)
        return np.concatenate([t, t], axis=1)
    o["s5A_b"] = f(np.stack([layA3(f(inp["s5_b_re"])), layA3(f(inp["s5_b_im"]))], axis=2))
    cre = f(inp["s5_c_re"]).transpose(0, 1, 2, 4, 3)
    cim = f(inp["s5_c_im"]).transpose(0, 1, 2, 4, 3)
    o["s5A_c"] = f(np.stack([layA3(cre), layA3(cim)], axis=2))
    def layB(a):
        t = np.repeat(a[:, :, :, None, :], 16, axis=3).reshape(depth, 2, 256, 64)
        t = t.reshape(depth, 2, 2, 128, 64).transpose(0, 3, 2, 1, 4)
        return t
    o["s5B_a"] = f(np.stack([layB(are), layB(aim), layB(ldt_full)], axis=2))
    def layBb(b):
        t = b.transpose(0, 1, 2, 4, 3).reshape(depth, 2, 256, 64)
        return t.reshape(depth, 2, 2, 128, 64).transpose(0, 3, 2, 1, 4)
    o["s5B_b"] = f(np.stack([layBb(f(inp["s5_b_re"])), layBb(f(inp["s5_b_im"]))], axis=2))
    o["s5_dT"] = f(f(inp["s5_d"]).reshape(depth, 16, 16).transpose(0, 2, 1))
    return o


WSHAPES = {
    "w_in": (1024, 4000), "w_kpesw": (1024, 32), "w_uq": (256, 768), "w_uq_sw": (256, 768),
    "w_uk": (128, 512), "w_uv": (128, 512), "w_glu": (256, 256), "w_br_f": (256, 1024),
    "w_br_s": (256, 1024), "w_br_a": (512, 1024), "w_o": (1024, 1024), "w_gu": (1024, 5632),
    "w_down": (2816, 1024),
}
PSHAPES = {"vecs": (128, 59), "s5A_a": (128, 3, 32), "s5A_b": (128, 2, 32, 16), "s5A_c": (128, 2, 32, 16),
           "s5B_a": (128, 3, 2, 2, 64), "s5B_b": (128, 2, 2, 2, 64), "s5_dT": (16, 16)}


class Prog:
    def __init__(self, cfg):
        self.cfg = cfg
        self.depth = cfg["depth"]
        self.seqs = cfg["seqs"]
        self.SEG = cfg["SEG"]
        self.debug = cfg.get("debug", False)
        self.k = KB()
        self.nc = self.k.nc
        self.dr = {}
        self.build()

    def din(self, name, shape, dtype=F32):
        t = self.k.dram(name, shape, dtype, kind="ExternalInput")
        self.dr[name] = t
        return t

    def dscr(self, name, shape, dtype, out=False):
        kind = "ExternalOutput" if (out or self.debug) else "Internal"
        t = self.k.dram(name, shape, dtype, kind=kind)
        self.dr[name] = t
        return t

    def dbg(self, name, tile, shape, dtype=F32):
        if not self.debug:
            return
        t = self.k.dram("dbg_" + name, list(shape), dtype, kind="ExternalOutput")
        self.dr["dbg_" + name] = t
        self.k.dma("sp", t.ap(), tile[:], reads=[tile], owner=tile)

    def bank(self, b):
        return self.pst[b // 2][:, (b % 2) * 512:(b % 2) * 512 + 512]

    def nextbank(self):
        b = self.bankrr
        self.bankrr = (self.bankrr + 1) % 8
        return b

    def build(self):
        k = self.k
        depth = self.depth
        for n, (K_, N_) in WSHAPES.items():
            self.din(n, [depth, K_, N_])
            self.dscr("b_" + n, [depth, K_, N_], BF16)
        for n, shp in PSHAPES.items():
            self.din(n, [depth] + list(shp))
        gc = _global_consts()
        for n, v in gc.items():
            self.din("c_" + n, list(v.shape))
        for si, s in enumerate(self.seqs):
            L = s["L"]
            self.din(f"xT{si}", [D, L])
            cs = _consts_for_seq(L, s["N1"], s["N2"])
            for n, v in cs.items():
                self.din(f"c{si}_{n}", list(v.shape))
            self.dscr(f"yT{si}", [D, L], F32, out=True)
            self.dscr(f"xb{si}", [D, L], F32)
            self.dscr(f"hf{si}", [4, L, 64], F32)
            self.dscr(f"hsT{si}", [256, L], F32)
            self.dscr(f"cqT{si}", [256, L], F32)
            self.dscr(f"ckvT{si}", [128, L], F32)
            self.dscr(f"kpeT{si}", [64, L], F32)
            self.dscr(f"fT{si}", [2, 256, L], BF16)
            self.dscr(f"s5T{si}", [256, L], BF16)
            self.dscr(f"attnT{si}", [512, L], BF16)
            self.dscr(f"qT{si}", [H, 96, L], BF16)
            self.dscr(f"kT{si}", [H, 96, L], BF16)
            self.dscr(f"v{si}", [L, 512], BF16)
            nseg = L // self.SEG
            self.dscr(f"sbst{si}", [nseg, 128, 16, self.SEG // 8 + 1], BF16)
        self.dscr("Kd", [16, 15, 16, 16], F32)

        self.pst = [k.ps(f"ps{i}", [128, 1024]) for i in range(4)]
        self.bdep = [Dep(f"bank{i}") for i in range(8)]
        self.bankrr = 0
        self.ones = k.sb("ones", [128, 128], F32)
        k.dma("sp", self.ones[:], self.dr["c_ONES"].ap(), writes=[self.ones], owner=self.ones)

        cd = Dep("cast")
        for n, (K_, N_) in WSHAPES.items():
            for l in range(depth):
                rstep = 256
                for r0 in range(0, K_, rstep):
                    r1 = min(K_, r0 + rstep)
                    k.dma("pool", self.dr["b_" + n].ap()[l, r0:r1, :], self.dr[n].ap()[l, r0:r1, :],
                          writes=[cd], owner=cd)
        k.flush()

        for l in range(depth):
            self.layer(l)

    def xsrc(self, l, si):
        return self.dr[f"xT{si}"] if l == 0 else self.dr[f"xb{si}"]

    def xdst(self, l, si):
        return self.dr[f"yT{si}"] if l == self.depth - 1 else self.dr[f"xb{si}"]

    def layer(self, l):
        k = self.k
        phases = self.cfg.get("phases", ("p1", "s5", "fourier", "mla", "p3"))
        if "p1" in phases:
            with contextlib.ExitStack() as st:
                self.p1_setup(l, st)
                for si in range(len(self.seqs)):
                    self.p1(l, si, st)
                k.flush()
        if "s5" in phases:
            with contextlib.ExitStack() as st:
                self.s5_prep(l, st)
                for si in range(len(self.seqs)):
                    self.s5_seq(l, si, st)
                k.flush()
        if "fourier" in phases:
            for si in range(len(self.seqs)):
                with contextlib.ExitStack() as st:
                    self.fourier(l, si, st)
                    k.flush()
        if "mla" in phases:
            for si in range(len(self.seqs)):
                with contextlib.ExitStack() as st:
                    self.mla_pre(l, si, st)
                    k.flush()
                with contextlib.ExitStack() as st:
                    self.attn(l, si, st)
                    k.flush()
        if "p3" in phases:
            with contextlib.ExitStack() as st:
                self.p3_setup(l, st)
                for si in range(len(self.seqs)):
                    self.p3(l, si, st)
                k.flush()

    def p1_setup(self, l, st):
        k = self.k
        self.W1 = k.sb("W1", [128, 8, 960], BF16, st)
        win = self.dr["b_w_in"].ap()[l].rearrange("(kc p) n -> p kc n", p=128)
        k.dma("sp", self.W1[:, :, 0:928], win[:, :, 0:928], writes=[self.W1], owner=self.W1)
        wsw = self.dr["b_w_kpesw"].ap()[l].rearrange("(kc p) n -> p kc n", p=128)
        k.dma("sp", self.W1[:, :, 928:960], wsw, writes=[self.W1], owner=self.W1)
        self.p1_xf = [k.sb(f"xf{i}", [128, 8, 512], F32, st) for i in range(2)]
        self.p1_xb = [k.sb(f"xbf{i}", [128, 8, 512], BF16, st) for i in range(2)]
        self.p1_st = [k.sb(f"p1st{i}", [128, 6, 512], F32, st) for i in range(2)]
        self.p1_hf = [k.sb(f"p1hf{i}", [128, 4, 256], F32, st) for i in range(2)]
        self.p1_i = 0

    def p1(self, l, si, st):
        k = self.k
        L = self.seqs[si]["L"]
        xs = self.xsrc(l, si).ap().rearrange("(kc p) t -> p kc t", p=128)
        W1 = self.W1
        for b in range(L // 512):
            i = self.p1_i % 2
            self.p1_i += 1
            xf, xb, stg, hfs = self.p1_xf[i], self.p1_xb[i], self.p1_st[i], self.p1_hf[i]
            ts = slice(b * 512, (b + 1) * 512)
            k.dma("sp", xf[:], xs[:, :, ts], writes=[xf], owner=xf)
            k.op("act", lambda e, xf=xf, xb=xb: e.activation(xb[:, 0:4, :], xf[:, 0:4, :], AF.Copy), reads=[xf], writes=[xb])
            k.op("pool", lambda e, xf=xf, xb=xb: e.tensor_copy(xb[:, 4:8, :], xf[:, 4:8, :]), reads=[xf], writes=[xb])
            for cc in range(6):
                c0 = 256 + cc * 128
                M = 128 if cc < 5 else 64
                bk = self.nextbank()
                for kc in range(8):
                    k.op("pe", lambda e, bk=bk, kc=kc, c0=c0, M=M, xb=xb: e.matmul(
                        self.bank(bk)[0:M, :], lhsT=W1[:, kc, c0:c0 + M], rhs=xb[:, kc, :],
                        start=(kc == 0), stop=(kc == 7)), reads=[W1, xb], writes=[self.bdep[bk]])
                eng = "act" if cc % 2 == 0 else "dve"
                if eng == "act":
                    k.op("act", lambda e, bk=bk, cc=cc, M=M, stg=stg: e.activation(stg[0:M, cc, :], self.bank(bk)[0:M, :], AF.Copy),
                         reads=[self.bdep[bk]], writes=[stg])
                else:
                    k.op("dve", lambda e, bk=bk, cc=cc, M=M, stg=stg: e.tensor_copy(stg[0:M, cc, :], self.bank(bk)[0:M, :]),
                         reads=[self.bdep[bk]], writes=[stg])
            k.dma("pool", self.dr[f"hsT{si}"].ap().rearrange("(c p) t -> p c t", p=128)[:, :, ts], stg[:, 0:2, :], reads=[stg], owner=stg)
            k.dma("pool", self.dr[f"cqT{si}"].ap().rearrange("(c p) t -> p c t", p=128)[:, :, ts], stg[:, 2:4, :], reads=[stg], owner=stg)
            k.dma("pool", self.dr[f"ckvT{si}"].ap()[:, ts], stg[:, 4, :], reads=[stg], owner=stg)
            k.dma("pool", self.dr[f"kpeT{si}"].ap()[:, ts], stg[0:64, 5, :], reads=[stg], owner=stg)
            for sub in range(4):
                bk = self.nextbank() if sub % 2 == 0 else bk
                o0 = (sub % 2) * 256
                for kc in range(8):
                    k.op("pe", lambda e, bk=bk, kc=kc, o0=o0, sub=sub, xb=xb: e.matmul(
                        self.bank(bk)[:, o0:o0 + 256], lhsT=xb[:, kc, sub * 128:(sub + 1) * 128], rhs=W1[:, kc, 0:256],
                        start=(kc == 0), stop=(kc == 7)), reads=[W1, xb], writes=[self.bdep[bk]])
                if sub % 2 == 1:
                    k.op("act", lambda e, bk=bk, sub=sub, hfs=hfs: e.activation(
                        hfs[:, sub - 1:sub + 1, :], self.bank(bk).rearrange("p (a b) -> p a b", a=2), AF.Copy),
                        reads=[self.bdep[bk]], writes=[hfs])
            hfd = self.dr[f"hf{si}"].ap().rearrange("g (n p) c -> p n g c", p=128)
            for g in range(4):
                k.dma("pool", hfd[:, b * 4:(b + 1) * 4, g, :], hfs[:, :, g * 64:(g + 1) * 64], reads=[hfs], owner=hfs)

    def cmul(self, eng, outr, outi, ar, ai, br, bi, t1, t2, reads, writes, tmps):
        k = self.k
        E = eng
        k.op(E, lambda e: e.tensor_tensor(t1, ar, br, ALU.mult), reads=reads, writes=tmps)
        k.op(E, lambda e: e.tensor_tensor(t2, ai, bi, ALU.mult), reads=reads, writes=tmps)
        k.op(E, lambda e: e.tensor_tensor(outr, t1, t2, ALU.subtract), reads=tmps, writes=writes)
        k.op(E, lambda e: e.tensor_tensor(t1, ar, bi, ALU.mult), reads=reads + writes, writes=tmps)
        k.op(E, lambda e: e.tensor_tensor(t2, ai, br, ALU.mult), reads=reads, writes=tmps)
        k.op(E, lambda e: e.tensor_tensor(outi, t1, t2, ALU.add), reads=tmps, writes=writes)

    def abar_calc(self, st, tag, src, F):
        k = self.k
        R = k.sb("abR" + tag, [128, 4, F], F32, st)
        T = k.sb("abT" + tag, [128, 8, F], F32, st)
        are, aim, ldt = src[:, 0, :], src[:, 1, :], src[:, 2, :]
        dt, xr, xi, mag, s, c, m, ab = [T[:, i, :] for i in range(8)]
        k.op("act", lambda e: e.activation(dt, ldt, AF.Exp), reads=[src], writes=[T])
        k.op("dve", lambda e: e.tensor_tensor(xr, are, dt, ALU.mult), reads=[src, T], writes=[T])
        k.op("dve", lambda e: e.tensor_tensor(xi, aim, dt, ALU.mult), reads=[src, T], writes=[T])
        k.op("act", lambda e: e.activation(mag, xr, AF.Exp), reads=[T], writes=[T])
        for _ in range(4):
            k.op("dve", lambda e: e.tensor_scalar(m, xi, PI, -2 * PI, ALU.is_gt, ALU.mult), reads=[T], writes=[T])
            k.op("dve", lambda e: e.tensor_tensor(xi, xi, m, ALU.add), reads=[T], writes=[T])
        k.op("act", lambda e: e.activation(s, xi, AF.Sin), reads=[T], writes=[T])
        k.op("act", lambda e: e.activation(ab, xi, AF.Abs), reads=[T], writes=[T])
        k.op("act", lambda e: e.activation(c, ab, AF.Sin, scale=-1.0, bias=PI / 2), reads=[T], writes=[T])
        k.op("dve", lambda e: e.tensor_tensor(R[:, 0, :], mag, c, ALU.mult), reads=[T], writes=[R])
        k.op("dve", lambda e: e.tensor_tensor(R[:, 1, :], mag, s, ALU.mult), reads=[T], writes=[R])
        den, rden, ir, ii, nr = dt, xr, mag, s, c
        k.op("dve", lambda e: e.tensor_tensor(den, are, are, ALU.mult), reads=[src, T], writes=[T])
        k.op("dve", lambda e: e.tensor_tensor(m, aim, aim, ALU.mult), reads=[src, T], writes=[T])
        k.op("dve", lambda e: e.tensor_tensor(den, den, m, ALU.add), reads=[T], writes=[T])
        k.op("dve", lambda e: e.reciprocal(rden, den), reads=[T], writes=[T])
        k.op("dve", lambda e: e.tensor_tensor(ir, are, rden, ALU.mult), reads=[src, T], writes=[T])
        k.op("dve", lambda e: e.scalar_tensor_tensor(ii, aim, -1.0, rden, ALU.mult, ALU.mult), reads=[src, T], writes=[T])
        k.op("dve", lambda e: e.tensor_scalar(nr, R[:, 0, :], -1.0, None, ALU.add), reads=[R, T], writes=[T])
        self.cmul("dve", R[:, 2, :], R[:, 3, :], nr, R[:, 1, :], ir, ii, xi, ab, [T, R], [R], [T])
        return R

    def s5_prep(self, l, st):
        k = self.k
        NJ = self.SEG // 8
        self.KcT = k.sb("KcT", [128, 2, 15, 64], BF16, st)
        self.EL = k.sb("EL", [128, 2, 4, 2, 8, 128], BF16, st)
        self.WL = k.sb("WL", [128, 32, 8, 64], BF16, st)
        self.SQ = k.sb("SQ", [128, 2, 12, 32], F32, st)
        self.I2 = k.sb("I2", [128, 64], F32, st)
        self.wglu = k.sb("wglu", [128, 2, 256], BF16, st)
        k.dma("sp", self.I2[:], self.dr["c_I2"].ap(), writes=[self.I2], owner=self.I2)
        k.dma("sp", self.wglu[:], self.dr["b_w_glu"].ap()[l].rearrange("(kc p) n -> p kc n", p=128), writes=[self.wglu], owner=self.wglu)
        k.op("pool", lambda e: e.memset(self.KcT[:], 0.0), writes=[self.KcT])
        k.op("pool", lambda e: e.memset(self.WL[:], 0.0), writes=[self.WL])
        with contextlib.ExitStack() as ps:
            sA = k.sb("sA", [128, 3, 32], F32, ps)
            bA = k.sb("bA", [128, 2, 32, 16], F32, ps)
            cA = k.sb("cA", [128, 2, 32, 16], F32, ps)
            k.dma("sp", sA[:], self.dr["s5A_a"].ap()[l], writes=[sA], owner=sA)
            k.dma("sp", bA[:], self.dr["s5A_b"].ap()[l], writes=[bA], owner=bA)
            k.dma("sp", cA[:], self.dr["s5A_c"].ap()[l], writes=[cA], owner=cA)
            RA = self.abar_calc(ps, "A", sA, 32)
            PW = k.sb("PWA", [128, 2, 9, 32], F32, ps)
            tA = k.sb("tA", [128, 2, 32], F32, ps)
            k.op("dve", lambda e: e.memset(PW[:, 0, 0, :], 1.0), writes=[PW])
            k.op("dve", lambda e: e.memset(PW[:, 1, 0, :], 0.0), writes=[PW])
            for t in range(1, 9):
                self.cmul("dve", PW[:, 0, t, :], PW[:, 1, t, :], PW[:, 0, t - 1, :], PW[:, 1, t - 1, :], RA[:, 0, :], RA[:, 1, :],
                          tA[:, 0, :], tA[:, 1, :], [PW, RA], [PW], [tA])
            SQ = self.SQ
            SQf = k.sb("SQf", [128, 2, 12, 32], F32, ps)
            k.op("dve", lambda e: e.tensor_copy(SQf[:, :, 0, :], PW[:, :, 8, :]), reads=[PW], writes=[SQf])
            for m in range(1, 12):
                self.cmul("dve", SQf[:, 0, m, :], SQf[:, 1, m, :], SQf[:, 0, m - 1, :], SQf[:, 1, m - 1, :],
                          SQf[:, 0, m - 1, :], SQf[:, 1, m - 1, :], tA[:, 0, :], tA[:, 1, :], [SQf], [SQf], [tA])
            k.op("dve", lambda e: e.tensor_copy(SQ[:, 0, :, :], SQf[:, 0, :, :]), reads=[SQf], writes=[SQ])
            k.op("dve", lambda e: e.tensor_copy(SQ[0:64, 1, :, :], SQf[0:64, 1, :, :]), reads=[SQf], writes=[SQ])
            k.op("dve", lambda e: e.tensor_scalar(SQ[64:128, 1, :, :], SQf[64:128, 1, :, :], -1.0, None, ALU.mult), reads=[SQf], writes=[SQ])
            bbA = k.sb("bbA", [128, 2, 32, 16], F32, ps)
            tB = k.sb("tB", [128, 2, 32, 16], F32, ps)
            cr = RA[:, 2, :].unsqueeze(2).broadcast_to([128, 32, 16])
            ci = RA[:, 3, :].unsqueeze(2).broadcast_to([128, 32, 16])
            self.cmul("dve", bbA[:, 0], bbA[:, 1], cr, ci, bA[:, 0], bA[:, 1], tB[:, 0], tB[:, 1], [RA, bA], [bbA], [tB])
            CAf = k.sb("CAf", [128, 2, 16, 9, 16], F32, ps)
            tC = k.sb("tC", [128, 2, 16, 9, 16], F32, ps)
            CAS = k.sb("CAS", [128, 32, 9, 16], F32, ps)
            shp = [128, 16, 9, 16]
            for dd_ in range(2):
                gs_ = slice(dd_ * 16, dd_ * 16 + 16)
                pr = PW[:, 0, :, gs_].rearrange("p t g -> p g t").unsqueeze(3).broadcast_to(shp)
                pi_ = PW[:, 1, :, gs_].rearrange("p t g -> p g t").unsqueeze(3).broadcast_to(shp)
                c_r = cA[:, 0, gs_, :].unsqueeze(2).broadcast_to(shp)
                c_i = cA[:, 1, gs_, :].unsqueeze(2).broadcast_to(shp)
                self.cmul("dve", CAf[:, 0], CAf[:, 1], c_r, c_i, pr, pi_, tC[:, 0], tC[:, 1], [PW, cA], [CAf], [tC])
                k.op("dve", lambda e, gs_=gs_: e.tensor_copy(CAS[0:64, gs_], CAf[0:64, 0]), reads=[CAf], writes=[CAS])
                k.op("dve", lambda e, gs_=gs_: e.tensor_scalar(CAS[64:128, gs_], CAf[64:128, 1], -1.0, None, ALU.mult), reads=[CAf], writes=[CAS])
            BBS = k.sb("BBS", [128, 32, 16], F32, ps)
            k.op("dve", lambda e: e.tensor_copy(BBS[0:64], bbA[0:64, 0]), reads=[bbA], writes=[BBS])
            k.op("dve", lambda e: e.tensor_copy(BBS[64:128], bbA[64:128, 1]), reads=[bbA], writes=[BBS])
            WLv = self.WL[:].rearrange("p (d h e) t (f c) -> p d h e t f c", d=2, h=4, e=4, f=4)
            CASv = CAS[:].rearrange("p (d h e) t c -> p d h e t c", d=2, h=4, e=4)
            for e_ in range(4):
                k.op("dve", lambda e, e_=e_: e.tensor_copy(WLv[:, 0, :, e_, :, e_, :], CASv[:, 0, :, e_, 1:9, :]), reads=[CAS], writes=[self.WL])
                for t in range(8):
                    k.op("pool", lambda e, e_=e_, t=t: e.tensor_copy(WLv[:, 1, :, e_, t, e_, :], CASv[:, 1, :, e_, 8 - t, :]), reads=[CAS], writes=[self.WL])
            Ksb = k.sb("Ksb", [16, 32, 8, 16], F32, ps)
            for q in range(8):
                bk = self.nextbank()
                for j in range(4):
                    kg = q * 4 + j
                    k.op("pe", lambda e, bk=bk, kg=kg, j=j: e.matmul(
                        self.bank(bk)[0:16, j * 128:(j + 1) * 128], lhsT=BBS[:, kg, :], rhs=CAS[:, kg, 0:8, :],
                        start=True, stop=True), reads=[BBS, CAS], writes=[self.bdep[bk]])
                k.op("act", lambda e, bk=bk, q=q: e.activation(Ksb[:, q * 4:(q + 1) * 4, :, :], self.bank(bk)[0:16, :].rearrange("p (a b c) -> p a b c", a=4, b=8), AF.Copy),
                     reads=[self.bdep[bk]], writes=[Ksb])
            Kcs = k.sb("Kcs", [16, 16, 15, 16], F32, ps)
            dT = k.sb("dT", [16, 16], F32, ps)
            I16 = k.sb("I16", [16, 16], F32, ps)
            dI = k.sb("dI", [16, 16, 16], F32, ps)
            k.dma("sp", dT[:], self.dr["s5_dT"].ap()[l], writes=[dT], owner=dT)
            k.dma("sp", I16[:], self.dr["c_I16"].ap(), writes=[I16], owner=I16)
            k.op("dve", lambda e: e.tensor_copy(Kcs[:, :, 8:15, :], Ksb[:, 0:16, 1:8, :]), reads=[Ksb], writes=[Kcs])
            for t in range(1, 8):
                k.op("dve", lambda e, t=t: e.tensor_copy(Kcs[:, :, 7 - t, :], Ksb[:, 16:32, t, :]), reads=[Ksb], writes=[Kcs])
            k.op("dve", lambda e: e.tensor_tensor(Kcs[:, :, 7, :], Ksb[:, 0:16, 0, :], Ksb[:, 16:32, 0, :], ALU.add), reads=[Ksb], writes=[Kcs])
            k.op("dve", lambda e: e.tensor_tensor(dI[:], I16[:].unsqueeze(1).broadcast_to([16, 16, 16]),
                                                   dT[:].unsqueeze(2).broadcast_to([16, 16, 16]), ALU.mult), reads=[I16, dT], writes=[dI])
            k.op("dve", lambda e: e.tensor_tensor(Kcs[:, :, 7, :], Kcs[:, :, 7, :], dI[:], ALU.add), reads=[dI, Kcs], writes=[Kcs])
            kdd = Dep("Kd")
            k.dma("sp", self.dr["Kd"].ap().rearrange("g s c d -> c g s d"), Kcs[:], reads=[Kcs], writes=[kdd], owner=Kcs)
            for g in range(16):
                tile_, hh, e_ = g // 8, (g % 8) // 4, g % 4
                p0 = 64 * hh + 16 * e_
                k.dma("pool", self.KcT[p0:p0 + 16, tile_, :, 16 * e_:16 * e_ + 16], self.dr["Kd"].ap()[g].rearrange("s c d -> c s d"),
                      reads=[kdd], writes=[self.KcT], owner=self.KcT)
            if l == 0:
                self.dbg("RA", RA, [128, 4, 32]); self.dbg("PW", PW, [128, 2, 9, 32]); self.dbg("CAS", CAS, [128, 32, 9, 16])
                self.dbg("BBS", BBS, [128, 32, 16]); self.dbg("Ksb", Ksb, [16, 32, 8, 16]); self.dbg("Kcs", Kcs, [16, 16, 15, 16])
                self.dbg("SQ", self.SQ, [128, 2, 12, 32])
            k.flush()
        with contextlib.ExitStack() as ps:
            sB = k.sb("sB", [128, 3, 256], F32, ps)
            bB = k.sb("bB", [128, 2, 256], F32, ps)
            EM = k.sb("EM", [128, 4], F32, ps)
            k.dma("sp", sB[:], self.dr["s5B_a"].ap()[l].rearrange("p a t d n -> p a (t d n)"), writes=[sB], owner=sB)
            k.dma("sp", bB[:], self.dr["s5B_b"].ap()[l].rearrange("p a t d n -> p a (t d n)"), writes=[bB], owner=bB)
            k.dma("sp", EM[:], self.dr["c_EMASK"].ap(), writes=[EM], owner=EM)
            RB = self.abar_calc(ps, "B", sB, 256)
            PB = k.sb("PB", [128, 2, 8, 256], F32, ps)
            tD = k.sb("tD", [128, 2, 256], F32, ps)
            self.cmul("dve", PB[:, 0, 0, :], PB[:, 1, 0, :], RB[:, 2, :], RB[:, 3, :], bB[:, 0, :], bB[:, 1, :], tD[:, 0, :], tD[:, 1, :], [RB, bB], [PB], [tD])
            for t in range(1, 8):
                self.cmul("dve", PB[:, 0, t, :], PB[:, 1, t, :], PB[:, 0, t - 1, :], PB[:, 1, t - 1, :], RB[:, 0, :], RB[:, 1, :],
                          tD[:, 0, :], tD[:, 1, :], [PB, RB], [PB], [tD])
            PBv = PB[:].rearrange("p r k (t d n) -> p r k t d n", t=2, d=2)
            for e_ in range(4):
                for d_ in range(2):
                    for t2 in range(8):
                        pw = 7 - t2 if d_ == 0 else t2
                        for r in range(2):
                            eng = "dve" if (t2 + r) % 2 == 0 else "pool"
                            k.op(eng, lambda e, e_=e_, d_=d_, t2=t2, pw=pw, r=r: e.tensor_scalar(
                                self.EL[:, :, e_, d_, t2, r * 64:(r + 1) * 64], PBv[:, r, pw, :, d_, :], EM[:, e_:e_ + 1], None, ALU.mult),
                                reads=[PB, EM], writes=[self.EL])
            if l == 0:
                self.dbg("RB", RB, [128, 4, 256]); self.dbg("PB", PB, [128, 2, 8, 256])
                self.dbg("EL", self.EL, [128, 2, 4, 2, 8, 128], BF16); self.dbg("WL", self.WL, [128, 32, 8, 64], BF16)
                self.dbg("KcT", self.KcT, [128, 2, 15, 64], BF16)
            k.flush()

    def s5_seq(self, l, si, st_outer):
        k = self.k
        L = self.seqs[si]["L"]
        SEG = self.SEG
        NJ = SEG // 8
        NC1 = NJ + 1
        nseg = L // SEG
        steps = []
        d_ = 1
        while d_ < NC1:
            steps.append(d_)
            d_ *= 2
        with contextlib.ExitStack() as st:
            uT = [k.sb(f"uT{i}", [128, 2, SEG], BF16, st) for i in range(2)]
            Sm = k.sb("Sm", [128, 16, NC1], F32, st)
            Sbf = [k.sb(f"Sbf{i}", [128, 16, NC1], BF16, st) for i in range(2)]
            Rot = [k.sb(f"Rot{i}", [128, 16, 128], BF16, st) for i in range(2)]
            carry = k.sb("carry", [128, 2, 16], F32, st)
            gF = k.sb("gF", [128, 2, SEG], F32, st)
            gB = k.sb("gB", [128, 2, SEG], BF16, st)
            sg = [k.sb(f"sg{i}", [128, 512], F32, st) for i in range(2)]
            so = [k.sb(f"so{i}", [128, 2, 512], BF16, st) for i in range(2)]
            hsd = self.dr[f"hsT{si}"].ap().rearrange("(c p) t -> p c t", p=128)
            sbst = self.dr[f"sbst{si}"]
            sbdep = [Dep(f"sbst{q}") for q in range(nseg)]
            ui = [0]

            def load_u(q):
                u = uT[ui[0] % 2]
                ui[0] += 1
                k.dma("pool", u[:], hsd[:, :, q * SEG:(q + 1) * SEG], writes=[u], owner=u)
                return u

            def scan(dirn, u, first):
                S = Sbf[dirn]
                c_off = 1 if dirn == 0 else 0
                for gp in range(8):
                    bk = self.nextbank()
                    for j in range(2):
                        g = gp * 2 + j
                        tile_, hh, e_ = g // 8, (g % 8) // 4, g % 4
                        for t2 in range(8):
                            k.op("pe", lambda e, bk=bk, j=j, tile_=tile_, hh=hh, e_=e_, t2=t2, u=u: e.matmul(
                                self.bank(bk)[:, j * NJ:(j + 1) * NJ], lhsT=self.EL[64 * hh:64 * hh + 64, tile_, e_, dirn, t2, :],
                                rhs=u[64 * hh:64 * hh + 64, tile_, t2:SEG:8], start=(t2 == 0), stop=(t2 == 7)),
                                reads=[self.EL, u], writes=[self.bdep[bk]])
                    k.op("act", lambda e, bk=bk, gp=gp: e.activation(
                        Sm[:, gp * 2:gp * 2 + 2, c_off:c_off + NJ], self.bank(bk)[:, 0:2 * NJ].rearrange("p (a b) -> p a b", a=2), AF.Copy),
                        reads=[self.bdep[bk]], writes=[Sm])
                ccol = 0 if dirn == 0 else NJ
                if first:
                    k.op("dve", lambda e: e.memset(Sm[:, :, ccol:ccol + 1], 0.0), writes=[Sm])
                else:
                    k.op("dve", lambda e: e.tensor_copy(Sm[:, :, ccol:ccol + 1], carry[:, dirn, :].unsqueeze(2)), reads=[carry], writes=[Sm])
                k.op("pool", lambda e: e.tensor_copy(S[:], Sm[:]), reads=[Sm], writes=[S])
                for mi, dd in enumerate(steps):
                    R = Rot[mi % 2]
                    ar = self.SQ[:, 0, mi, dirn * 16:(dirn + 1) * 16]
                    ai = self.SQ[:, 1, mi, dirn * 16:(dirn + 1) * 16]
                    I2 = self.I2
                    for (r0, c0, src) in ((0, 0, ar), (0, 64, ai), (64, 0, ai), (64, 64, ar)):
                        k.op("dve", lambda e, r0=r0, c0=c0, src=src, R=R: e.tensor_tensor(
                            R[r0:r0 + 64, :, c0:c0 + 64], I2[r0:r0 + 64, :].unsqueeze(1).broadcast_to([64, 16, 64]),
                            src[r0:r0 + 64, :].unsqueeze(2).broadcast_to([64, 16, 64]), ALU.mult), reads=[I2, self.SQ], writes=[R])
                    n = NC1 - dd
                    if dirn == 0:
                        srcs, dsts = slice(0, n), slice(dd, NC1)
                    else:
                        srcs, dsts = slice(dd, NC1), slice(0, n)
                    per = 2 if n <= 256 else 1
                    for g0 in range(0, 16, per):
                        bk = self.nextbank()
                        for j in range(per):
                            g = g0 + j
                            k.op("pe", lambda e, bk=bk, j=j, g=g, R=R, n=n, srcs=srcs: e.matmul(
                                self.bank(bk)[:, j * 256:j * 256 + n], lhsT=R[:, g, :], rhs=S[:, g, srcs], start=True, stop=True),
                                reads=[R, S], writes=[self.bdep[bk]])
                        for j in range(per):
                            g = g0 + j
                            k.op("dve", lambda e, bk=bk, j=j, g=g, n=n, dsts=dsts: e.tensor_tensor(
                                Sm[:, g, dsts], self.bank(bk)[:, j * 256:j * 256 + n], Sm[:, g, dsts], ALU.add),
                                reads=[self.bdep[bk], Sm], writes=[Sm])
                    k.op("pool", lambda e: e.tensor_copy(S[:], Sm[:]), reads=[Sm], writes=[S])
                ocol = NJ if dirn == 0 else 0
                k.op("dve", lambda e: e.tensor_copy(carry[:, dirn, :].unsqueeze(2), Sm[:, :, ocol:ocol + 1]), reads=[Sm], writes=[carry])

            for q in range(nseg - 1, -1, -1):
                u = load_u(q)
                scan(1, u, q == nseg - 1)
                k.dma("sp", sbst.ap()[q], Sbf[1][:], reads=[Sbf[1]], writes=[sbdep[q]], owner=Sbf[1])
            for q in range(nseg):
                u = load_u(q)
                scan(0, u, q == 0)
                k.dma("sp", Sbf[1][:], sbst.ap()[q], reads=[sbdep[q]], writes=[Sbf[1]], owner=Sbf[1])
                for h4 in range(4):
                    tile_, hh = h4 // 2, h4 % 2
                    pr = slice(64 * hh, 64 * hh + 64)
                    for tp in range(4):
                        bk = self.nextbank()
                        for j in range(2):
                            t = tp * 2 + j
                            oc = slice(j * NJ, (j + 1) * NJ)
                            for t2 in range(8):
                                k.op("pe", lambda e, bk=bk, oc=oc, t=t, t2=t2, tile_=tile_, pr=pr, u=u: e.matmul(
                                    self.bank(bk)[pr, oc], lhsT=self.KcT[pr, tile_, t - t2 + 7, :], rhs=u[pr, tile_, t2:SEG:8],
                                    start=(t2 == 0), stop=False), reads=[self.KcT, u], writes=[self.bdep[bk]])
                            for e_ in range(4):
                                g = h4 * 4 + e_
                                k.op("pe", lambda e, bk=bk, oc=oc, t=t, g=g, pr=pr: e.matmul(
                                    self.bank(bk)[pr, oc], lhsT=self.WL[:, g, t, :], rhs=Sbf[0][:, g, 0:NJ],
                                    start=False, stop=False), reads=[self.WL, Sbf[0]], writes=[self.bdep[bk]])
                                k.op("pe", lambda e, bk=bk, oc=oc, t=t, g=g, pr=pr, e_=e_: e.matmul(
                                    self.bank(bk)[pr, oc], lhsT=self.WL[:, 16 + g, t, :], rhs=Sbf[1][:, g, 1:NC1],
                                    start=False, stop=(e_ == 3)), reads=[self.WL, Sbf[1]], writes=[self.bdep[bk]])
                        k.op("act", lambda e, bk=bk, tp=tp, tile_=tile_, pr=pr: e.activation(
                            gF[pr, tile_, :].rearrange("p (j t) -> p t j", t=8)[:, tp * 2:tp * 2 + 2, :],
                            self.bank(bk)[pr, 0:2 * NJ].rearrange("p (a b) -> p a b", a=2), AF.Gelu_apprx_tanh),
                            reads=[self.bdep[bk]], writes=[gF])
                if l == 0 and si == 0 and q == 0:
                    self.dbg("Sf", Sbf[0], [128, 16, NC1], BF16); self.dbg("Sb", Sbf[1], [128, 16, NC1], BF16)
                    self.dbg("gF", gF, [128, 2, SEG]); self.dbg("uT", u, [128, 2, SEG], BF16)
                k.op("pool", lambda e: e.tensor_copy(gB[:], gF[:]), reads=[gF], writes=[gB])
                GB = min(512, SEG)
                for b in range(SEG // GB):
                    ts = slice(b * GB, (b + 1) * GB)
                    sob = so[b % 2]
                    for cc in range(2):
                        bk = self.nextbank()
                        for kc in range(2):
                            k.op("pe", lambda e, bk=bk, cc=cc, kc=kc, ts=ts: e.matmul(
                                self.bank(bk)[:, 0:GB], lhsT=self.wglu[:, kc, cc * 128:(cc + 1) * 128], rhs=gB[:, kc, ts],
                                start=(kc == 0), stop=(kc == 1)), reads=[self.wglu, gB], writes=[self.bdep[bk]])
                        sgt = sg[cc]
                        k.op("act", lambda e, bk=bk, sgt=sgt: e.activation(sgt[:, 0:GB], self.bank(bk)[:, 0:GB], AF.Sigmoid), reads=[self.bdep[bk]], writes=[sgt])
                        k.op("dve", lambda e, cc=cc, ts=ts, sgt=sgt, sob=sob: e.tensor_tensor(sob[:, cc, 0:GB], gF[:, cc, ts], sgt[:, 0:GB], ALU.mult),
                             reads=[gF, sgt], writes=[sob])
                    t0 = q * SEG + b * GB
                    k.dma("sp", self.dr[f"s5T{si}"].ap().rearrange("(c p) t -> p c t", p=128)[:, :, t0:t0 + GB], sob[:, :, 0:GB], reads=[sob], owner=sob)
            k.flush()

    def fourier(self, l, si, st):
        k = self.k
        s = self.seqs[si]
        L, N1, N2 = s["L"], s["N1"], s["N2"]
        FA = k.sb("FA", [N1, 2 * N1], BF16, st)
        FC1 = k.sb("FC1", [N2, 2 * N2], BF16, st)
        FC2 = k.sb("FC2", [N2, 2 * N2], BF16, st)
        TW = k.sb("TW", [N2, 2, N1], F32, st)
        k.dma("pool", FA[:], self.dr[f"c{si}_FA"].ap(), writes=[FA], owner=FA)
        k.dma("pool", FC1[:], self.dr[f"c{si}_FC1"].ap(), writes=[FC1], owner=FC1)
        k.dma("pool", FC2[:], self.dr[f"c{si}_FC2"].ap(), writes=[FC2], owner=FC2)
        k.dma("sp", TW[:], self.dr[f"c{si}_TW"].ap(), writes=[TW], owner=TW)
        xg = [k.sb(f"xg{i}", [N1, N2, 64], BF16, st) for i in range(2)]
        AR = k.sb("AR", [N2, N1, 64], BF16, st)
        AI = k.sb("AI", [N2, N1, 64], BF16, st)
        XT = k.sb("XT", [64, 2, L], BF16, st)
        nbA = 512 // (2 * N1)
        tmp = [k.sb(f"ftmp{i}", [N2, nbA, N1], F32, st) for i in range(4)]
        nbC = 512 // (2 * N2)
        hfd = self.dr[f"hf{si}"].ap()
        for g in range(4):
            x = xg[g % 2]
            k.dma("pool", x[:], hfd[g].rearrange("(a b) c -> a b c", a=N1), writes=[x], owner=x)
            for c0 in range(0, 64, nbA):
                bk = self.nextbank()
                for j in range(nbA):
                    ch = c0 + j
                    k.op("pe", lambda e, bk=bk, j=j, ch=ch, x=x: e.matmul(
                        self.bank(bk)[0:N2, j * 2 * N1:(j + 1) * 2 * N1], lhsT=x[:, :, ch], rhs=FA[:], start=True, stop=True),
                        reads=[x, FA], writes=[self.bdep[bk]])
                pv = self.bank(bk)[0:N2, 0:nbA * 2 * N1].rearrange("p (c r k) -> p c r k", c=nbA, r=2)
                Ar, Ai = pv[:, :, 0, :], pv[:, :, 1, :]
                twr = TW[:, 0, :].unsqueeze(1).broadcast_to([N2, nbA, N1])
                twi = TW[:, 1, :].unsqueeze(1).broadcast_to([N2, nbA, N1])
                bd = self.bdep[bk]
                k.op("dve", lambda e, Ar=Ar, twr=twr: e.tensor_tensor(tmp[0][:], Ar, twr, ALU.mult), reads=[bd, TW], writes=[tmp[0]])
                k.op("dve", lambda e, Ai=Ai, twi=twi: e.tensor_tensor(tmp[1][:], Ai, twi, ALU.mult), reads=[bd, TW], writes=[tmp[1]])
                k.op("dve", lambda e, Ar=Ar, twi=twi: e.tensor_tensor(tmp[2][:], Ar, twi, ALU.mult), reads=[bd, TW], writes=[tmp[2]])
                k.op("dve", lambda e, Ai=Ai, twr=twr: e.tensor_tensor(tmp[3][:], Ai, twr, ALU.mult), reads=[bd, TW], writes=[tmp[3]])
                k.op("pool", lambda e, c0=c0: e.tensor_tensor(AR[:, :, c0:c0 + nbA].rearrange("p k c -> p c k"), tmp[0][:], tmp[1][:], ALU.subtract),
                     reads=[tmp[0], tmp[1]], writes=[AR])
                k.op("pool", lambda e, c0=c0: e.tensor_tensor(AI[:, :, c0:c0 + nbA].rearrange("p k c -> p c k"), tmp[2][:], tmp[3][:], ALU.add),
                     reads=[tmp[2], tmp[3]], writes=[AI])
            for k10 in range(0, N1, nbC):
                bk = self.nextbank()
                for j in range(nbC):
                    k1 = k10 + j
                    oc = slice(j * 2 * N2, (j + 1) * 2 * N2)
                    k.op("pe", lambda e, bk=bk, oc=oc, k1=k1: e.matmul(self.bank(bk)[0:64, oc], lhsT=AR[:, k1, :], rhs=FC1[:], start=True, stop=False),
                         reads=[AR, FC1], writes=[self.bdep[bk]])
                    k.op("pe", lambda e, bk=bk, oc=oc, k1=k1: e.matmul(self.bank(bk)[0:64, oc], lhsT=AI[:, k1, :], rhs=FC2[:], start=False, stop=True),
                         reads=[AI, FC2], writes=[self.bdep[bk]])
                pv = self.bank(bk)[0:64, 0:nbC * 2 * N2].rearrange("p (i r k) -> p r k i", i=nbC, r=2)
                for r in range(2):
                    eng = "act" if r == 0 else "dve"
                    outv = XT[:, r, :].rearrange("p (k2 k1) -> p k2 k1", k1=N1)[:, :, k10:k10 + nbC]
                    if eng == "act":
                        k.op("act", lambda e, outv=outv, pv=pv, r=r: e.activation(outv, pv[:, r, :, :], AF.Copy), reads=[self.bdep[bk]], writes=[XT])
                    else:
                        k.op("dve", lambda e, outv=outv, pv=pv, r=r: e.tensor_copy(outv, pv[:, r, :, :]), reads=[self.bdep[bk]], writes=[XT])
            for r in range(2):
                k.dma("sp", self.dr[f"fT{si}"].ap()[r, g * 64:(g + 1) * 64, :], XT[:, r, :], reads=[XT], owner=XT)

    def mla_pre(self, l, si, st):
        k = self.k
        L = self.seqs[si]["L"]
        wq = k.sb("wq", [128, 2, 768], BF16, st)
        wqs = k.sb("wqs", [128, 2, 768], BF16, st)
        wk = k.sb("wk", [128, 512], BF16, st)
        wv = k.sb("wv", [128, 512], BF16, st)
        vec = k.sb("vec", [128, 59], F32, st)
        k.dma("sp", wq[:], self.dr["b_w_uq"].ap()[l].rearrange("(kc p) n -> p kc n", p=128), writes=[wq], owner=wq)
        k.dma("sp", wqs[:], self.dr["b_w_uq_sw"].ap()[l].rearrange("(kc p) n -> p kc n", p=128), writes=[wqs], owner=wqs)
        k.dma("sp", wk[:], self.dr["b_w_uk"].ap()[l], writes=[wk], owner=wk)
        k.dma("sp", wv[:], self.dr["b_w_uv"].ap()[l], writes=[wv], owner=wv)
        k.dma("sp", vec[:], self.dr["vecs"].ap()[l], writes=[vec], owner=vec)
        cq = [k.sb(f"cq{i}", [128, 2, 512], F32, st) for i in range(2)]
        ckv = [k.sb(f"ckv{i}", [128, 512], F32, st) for i in range(2)]
        kpa = [k.sb(f"kpa{i}", [96, 512], F32, st) for i in range(2)]
        kpb = [k.sb(f"kpb{i}", [96, 512], F32, st) for i in range(2)]
        rope = [k.sb(f"rope{i}", [96, 2, 2, 512], F32, st) for i in range(2)]
        sq = k.sb("sq", [128, 3, 512], F32, st)
        rs = k.sb("rs", [128, 2, 512], F32, st)
        cqn = k.sb("cqn", [128, 2, 512], BF16, st)
        ckvn = k.sb("ckvn", [128, 512], BF16, st)
        rt = k.sb("rt", [96, 3, 512], F32, st)
        Qst = [k.sb(f"Qst{i}", [96, H, 512], BF16, st) for i in range(2)]
        Kst = [k.sb(f"Kst{i}", [96, H, 512], BF16, st) for i in range(2)]
        Vst = [k.sb(f"Vst{i}", [128, 4, 512], BF16, st) for i in range(2)]
        cqd = self.dr[f"cqT{si}"].ap().rearrange("(c p) t -> p c t", p=128)
        rp = self.dr[f"c{si}_ROPE"].ap()
        for b in range(L // 512):
            i = b % 2
            ts = slice(b * 512, (b + 1) * 512)
            k.dma("sp", cq[i][:], cqd[:, :, ts], writes=[cq[i]], owner=cq[i])
            k.dma("sp", ckv[i][:], self.dr[f"ckvT{si}"].ap()[:, ts], writes=[ckv[i]], owner=ckv[i])
            k.dma("sp", kpa[i][64:96, :], self.dr[f"kpeT{si}"].ap()[0:32, ts], writes=[kpa[i]], owner=kpa[i])
            k.dma("sp", kpb[i][64:96, :], self.dr[f"kpeT{si}"].ap()[32:64, ts], writes=[kpb[i]], owner=kpb[i])
            k.dma("sp", rope[i][64:96], rp[:, :, :, ts].rearrange("a b r t -> r a b t"), writes=[rope[i]], owner=rope[i])
            k.op("act", lambda e, i=i: e.activation(sq[:, 0:2, :], cq[i][:], AF.Square), reads=[cq[i]], writes=[sq])
            k.op("act", lambda e, i=i: e.activation(sq[:, 2, :], ckv[i][:], AF.Square), reads=[ckv[i]], writes=[sq])
            bq = self.nextbank()
            for kc in range(2):
                k.op("pe", lambda e, bq=bq, kc=kc: e.matmul(self.bank(bq), lhsT=self.ones[:], rhs=sq[:, kc, :], start=(kc == 0), stop=(kc == 1)),
                     reads=[self.ones, sq], writes=[self.bdep[bq]])
            bkv = self.nextbank()
            k.op("pe", lambda e, bkv=bkv: e.matmul(self.bank(bkv), lhsT=self.ones[:], rhs=sq[:, 2, :], start=True, stop=True),
                 reads=[self.ones, sq], writes=[self.bdep[bkv]])
            k.op("act", lambda e, bq=bq: e.activation(rs[:, 0, :], self.bank(bq), AF.Sqrt, scale=1.0 / 256, bias=EPS), reads=[self.bdep[bq]], writes=[rs])
            k.op("act", lambda e, bkv=bkv: e.activation(rs[:, 1, :], self.bank(bkv), AF.Sqrt, scale=1.0 / 128, bias=EPS), reads=[self.bdep[bkv]], writes=[rs])
            k.op("dve", lambda e: e.reciprocal(rs[:], rs[:]), reads=[rs], writes=[rs])
            for kc in range(2):
                k.op("dve", lambda e, kc=kc, i=i: e.scalar_tensor_tensor(cqn[:, kc, :], cq[i][:, kc, :], vec[:, 56 + kc:57 + kc], rs[:, 0, :], ALU.mult, ALU.mult),
                     reads=[cq[i], vec, rs], writes=[cqn])
            k.op("dve", lambda e, i=i: e.scalar_tensor_tensor(ckvn[:], ckv[i][:], vec[:, 58:59], rs[:, 1, :], ALU.mult, ALU.mult),
                 reads=[ckv[i], vec, rs], writes=[ckvn])
            R = rope[i]
            k.op("pool", lambda e, i=i, R=R: e.tensor_tensor(rt[64:96, 0, :], kpa[i][64:96, :], R[64:96, 1, 0, :], ALU.mult), reads=[kpa[i], R], writes=[rt])
            k.op("pool", lambda e, i=i, R=R: e.tensor_tensor(rt[64:96, 1, :], kpb[i][64:96, :], R[64:96, 1, 1, :], ALU.mult), reads=[kpb[i], R], writes=[rt])
            k.op("pool", lambda e: e.tensor_tensor(rt[64:96, 2, :], rt[64:96, 0, :], rt[64:96, 1, :], ALU.add), reads=[rt], writes=[rt])
            Q, K_, V = Qst[i], Kst[i], Vst[i]
            k.op("pool", lambda e, K_=K_: e.tensor_copy(K_[64:96, :, :], rt[64:96, 2, :].unsqueeze(1).broadcast_to([32, H, 512])), reads=[rt], writes=[K_])
            for h in range(H):
                ba = self.nextbank()
                bb = self.nextbank()
                for kc in range(2):
                    k.op("pe", lambda e, ba=ba, kc=kc, h=h: e.matmul(self.bank(ba)[0:96, :], lhsT=wq[:, kc, h * 96:(h + 1) * 96], rhs=cqn[:, kc, :],
                                                                    start=(kc == 0), stop=(kc == 1)), reads=[wq, cqn], writes=[self.bdep[ba]])
                for kc in range(2):
                    k.op("pe", lambda e, bb=bb, kc=kc, h=h: e.matmul(self.bank(bb)[0:96, :], lhsT=wqs[:, kc, h * 96:(h + 1) * 96], rhs=cqn[:, kc, :],
                                                                    start=(kc == 0), stop=(kc == 1)), reads=[wqs, cqn], writes=[self.bdep[bb]])
                k.op("act", lambda e, ba=ba, h=h, Q=Q: e.activation(Q[0:64, h, :], self.bank(ba)[0:64, :], AF.Copy, scale=96 ** -0.5),
                     reads=[self.bdep[ba]], writes=[Q])
                k.op("dve", lambda e, ba=ba, R=R: e.tensor_tensor(rt[64:96, 0, :], self.bank(ba)[64:96, :], R[64:96, 0, 0, :], ALU.mult),
                     reads=[self.bdep[ba], R], writes=[rt])
                k.op("dve", lambda e, bb=bb, R=R: e.tensor_tensor(rt[64:96, 1, :], self.bank(bb)[64:96, :], R[64:96, 0, 1, :], ALU.mult),
                     reads=[self.bdep[bb], R], writes=[rt])
                k.op("dve", lambda e, h=h, Q=Q: e.tensor_tensor(Q[64:96, h, :], rt[64:96, 0, :], rt[64:96, 1, :], ALU.add), reads=[rt], writes=[Q])
                bk = self.nextbank()
                k.op("pe", lambda e, bk=bk, h=h: e.matmul(self.bank(bk)[0:64, :], lhsT=wk[:, h * 64:(h + 1) * 64], rhs=ckvn[:], start=True, stop=True),
                     reads=[wk, ckvn], writes=[self.bdep[bk]])
                k.op("act", lambda e, bk=bk, h=h, K_=K_: e.activation(K_[0:64, h, :], self.bank(bk)[0:64, :], AF.Copy), reads=[self.bdep[bk]], writes=[K_])
            for sub in range(4):
                bk = self.nextbank()
                k.op("pe", lambda e, bk=bk, sub=sub: e.matmul(self.bank(bk), lhsT=ckvn[:, sub * 128:(sub + 1) * 128], rhs=wv[:], start=True, stop=True),
                     reads=[wv, ckvn], writes=[self.bdep[bk]])
                k.op("act", lambda e, bk=bk, sub=sub, V=V: e.activation(V[:, sub, :], self.bank(bk), AF.Copy), reads=[self.bdep[bk]], writes=[V])
            k.dma("sp", self.dr[f"qT{si}"].ap()[:, :, ts].rearrange("h r t -> r h t"), Q[:], reads=[Q], owner=Q)
            k.dma("sp", self.dr[f"kT{si}"].ap()[:, :, ts].rearrange("h r t -> r h t"), K_[:], reads=[K_], owner=K_)
            k.dma("sp", self.dr[f"v{si}"].ap()[ts, :].rearrange("(s p) c -> p s c", p=128), V[:], reads=[V], owner=V)

    def attn(self, l, si, st):
        k = self.k
        L = self.seqs[si]["L"]
        QB = min(1024, L)
        NKT = L // 128
        NH = QB // 512
        KT = [k.sb(f"KT{i}", [96, L], BF16, st) for i in range(2)]
        VT = [k.sb(f"VT{i}", [128, NKT, 128], BF16, st) for i in range(2)]
        Qb = [k.sb(f"Qb{i}", [96, QB], BF16, st) for i in range(2)]
        Pt = [k.sb(f"Pt{i}", [128, QB], BF16, st) for i in range(3)]
        rden = k.sb("rden", [128, QB], F32, st)
        ost = [k.sb(f"ost{i}", [64, QB], BF16, st) for i in range(2)]
        for i in range(2):
            k.op("pool", lambda e, i=i: e.memset(VT[i][:, :, 64:128], 1.0), writes=[VT[i]])
        sdep = [Dep("S0"), Dep("S1")]
        odep = [Dep("O0"), Dep("O1")]
        qd = self.dr[f"qT{si}"].ap()
        kd = self.dr[f"kT{si}"].ap()
        vd = self.dr[f"v{si}"].ap()
        ad = self.dr[f"attnT{si}"].ap()
        cnt = 0
        pcnt = 0
        for h in range(H):
            Kt, Vt = KT[h % 2], VT[h % 2]
            k.dma("sp", Kt[:], kd[h], writes=[Kt], owner=Kt)
            vsrc = vd[:, h * 64:(h + 1) * 64].rearrange("(t p) c -> p t c", p=128)
            nsp = max(1, NKT // 32)
            for sp_ in range(nsp):
                tsl = slice(sp_ * NKT // nsp, (sp_ + 1) * NKT // nsp)
                k.dma("pool", Vt[:, tsl, 0:64], vsrc[:, tsl, :], writes=[Vt], owner=Vt)
            for qb in range(L // QB):
                Q = Qb[cnt % 2]
                O = self.pst[2 + cnt % 2]
                od = odep[cnt % 2]
                osb = ost[cnt % 2]
                cnt += 1
                k.dma("sp", Q[:], qd[h, :, qb * QB:(qb + 1) * QB], writes=[Q], owner=Q)

                def smm(kt, Q=Q, Kt=Kt):
                    S = self.pst[kt % 2]
                    for hf in range(NH):
                        k.op("pe", lambda e, S=S, hf=hf, kt=kt: e.matmul(S[:, hf * 512:(hf + 1) * 512], lhsT=Kt[:, kt * 128:(kt + 1) * 128],
                                                                          rhs=Q[:, hf * 512:(hf + 1) * 512], start=True, stop=True),
                             reads=[Kt, Q], writes=[sdep[kt % 2]])
                smm(0)
                for kt in range(NKT):
                    if kt + 1 < NKT:
                        smm(kt + 1)
                    S = self.pst[kt % 2]
                    P = Pt[pcnt % 3]
                    pcnt += 1
                    k.op("act", lambda e, S=S, P=P: e.activation(P[:], S[:, 0:QB], AF.Exp), reads=[sdep[kt % 2]], writes=[P])
                    for hf in range(NH):
                        k.op("pe", lambda e, O=O, hf=hf, kt=kt, P=P, Vt=Vt: e.matmul(O[:, hf * 512:(hf + 1) * 512], lhsT=Vt[:, kt, :],
                                                                                    rhs=P[:, hf * 512:(hf + 1) * 512], start=(kt == 0), stop=(kt == NKT - 1)),
                             reads=[Vt, P], writes=[od])
                k.op("dve", lambda e, O=O: e.reciprocal(rden[64:128, :], O[64:128, 0:QB]), reads=[od], writes=[rden])
                k.op("dve", lambda e, O=O, osb=osb: e.tensor_tensor(osb[:], O[0:64, 0:QB], rden[64:128, :], ALU.mult), reads=[od, rden], writes=[osb])
                k.dma("sp", ad[h * 64:(h + 1) * 64, qb * QB:(qb + 1) * QB], osb[:], reads=[osb], owner=osb)

    def p3_setup(self, l, st):
        k = self.k
        self.vec3 = k.sb("vec3", [128, 59], F32, st)
        k.dma("sp", self.vec3[:], self.dr["vecs"].ap()[l], writes=[self.vec3], owner=self.vec3)
        self.cs64 = k.sb("cs64", [128, 2, 128], BF16, st)
        k.dma("pool", self.cs64[:], self.dr["c_CS64"].ap(), writes=[self.cs64], owner=self.cs64)
        self.wring = [k.sb(f"wr{i}", [128, 4096], BF16, st) for i in range(4)]
        self.wri = 0
        self.p3x = [k.sb(f"p3x{i}", [128, 8, 512], F32, st) for i in range(2)]
        self.p3in = [k.sb(f"p3in{i}", [128, 12, 512], BF16, st) for i in range(2)]
        self.gs = [k.sb(f"gs{i}", [128, 512], F32, st) for i in range(6)]
        self.mt = [k.sb(f"mt{i}", [128, 512], F32, st) for i in range(3)]
        self.mg = k.sb("mg", [128, 8, 512], BF16, st)
        self.z = k.sb("z", [128, 8, 512], F32, st)
        self.x1 = k.sb("x1", [128, 8, 512], F32, st)
        self.x1b = k.sb("x1b", [128, 8, 512], BF16, st)
        self.xbf = k.sb("xbf", [128, 8, 512], BF16, st)
        self.hb = k.sb("hb", [128, NFF, 512], BF16, st)
        self.lsq = [k.sb(f"lsq{i}", [128, 512], F32, st) for i in range(2)]
        self.lst = k.sb("lst", [128, 4, 512], F32, st)
        self.p3i = 0

    def wload(self, src_ap, shape_str, **kw):
        k = self.k
        w = self.wring[self.wri % 4]
        self.wri += 1
        n = 1
        for d_ in src_ap.shape[1:]:
            n *= d_
        view = w[:, 0:n].rearrange(shape_str, **kw) if shape_str else w[:, 0:n]
        k.dma("sp", view, src_ap, writes=[w], owner=w)
        return w, view

    def layernorm(self, zin, gcol, bcol, out32, out16):
        k = self.k
        vec = self.vec3
        b1 = self.nextbank()
        for kc in range(8):
            k.op("pe", lambda e, kc=kc: e.matmul(self.bank(b1), lhsT=self.ones[:], rhs=zin[:, kc, :], start=(kc == 0), stop=(kc == 7)),
                 reads=[self.ones, zin], writes=[self.bdep[b1]])
        b2 = self.nextbank()
        for kc in range(8):
            sq = self.lsq[kc % 2]
            k.op("act", lambda e, kc=kc, sq=sq: e.activation(sq[:], zin[:, kc, :], AF.Square), reads=[zin], writes=[sq])
            k.op("pe", lambda e, kc=kc, sq=sq: e.matmul(self.bank(b2), lhsT=self.ones[:], rhs=sq[:], start=(kc == 0), stop=(kc == 7)),
                 reads=[self.ones, sq], writes=[self.bdep[b2]])
        lst = self.lst
        mean, msq, var, rstd = lst[:, 0, :], lst[:, 1, :], lst[:, 2, :], lst[:, 3, :]
        k.op("dve", lambda e: e.tensor_scalar(mean, self.bank(b1), 1.0 / D, None, ALU.mult), reads=[self.bdep[b1]], writes=[lst])
        k.op("dve", lambda e: e.tensor_tensor(msq, mean, mean, ALU.mult), reads=[lst], writes=[lst])
        k.op("dve", lambda e: e.scalar_tensor_tensor(var, self.bank(b2), 1.0 / D, msq, ALU.mult, ALU.subtract), reads=[self.bdep[b2], lst], writes=[lst])
        k.op("act", lambda e: e.activation(rstd, var, AF.Sqrt, bias=EPS), reads=[lst], writes=[lst])
        k.op("dve", lambda e: e.reciprocal(rstd, rstd), reads=[lst], writes=[lst])
        for kc in range(8):
            o32 = out32[:, kc, :]
            k.op("dve", lambda e, kc=kc, o32=o32: e.tensor_tensor(o32, zin[:, kc, :], mean, ALU.subtract), reads=[zin, lst], writes=[out32])
            k.op("dve", lambda e, kc=kc, o32=o32: e.tensor_tensor(o32, o32, rstd, ALU.mult), reads=[out32, lst], writes=[out32])
            k.op("dve", lambda e, kc=kc, o32=o32: e.tensor_scalar(o32, o32, vec[:, gcol + kc:gcol + kc + 1], vec[:, bcol + kc:bcol + kc + 1], ALU.mult, ALU.add),
                 reads=[out32, vec], writes=[out32])
            if out16 is not None:
                k.op("pool", lambda e, kc=kc, o32=o32: e.tensor_copy(out16[:, kc, :], o32), reads=[out32], writes=[out16])

    def p3(self, l, si, st):
        k = self.k
        L = self.seqs[si]["L"]
        xs = self.xsrc(l, si).ap().rearrange("(kc p) t -> p kc t", p=128)
        xd = self.xdst(l, si).ap().rearrange("(kc p) t -> p kc t", p=128)
        fT = self.dr[f"fT{si}"].ap()
        vec = self.vec3
        wb = lambda n: self.dr["b_" + n].ap()[l]
        fscale = 1.0 / math.sqrt(L * 64.0)
        for b in range(L // 512):
            i = self.p3i % 2
            self.p3i += 1
            ts = slice(b * 512, (b + 1) * 512)
            x, pin = self.p3x[i], self.p3in[i]
            xbf = self.xbf
            k.dma("pool", x[:], xs[:, :, ts], writes=[x], owner=x)
            k.dma("pool", pin[:, 0:4, :], fT[:, :, ts].rearrange("r (c p) t -> p (r c) t", p=128), writes=[pin], owner=pin)
            k.dma("pool", pin[:, 4:6, :], self.dr[f"s5T{si}"].ap().rearrange("(c p) t -> p c t", p=128)[:, :, ts], writes=[pin], owner=pin)
            k.dma("pool", pin[:, 6:10, :], self.dr[f"attnT{si}"].ap().rearrange("(c p) t -> p c t", p=128)[:, :, ts], writes=[pin], owner=pin)
            k.op("act", lambda e, x=x: e.activation(xbf[:, 0:4, :], x[:, 0:4, :], AF.Copy), reads=[x], writes=[xbf])
            k.op("pool", lambda e, x=x: e.tensor_copy(xbf[:, 4:8, :], x[:, 4:8, :]), reads=[x], writes=[xbf])
            for cc in range(2):
                bk = self.nextbank()
                k.op("pe", lambda e, bk=bk, cc=cc, pin=pin: e.matmul(self.bank(bk), lhsT=self.cs64[:, 0, :], rhs=pin[:, cc, :], start=True, stop=False),
                     reads=[self.cs64, pin], writes=[self.bdep[bk]])
                k.op("pe", lambda e, bk=bk, cc=cc, pin=pin: e.matmul(self.bank(bk), lhsT=self.cs64[:, 1, :], rhs=pin[:, 2 + cc, :], start=False, stop=True),
                     reads=[self.cs64, pin], writes=[self.bdep[bk]])
                k.op("act", lambda e, bk=bk, cc=cc, pin=pin: e.activation(pin[:, 10 + cc, :], self.bank(bk), AF.Copy, scale=fscale),
                     reads=[self.bdep[bk]], writes=[pin])
            brk = [(0, (10, 11)), (2, (4, 5)), (4, (6, 7, 8, 9))]
            for G in range(2):
                gw = []
                for br in range(3):
                    c0 = 928 + br * 1024 + G * 512
                    gw.append(self.wload(wb("w_in").rearrange("(kc p) n -> p kc n", p=128)[:, :, c0:c0 + 512], "p (kc n) -> p kc n", kc=8))
                bw = self.wring[self.wri % 4]
                self.wri += 1
                bwv = bw[:, 0:4096].rearrange("p (kc n) -> p kc n", kc=8)
                cs_ = slice(G * 512, (G + 1) * 512)
                k.dma("sp", bwv[:, 0:2, :], wb("w_br_f").rearrange("(kc p) n -> p kc n", p=128)[:, :, cs_], writes=[bw], owner=bw)
                k.dma("sp", bwv[:, 2:4, :], wb("w_br_s").rearrange("(kc p) n -> p kc n", p=128)[:, :, cs_], writes=[bw], owner=bw)
                k.dma("sp", bwv[:, 4:8, :], wb("w_br_a").rearrange("(kc p) n -> p kc n", p=128)[:, :, cs_], writes=[bw], owner=bw)
                for c4 in range(4):
                    cc = G * 4 + c4
                    ws = slice(c4 * 128, (c4 + 1) * 128)
                    gsl = []
                    for br in range(3):
                        bk = self.nextbank()
                        wt, wv_ = gw[br]
                        for kc in range(8):
                            k.op("pe", lambda e, bk=bk, kc=kc, wv_=wv_, ws=ws: e.matmul(self.bank(bk), lhsT=wv_[:, kc, ws], rhs=xbf[:, kc, :],
                                                                                      start=(kc == 0), stop=(kc == 7)),
                                 reads=[wt, xbf], writes=[self.bdep[bk]])
                        g = self.gs[(cc * 3 + br) % 6]
                        col = 32 + br * 8 + cc
                        k.op("act", lambda e, bk=bk, g=g, col=col: e.activation(g[:], self.bank(bk), AF.Sigmoid, bias=vec[:, col:col + 1]),
                             reads=[self.bdep[bk], vec], writes=[g])
                        gsl.append(g)
                    for br in range(3):
                        bk = self.nextbank()
                        k0, srcs = brk[br]
                        n = len(srcs)
                        for j, sidx in enumerate(srcs):
                            k.op("pe", lambda e, bk=bk, j=j, sidx=sidx, k0=k0, n=n, ws=ws, bwv=bwv, pin=pin: e.matmul(
                                self.bank(bk), lhsT=bwv[:, k0 + j, ws], rhs=pin[:, sidx, :], start=(j == 0), stop=(j == n - 1)),
                                reads=[bw, pin], writes=[self.bdep[bk]])
                        m = self.mt[br]
                        g = gsl[br]
                        k.op("dve", lambda e, bk=bk, m=m, g=g: e.tensor_tensor(m[:], self.bank(bk), g[:], ALU.mult), reads=[self.bdep[bk], g], writes=[m])
                    k.op("pool", lambda e: e.tensor_tensor(self.mt[0][:], self.mt[0][:], self.mt[1][:], ALU.add), reads=[self.mt[0], self.mt[1]], writes=[self.mt[0]])
                    k.op("pool", lambda e, cc=cc: e.tensor_tensor(self.mg[:, cc, :], self.mt[0][:], self.mt[2][:], ALU.add), reads=[self.mt[0], self.mt[2]], writes=[self.mg])
            self._p3_rest(l, si, b, ts, x, xd)

    def _p3_rest(self, l, si, b, ts, x, xd):
        k = self.k
        vec = self.vec3
        wb = lambda n: self.dr["b_" + n].ap()[l]
        for G in range(2):
            wt, wv_ = self.wload(wb("w_o").rearrange("(kc p) n -> p kc n", p=128)[:, :, G * 512:(G + 1) * 512], "p (kc n) -> p kc n", kc=8)
            for c4 in range(4):
                cc = G * 4 + c4
                bk = self.nextbank()
                for kc in range(8):
                    k.op("pe", lambda e, bk=bk, kc=kc, wv_=wv_, c4=c4: e.matmul(self.bank(bk), lhsT=wv_[:, kc, c4 * 128:(c4 + 1) * 128], rhs=self.mg[:, kc, :],
                                                                              start=(kc == 0), stop=(kc == 7)), reads=[wt, self.mg], writes=[self.bdep[bk]])
                k.op("dve", lambda e, bk=bk, cc=cc: e.scalar_tensor_tensor(self.z[:, cc, :], x[:, cc, :], ALPHA, self.bank(bk), ALU.mult, ALU.add),
                     reads=[x, self.bdep[bk]], writes=[self.z])
        self.layernorm(self.z, 0, 8, self.x1, self.x1b)
        wgu = wb("w_gu").rearrange("(kc p) (two f) -> p kc two f", p=128, two=2)
        for f0 in range(0, NFF, 2):
            wt = self.wring[self.wri % 4]
            self.wri += 1
            wv_ = wt[:, 0:4096].rearrange("p (kc two f) -> p kc two f", kc=8, two=2)
            for two in range(2):
                k.dma("sp", wv_[:, :, two, :], wgu[:, :, two, f0 * 128:(f0 + 2) * 128], writes=[wt], owner=wt)
            for j in range(2):
                f = f0 + j
                bg = self.nextbank()
                bu = self.nextbank()
                for two, bk in ((0, bg), (1, bu)):
                    for kc in range(8):
                        k.op("pe", lambda e, bk=bk, kc=kc, wv_=wv_, two=two, j=j: e.matmul(self.bank(bk), lhsT=wv_[:, kc, two, j * 128:(j + 1) * 128],
                                                                                         rhs=self.x1b[:, kc, :], start=(kc == 0), stop=(kc == 7)),
                             reads=[wt, self.x1b], writes=[self.bdep[bk]])
                sgt = self.gs[f % 6]
                k.op("act", lambda e, bg=bg, sgt=sgt: e.activation(sgt[:], self.bank(bg), AF.Silu), reads=[self.bdep[bg]], writes=[sgt])
                k.op("dve", lambda e, bu=bu, sgt=sgt, f=f: e.tensor_tensor(self.hb[:, f, :], self.bank(bu), sgt[:], ALU.mult),
                     reads=[self.bdep[bu], sgt], writes=[self.hb])
        wdn = wb("w_down").rearrange("(kc p) n -> p kc n", p=128)
        for cc in range(8):
            wt, wv_ = self.wload(wdn[:, :, cc * 128:(cc + 1) * 128], "p (kc n) -> p kc n", kc=NFF)
            bk = self.nextbank()
            for kc in range(NFF):
                k.op("pe", lambda e, bk=bk, kc=kc, wv_=wv_: e.matmul(self.bank(bk), lhsT=wv_[:, kc, :], rhs=self.hb[:, kc, :],
                                                                    start=(kc == 0), stop=(kc == NFF - 1)), reads=[wt, self.hb], writes=[self.bdep[bk]])
            k.op("dve", lambda e, bk=bk, cc=cc: e.scalar_tensor_tensor(self.z[:, cc, :], self.x1[:, cc, :], ALPHA, self.bank(bk), ALU.mult, ALU.add),
                 reads=[self.x1, self.bdep[bk]], writes=[self.z])
        self.layernorm(self.z, 16, 24, self.x1, None)
        k.dma("pool", xd[:, :, ts], self.x1[:], reads=[self.x1], owner=self.x1)


FULL_CFG = dict(depth=4, SEG=2048, seqs=[dict(L=4096, N1=64, N2=64), dict(L=16384, N1=128, N2=128)])


def build_program(cfg):
    p = Prog(cfg)
    nc = p.k.finish()
    return p, nc


def make_in_maps(cfg, inputs, n_cores, x_per_core):
    depth = cfg["depth"]
    par = _stage_params(inputs, depth)
    base = dict(par)
    for n, v in _global_consts().items():
        base["c_" + n] = v
    for si, s in enumerate(cfg["seqs"]):
        for n, v in _consts_for_seq(s["L"], s["N1"], s["N2"]).items():
            base[f"c{si}_{n}"] = v
    maps = []
    for c in range(n_cores):
        m = dict(base)
        for si in range(len(cfg["seqs"])):
            m[f"xT{si}"] = np.ascontiguousarray(np.asarray(x_per_core[c][si], dtype=np.float32).T)
        maps.append(m)
    return maps


def kernel(**inputs):
    cfg = FULL_CFG
    xp = np.asarray(inputs["x_prompt"], dtype=np.float32)
    xs = np.asarray(inputs["x_sample"], dtype=np.float32)
    n = 8
    x_per_core = [[xp[c], xs[c // 4]] for c in range(n)]
    _, nc = build_program(cfg)
    maps = make_in_maps(cfg, inputs, n, x_per_core)
    res = run_bass_kernel_spmd(nc, maps, core_ids=list(range(n)))
    y_prompt = np.stack([np.ascontiguousarray(res.results[c]["yT0"].T) for c in range(n)], axis=0).astype(np.float32)
    y_sample = np.stack([np.ascontiguousarray(res.results[4 * s]["yT1"].T) for s in range(2)], axis=0).astype(np.float32)
    return (y_prompt, y_sample)
```

```python
import contextlib
import math
import numpy as np
import concourse.bass as bass
import concourse.mybir as mybir
from concourse.bass_utils import run_bass_kernel_spmd

F32 = mybir.dt.float32
BF16 = mybir.dt.bfloat16
AF = mybir.ActivationFunctionType
ALU = mybir.AluOpType

D = 1024
H = 8
DFF = 2816
NFF = DFF // 128
ALPHA = (2 * 4) ** 0.25
EPS = 1e-5
PI = math.pi


class Dep:
    __slots__ = ("name", "w", "r", "sems", "wdma")

    def __init__(self, name=""):
        self.name = name
        self.w = None
        self.r = {}
        self.sems = {}
        self.wdma = None


class Tile:
    def __init__(self, t, name):
        self.t = t
        self.dep = Dep(name)
        self.name = name

    def __getitem__(self, key):
        return self.t[key]


class KB:
    ENG = ("pe", "act", "dve", "pool", "sp")

    def __init__(self):
        self.nc = bass.Bass("TRN2", target_bir_lowering=False)
        self.es = contextlib.ExitStack()
        self.streams = {e: [] for e in self.ENG}
        self.cnt = {e: 0 for e in self.ENG}
        self.seen = {e: {} for e in self.ENG}
        self.esem = {}
        for e in self.ENG:
            self.esem[e] = self.es.enter_context(self.nc.semaphore("s_" + e))
        self.nsem = 0
        self.dma_sems = []
        self.free_recs = {"sw": [], "hw": []}
        self.active = []
        self.semowner = {}
        self.uid = 0
        self.ninst = 0
        self.ccsem = None
        self.cccnt = 0
        self.blockno = 0
        self.rcache = {}

    def sb(self, name, shape, dtype, stack=None):
        self.uid += 1
        st = stack if stack is not None else self.es
        t = st.enter_context(self.nc.sbuf_tensor(f"{name}_{self.uid}", list(shape), dtype))
        return Tile(t, name)

    def ps(self, name, shape, dtype=F32, stack=None):
        self.uid += 1
        st = stack if stack is not None else self.es
        t = st.enter_context(self.nc.psum_tensor(f"{name}_{self.uid}", list(shape), dtype))
        return Tile(t, name)

    def dram(self, name, shape, dtype, kind="Internal"):
        return self.nc.dram_tensor(name, list(shape), dtype, kind=kind)

    def _dsem(self, dep, q):
        if q not in dep.sems:
            if self.free_recs[q]:
                rec = self.free_recs[q].pop()
                rec[2] = False
            else:
                self.nsem += 1
                sem = self.es.enter_context(self.nc.semaphore(f"d{self.nsem}"))
                rec = [sem, 0, False]
                self.dma_sems.append(rec)
                self.semowner[id(sem)] = rec
            dep.sems[q] = rec
            self.active.append((dep, q))
        return dep.sems[q]

    def _need(self, eng, waits, ev):
        if ev is None:
            return
        sem, val = ev
        key = id(sem)
        ow = self.semowner.get(key)
        if ow is not None:
            ow[2] = True
        if self.seen[eng].get(key, 0) >= val:
            return
        if sem is self.esem.get(eng) and eng in ("pe", "sp"):
            return
        self.seen[eng][key] = val
        waits[key] = (sem, max(val, waits.get(key, (sem, 0))[1]))

    def _deps(self, eng, reads, writes, same_gen=None):
        waits = {}
        for d in reads:
            self._need(eng, waits, d.w)
        for d in writes:
            if not (same_gen is not None and d.wdma is same_gen):
                self._need(eng, waits, d.w)
            for key, (sem, val) in d.r.items():
                self._need(eng, waits, (sem, val))
        return list(waits.values())

    def _commit(self, ev, reads, writes):
        sem, val = ev
        k = id(sem)
        for d in reads:
            if d.r.get(k, (sem, 0))[1] < val:
                d.r[k] = (sem, val)
            d.wdma = None
        for d in writes:
            d.w = ev
            d.r = {}
            d.wdma = None

    @staticmethod
    def _d(x):
        out = []
        for v in x:
            if v is None:
                continue
            out.append(v.dep if isinstance(v, Tile) else v)
        return out

    def op(self, eng, fn, reads=(), writes=()):
        reads = self._d(reads)
        writes = self._d(writes)
        waits = self._deps(eng, reads, writes)
        self.cnt[eng] += 1
        ev = (self.esem[eng], self.cnt[eng])
        self.streams[eng].append((waits, fn, self.esem[eng], 1))
        self._commit(ev, reads, writes)
        self.ninst += 1
        return ev

    def dma(self, eng, out, in_, reads=(), writes=(), owner=None, **kw):
        reads = self._d(reads)
        writes = self._d(writes)
        owner = owner.dep if isinstance(owner, Tile) else owner
        rec = self._dsem(owner, "sw" if eng == "pool" else "hw")
        sem = rec[0]
        waits = self._deps(eng, reads, writes, same_gen=sem)
        if rec[2] and rec[1] > 0:
            w2 = {}
            self._need(eng, w2, (sem, rec[1]))
            if w2:
                waits = [x for x in waits if x[0] is not sem] + [(sem, rec[1])]
        rec[2] = False
        rec[1] += 16
        ev = (sem, rec[1])

        def fn(e, out=out, in_=in_, kw=kw):
            o_ = out(e) if callable(out) else out
            i_ = in_(e) if callable(in_) else in_
            return e.dma_start(out=o_, in_=i_, **kw)
        self.streams[eng].append((waits, fn, sem, 16))
        self._commit(ev, reads, writes)
        if owner in writes:
            owner.wdma = sem
        self.ninst += 1
        return ev

    def collective(self, ins_ap, outs_ap, groups, reads=(), writes=()):
        reads = self._d(reads)
        writes = self._d(writes)
        if self.ccsem is None:
            self.ccsem = self.es.enter_context(self.nc.semaphore("ccsem"))
        waits = self._deps("pool", reads, writes)
        self.cccnt += 1
        ev = (self.ccsem, self.cccnt)

        def fn(e):
            return e.collective_compute("AllGather", mybir.AluOpType.bypass, replica_groups=groups,
                                        ins=[ins_ap.opt()], outs=[outs_ap.opt()])
        self.streams["pool"].append((waits, fn, self.ccsem, 1))
        self._commit(ev, reads, writes)
        return ev

    def rbase(self, e, q, mult):
        key = (q, mult)
        if key not in self.rcache:
            self.rcache[key] = (e.partition_id() % 4) * mult
        return self.rcache[key]

    def barrier(self):
        evs = [(self.esem[e], self.cnt[e]) for e in self.ENG if self.cnt[e] > 0]
        if self.cccnt > 0:
            evs.append((self.ccsem, self.cccnt))
        for rec in self.dma_sems:
            if rec[1] > 0:
                evs.append((rec[0], rec[1]))
        for e in self.ENG:
            waits = {}
            for ev in evs:
                self._need(e, waits, ev)
            if waits:
                self.streams[e].append((list(waits.values()), None, None, 0))

    def flush(self):
        self.barrier()
        nc = self.nc
        self.blockno += 1
        hmap = {"pe": "tensor", "act": "scalar", "dve": "vector", "pool": "gpsimd", "sp": "sync"}
        with nc.Block() as block:
            for e in self.ENG:
                stream = self.streams[e]

                def body(eng, stream=stream):
                    for waits, fn, sem, inc in stream:
                        for (s, v) in waits:
                            eng.wait_ge(s, v)
                        if fn is not None:
                            fn(eng).then_inc(sem, inc)
                getattr(block, hmap[e])(body)
        self.streams = {e: [] for e in self.ENG}
        for dep, q in self.active:
            self.free_recs[q].append(dep.sems.pop(q))
        self.active = []

    def finish(self):
        self.flush()
        self.es.close()
        return self.nc


def _consts_for_seq(L, N1, N2):
    c = {}
    n1 = np.arange(N1)[:, None]
    k1 = np.arange(N1)[None, :]
    ang = 2 * np.pi * n1 * k1 / N1
    c["FA"] = np.concatenate([np.cos(ang), -np.sin(ang)], axis=1).astype(np.float32)
    n2 = np.arange(N2)[:, None]
    ang = 2 * np.pi * n2 * np.arange(N1)[None, :] / L
    c["TW"] = np.stack([np.cos(ang), -np.sin(ang)], axis=1).astype(np.float32)
    k2 = np.arange(N2)[None, :]
    ang = 2 * np.pi * n2 * k2 / N2
    Cm, Sm = np.cos(ang), np.sin(ang)
    c["FC1"] = np.concatenate([Cm, -Sm], axis=1).astype(np.float32)
    c["FC2"] = np.concatenate([Sm, Cm], axis=1).astype(np.float32)
    pos = np.arange(L, dtype=np.float32)
    inv = (10000.0 ** (-np.arange(0, 32, 2, dtype=np.float32) / 32)).astype(np.float32)
    a = (pos[None, :] * inv[:, None]).astype(np.float32)
    cos2 = np.concatenate([np.cos(a), np.cos(a)], axis=0)
    sin2 = np.concatenate([-np.sin(a), np.sin(a)], axis=0)
    scale = 96 ** -0.5
    c["ROPE"] = np.stack([np.stack([cos2 * scale, sin2 * scale]), np.stack([cos2, sin2])]).astype(np.float32)
    return c


def _global_consts():
    c = {}
    c["ONES"] = np.ones((128, 128), np.float32)
    j = np.arange(64)
    ang = 2 * np.pi * j[:, None] * j[None, :] / 64
    C64 = np.zeros((128, 128), np.float32)
    S64 = np.zeros((128, 128), np.float32)
    for b in range(2):
        C64[b * 64:(b + 1) * 64, b * 64:(b + 1) * 64] = np.cos(ang)
        S64[b * 64:(b + 1) * 64, b * 64:(b + 1) * 64] = np.sin(ang)
    c["CS64"] = np.stack([C64, S64], axis=1).astype(np.float32)
    I2 = np.zeros((128, 64), np.float32)
    I2[np.arange(128), np.arange(128) % 64] = 1.0
    c["I2"] = I2
    I16 = np.zeros((16, 16), np.float32)
    I16[np.arange(16), np.arange(16)] = 1.0
    c["I16"] = I16
    m = np.zeros((128, 4), np.float32)
    for p in range(128):
        m[p, (p % 64) // 16] = 1.0
    c["EMASK"] = m
    return c


def _stage_params(inp, depth):
    o = {}
    f = lambda a: np.ascontiguousarray(np.asarray(a, dtype=np.float32))
    w_in = f(inp["w_in"])
    o["w_in"] = w_in
    kpe = w_in[:, :, 896:928]
    o["w_kpesw"] = f(np.concatenate([kpe[:, :, 16:32], kpe[:, :, 0:16]], axis=2))
    w_uq = f(inp["w_uq"]).reshape(depth, 256, H, 96)
    o["w_uq"] = f(w_uq.reshape(depth, 256, 768))
    sw = np.concatenate([w_uq[..., 0:64], w_uq[..., 80:96], w_uq[..., 64:80]], axis=-1)
    o["w_uq_sw"] = f(sw.reshape(depth, 256, 768))
    w_ukv = f(inp["w_ukv"]).reshape(depth, 128, H, 128)
    o["w_uk"] = f(w_ukv[..., 0:64].reshape(depth, 128, 512))
    o["w_uv"] = f(w_ukv[..., 64:128].reshape(depth, 128, 512))
    for n in ("w_glu", "w_br_f", "w_br_s", "w_br_a", "w_o", "w_gu", "w_down"):
        o[n] = f(inp[n])
    def pk(v, kc):
        return f(v).reshape(depth, kc, 128).transpose(0, 2, 1)
    bg = f(inp["b_gate"]).reshape(depth, 3 * 8, 128).transpose(0, 2, 1)
    o["vecs"] = f(np.concatenate([pk(inp["ln1_g"], 8), pk(inp["ln1_b"], 8), pk(inp["ln2_g"], 8), pk(inp["ln2_b"], 8),
                                  bg, pk(inp["q_norm_g"], 2), pk(inp["kv_norm_g"], 1)], axis=2))
    are, aim, ldt = f(inp["s5_a_re"]), f(inp["s5_a_im"]), f(inp["s5_log_dt"])
    def layA(a):
        t = a.reshape(depth, 32, 64).transpose(0, 2, 1)
        return np.concatenate([t, t], axis=1)
    ldt_full = np.broadcast_to(ldt[..., None], (depth, 2, 16, 64))
    o["s5A_a"] = f(np.stack([layA(are), layA(aim), layA(ldt_full)], axis=2))
    def layA3(b):
        t = b.reshape(depth, 32, 64, 16).transpose(0, 2, 1, 3)
        return np.concatenate([t, t], axis=1)
    o["s5A_b"] = f(np.stack([layA3(f(inp["s5_b_re"])), layA3(f(inp["s5_b_im"]))], axis=2))
    cre = f(inp["s5_c_re"]).transpose(0, 1, 2, 4, 3)
    cim = f(inp["s5_c_im"]).transpose(0, 1, 2, 4, 3)
    o["s5A_c"] = f(np.stack([layA3(cre), layA3(cim)], axis=2))
    def layB(a):
        t = np.repeat(a[:, :, :, None, :], 16, axis=3).reshape(depth, 2, 256, 64)
        t = t.reshape(depth, 2, 2, 128, 64).transpose(0, 3, 2, 1, 4)
        return t
    o["s5B_a"] = f(np.stack([layB(are), layB(aim), layB(ldt_full)], axis=2))
    def layBb(b):
        t = b.transpose(0, 1, 2, 4, 3).reshape(depth, 2, 256, 64)
        return t.reshape(depth, 2, 2, 128, 64).transpose(0, 3, 2, 1, 4)
    o["s5B_b"] = f(np.stack([layBb(f(inp["s5_b_re"])), layBb(f(inp["s5_b_im"]))], axis=2))
    o["s5_dT"] = f(f(inp["s5_d"]).reshape(depth, 16, 16).transpose(0, 2, 1))
    return o


WSHAPES = {
    "w_in": (1024, 4000), "w_kpesw": (1024, 32), "w_uq": (256, 768), "w_uq_sw": (256, 768),
    "w_uk": (128, 512), "w_uv": (128, 512), "w_glu": (256, 256), "w_br_f": (256, 1024),
    "w_br_s": (256, 1024), "w_br_a": (512, 1024), "w_o": (1024, 1024), "w_gu": (1024, 5632),
    "w_down": (2816, 1024),
}
PSHAPES = {"vecs": (128, 59), "s5A_a": (128, 3, 32), "s5A_b": (128, 2, 32, 16), "s5A_c": (128, 2, 32, 16),
           "s5B_a": (128, 3, 2, 2, 64), "s5B_b": (128, 2, 2, 2, 64), "s5_dT": (16, 16)}


class Prog:
    def __init__(self, cfg):
        self.cfg = cfg
        self.depth = cfg["depth"]
        self.seqs = cfg["seqs"]
        self.SEG = cfg["SEG"]
        self.debug = cfg.get("debug", False)
        self.k = KB()
        self.nc = self.k.nc
        self.dr = {}
        self.build()

    def din(self, name, shape, dtype=F32):
        t = self.k.dram(name, shape, dtype, kind="ExternalInput")
        self.dr[name] = t
        return t

    def dscr(self, name, shape, dtype, out=False):
        kind = "ExternalOutput" if (out or self.debug) else "Internal"
        t = self.k.dram(name, shape, dtype, kind=kind)
        self.dr[name] = t
        return t

    def dbg(self, name, tile, shape, dtype=F32, deps=None):
        if not self.debug:
            return
        t = self.k.dram("dbg_" + name, list(shape), dtype, kind="ExternalOutput")
        self.dr["dbg_" + name] = t
        self.k.dma("sp", t.ap(), tile[:], reads=[tile] + list(deps or []), owner=tile)

    def bank(self, b):
        return self.pst[b // 2][:, (b % 2) * 512:(b % 2) * 512 + 512]

    def nextbank(self):
        b = self.bankrr
        self.bankrr = (self.bankrr + 1) % 8
        return b

    def build(self):
        k = self.k
        depth = self.depth
        for n, (K_, N_) in WSHAPES.items():
            self.din(n, [depth, K_, N_])
            self.dscr("b_" + n, [depth, K_, N_], BF16)
        for n, shp in PSHAPES.items():
            self.din(n, [depth] + list(shp))
        gc = _global_consts()
        for n, v in gc.items():
            self.din("c_" + n, list(v.shape))
        for si, s in enumerate(self.seqs):
            L = s["L"]
            self.din(f"xT{si}", [D, L])
            cs = _consts_for_seq(L, s["N1"], s["N2"])
            for n, v in cs.items():
                self.din(f"c{si}_{n}", list(v.shape))
            if not s.get("split"):
                self.dscr(f"yT{si}", [D, L], F32, out=True)
                self.dscr(f"xb{si}", [D, L], F32)
            self.dscr(f"hf{si}", [4, L, 64], F32)
            self.dscr(f"hsT{si}", [256, L], F32)
            self.dscr(f"cqT{si}", [256, L], F32)
            self.dscr(f"ckvT{si}", [128, L], F32)
            self.dscr(f"kpeT{si}", [64, L], F32)
            self.dscr(f"fT{si}", [2, 256, L], BF16)
            self.dscr(f"s5T{si}", [256, L], BF16)
            self.dscr(f"attnT{si}", [512, L], BF16)
            self.dscr(f"qT{si}", [H, 96, L], BF16)
            self.dscr(f"kT{si}", [H, 96, L], BF16)
            self.dscr(f"v{si}", [L, 512], BF16)
            nseg = L // self.SEG
            self.dscr(f"sbst{si}", [nseg, 128, 16, self.SEG // 8 + 2], BF16)
            if s.get("split"):
                Lq = L // 4
                self.dscr(f"yq{si}", [D, Lq], F32, out=True)
                self.dscr(f"xq{si}", [Lq // 256, D, 256], F32)
                self.dscr(f"xg{si}", [Lq // 256, 4 * D, 256], F32)
                self.dscr(f"attnq{si}", [512, Lq], BF16)
                self.dscr(f"qm{si}", [H, 96, Lq], BF16)
                self.dscr(f"fm{si}", [2, 256, Lq], BF16)
                self.dscr(f"s5m{si}", [256, Lq], BF16)
                self.dscr(f"xm{si}", [D, Lq], F32)
        self.dscr("Kd", [16, 15, 16, 16], F32)
        self.dscr("rotd", [2, 12, 128, 16, 128], BF16)

        self.pst = [k.ps(f"ps{i}", [128, 1024]) for i in range(4)]
        self.bdep = [Dep(f"bank{i}") for i in range(8)]
        self.bankrr = 0
        self.ones = k.sb("ones", [128, 128], F32)
        k.dma("sp", self.ones[:], self.dr["c_ONES"].ap(), writes=[self.ones], owner=self.ones)

        cd = Dep("cast")
        for n, (K_, N_) in WSHAPES.items():
            for l in range(depth):
                rstep = 256
                for r0 in range(0, K_, rstep):
                    r1 = min(K_, r0 + rstep)
                    k.dma("pool", self.dr["b_" + n].ap()[l, r0:r1, :], self.dr[n].ap()[l, r0:r1, :],
                          writes=[cd], owner=cd)
        k.flush()

        for l in range(depth):
            self.layer(l)

    def xsrc(self, l, si):
        return self.dr[f"xT{si}"] if l == 0 else self.dr[f"xb{si}"]

    def xblock(self, l, si, b):
        s = self.seqs[si]
        if s.get("split") and l > 0:
            nbq = (s["L"] // 4) // 512
            r, j = b // nbq, b % nbq
            xg = self.dr[f"xg{si}"].ap()
            return [(slice(h * 256, (h + 1) * 256), xg[2 * j + h, r * D:(r + 1) * D, :].rearrange("(kc p) t -> p kc t", p=128)) for h in range(2)]
        return [(slice(0, 512), self.xsrc(l, si).ap().rearrange("(kc p) t -> p kc t", p=128)[:, :, b * 512:(b + 1) * 512])]

    def xdst(self, l, si):
        return self.dr[f"yT{si}"] if l == self.depth - 1 else self.dr[f"xb{si}"]

    def layer(self, l):
        k = self.k
        phases = self.cfg.get("phases", ("p1", "s5", "fourier", "mla", "p3"))
        if "p1" in phases:
            with contextlib.ExitStack() as st:
                self.p1_setup(l, st)
                for si in range(len(self.seqs)):
                    self.p1(l, si, st)
                k.flush()
        if "s5" in phases:
            with contextlib.ExitStack() as st:
                self.s5_prep(l, st)
                for si in range(len(self.seqs)):
                    self.s5_seq(l, si, st)
                k.flush()
        if "fourier" in phases:
            for si in range(len(self.seqs)):
                with contextlib.ExitStack() as st:
                    self.fourier(l, si, st)
                    k.flush()
        if "mla" in phases:
            for si in range(len(self.seqs)):
                with contextlib.ExitStack() as st:
                    self.mla_pre(l, si, st)
                    k.flush()
                with contextlib.ExitStack() as st:
                    self.attn(l, si, st)
                    k.flush()
        if "p3" in phases:
            with contextlib.ExitStack() as st:
                self.p3_setup(l, st)
                for si in range(len(self.seqs)):
                    self.p3(l, si, st)
                k.flush()
            if l < self.depth - 1:
                for si, s_ in enumerate(self.seqs):
                    if s_.get("split"):
                        for c_ in range((s_["L"] // 4) // 256):
                            k.collective(self.dr[f"xq{si}"].ap()[c_], self.dr[f"xg{si}"].ap()[c_], [[0, 1, 2, 3], [4, 5, 6, 7]])
                        k.flush()

    def p1_setup(self, l, st):
        k = self.k
        self.W1 = k.sb("W1", [128, 8, 960], BF16, st)
        win = self.dr["b_w_in"].ap()[l].rearrange("(kc p) n -> p kc n", p=128)
        k.dma("sp", self.W1[:, :, 0:928], win[:, :, 0:928], writes=[self.W1], owner=self.W1)
        wsw = self.dr["b_w_kpesw"].ap()[l].rearrange("(kc p) n -> p kc n", p=128)
        k.dma("sp", self.W1[:, :, 928:960], wsw, writes=[self.W1], owner=self.W1)
        self.p1_xf = [k.sb(f"xf{i}", [128, 8, 512], F32, st) for i in range(2)]
        self.p1_xb = [k.sb(f"xbf{i}", [128, 8, 512], BF16, st) for i in range(2)]
        self.p1_st = [k.sb(f"p1st{i}", [128, 6, 512], F32, st) for i in range(2)]
        self.p1_hf = [k.sb(f"p1hf{i}", [128, 4, 256], F32, st) for i in range(2)]
        self.p1_i = 0

    def p1(self, l, si, st):
        k = self.k
        L = self.seqs[si]["L"]
        W1 = self.W1
        for b in range(L // 512):
            i = self.p1_i % 2
            self.p1_i += 1
            xf, xb, stg, hfs = self.p1_xf[i], self.p1_xb[i], self.p1_st[i], self.p1_hf[i]
            ts = slice(b * 512, (b + 1) * 512)
            for (dsl, sap) in self.xblock(l, si, b):
                k.dma("sp", xf[:, :, dsl], sap, writes=[xf], owner=xf)
            k.op("act", lambda e, xf=xf, xb=xb: e.activation(xb[:, 0:4, :], xf[:, 0:4, :], AF.Copy), reads=[xf], writes=[xb])
            k.op("pool", lambda e, xf=xf, xb=xb: e.tensor_copy(xb[:, 4:8, :], xf[:, 4:8, :]), reads=[xf], writes=[xb])
            for cc in range(6):
                c0 = 256 + cc * 128
                M = 128 if cc < 5 else 64
                bk = self.nextbank()
                for kc in range(8):
                    k.op("pe", lambda e, bk=bk, kc=kc, c0=c0, M=M, xb=xb: e.matmul(
                        self.bank(bk)[0:M, :], lhsT=W1[:, kc, c0:c0 + M], rhs=xb[:, kc, :],
                        start=(kc == 0), stop=(kc == 7)), reads=[W1, xb], writes=[self.bdep[bk]])
                eng = "act" if cc % 2 == 0 else "dve"
                if eng == "act":
                    k.op("act", lambda e, bk=bk, cc=cc, M=M, stg=stg: e.activation(stg[0:M, cc, :], self.bank(bk)[0:M, :], AF.Copy),
                         reads=[self.bdep[bk]], writes=[stg])
                else:
                    k.op("dve", lambda e, bk=bk, cc=cc, M=M, stg=stg: e.tensor_copy(stg[0:M, cc, :], self.bank(bk)[0:M, :]),
                         reads=[self.bdep[bk]], writes=[stg])
            k.dma("pool", self.dr[f"hsT{si}"].ap().rearrange("(c p) t -> p c t", p=128)[:, :, ts], stg[:, 0:2, :], reads=[stg], owner=stg)
            k.dma("pool", self.dr[f"cqT{si}"].ap().rearrange("(c p) t -> p c t", p=128)[:, :, ts], stg[:, 2:4, :], reads=[stg], owner=stg)
            k.dma("pool", self.dr[f"ckvT{si}"].ap()[:, ts], stg[:, 4, :], reads=[stg], owner=stg)
            k.dma("pool", self.dr[f"kpeT{si}"].ap()[:, ts], stg[0:64, 5, :], reads=[stg], owner=stg)
            for sub in range(4):
                bk = self.nextbank() if sub % 2 == 0 else bk
                o0 = (sub % 2) * 256
                for kc in range(8):
                    k.op("pe", lambda e, bk=bk, kc=kc, o0=o0, sub=sub, xb=xb: e.matmul(
                        self.bank(bk)[:, o0:o0 + 256], lhsT=xb[:, kc, sub * 128:(sub + 1) * 128], rhs=W1[:, kc, 0:256],
                        start=(kc == 0), stop=(kc == 7)), reads=[W1, xb], writes=[self.bdep[bk]])
                if sub % 2 == 1:
                    k.op("act", lambda e, bk=bk, sub=sub, hfs=hfs: e.activation(
                        hfs[:, sub - 1:sub + 1, :], self.bank(bk).rearrange("p (a b) -> p a b", a=2), AF.Copy),
                        reads=[self.bdep[bk]], writes=[hfs])
            hfd = self.dr[f"hf{si}"].ap().rearrange("g (n p) c -> p n g c", p=128)
            for g in range(4):
                k.dma("pool", hfd[:, b * 4:(b + 1) * 4, g, :], hfs[:, :, g * 64:(g + 1) * 64], reads=[hfs], owner=hfs)

    def cmul(self, eng, outr, outi, ar, ai, br, bi, t1, t2, reads, writes, tmps):
        k = self.k
        E = eng
        k.op(E, lambda e: e.tensor_tensor(t1, ar, br, ALU.mult), reads=reads, writes=tmps)
        k.op(E, lambda e: e.tensor_tensor(t2, ai, bi, ALU.mult), reads=reads, writes=tmps)
        k.op(E, lambda e: e.tensor_tensor(outr, t1, t2, ALU.subtract), reads=tmps, writes=writes)
        k.op(E, lambda e: e.tensor_tensor(t1, ar, bi, ALU.mult), reads=reads + writes, writes=tmps)
        k.op(E, lambda e: e.tensor_tensor(t2, ai, br, ALU.mult), reads=reads, writes=tmps)
        k.op(E, lambda e: e.tensor_tensor(outi, t1, t2, ALU.add), reads=tmps, writes=writes)

    def abar_calc(self, st, tag, src, F):
        k = self.k
        R = k.sb("abR" + tag, [128, 4, F], F32, st)
        T = k.sb("abT" + tag, [128, 8, F], F32, st)
        are, aim, ldt = src[:, 0, :], src[:, 1, :], src[:, 2, :]
        dt, xr, xi, mag, s, c, m, ab = [T[:, i, :] for i in range(8)]
        k.op("act", lambda e: e.activation(dt, ldt, AF.Exp), reads=[src], writes=[T])
        k.op("dve", lambda e: e.tensor_tensor(xr, are, dt, ALU.mult), reads=[src, T], writes=[T])
        k.op("dve", lambda e: e.tensor_tensor(xi, aim, dt, ALU.mult), reads=[src, T], writes=[T])
        k.op("act", lambda e: e.activation(mag, xr, AF.Exp), reads=[T], writes=[T])
        for _ in range(4):
            k.op("dve", lambda e: e.tensor_scalar(m, xi, PI, -2 * PI, ALU.is_gt, ALU.mult), reads=[T], writes=[T])
            k.op("dve", lambda e: e.tensor_tensor(xi, xi, m, ALU.add), reads=[T], writes=[T])
        k.op("act", lambda e: e.activation(s, xi, AF.Sin), reads=[T], writes=[T])
        k.op("act", lambda e: e.activation(ab, xi, AF.Abs), reads=[T], writes=[T])
        k.op("act", lambda e: e.activation(c, ab, AF.Sin, scale=-1.0, bias=PI / 2), reads=[T], writes=[T])
        k.op("dve", lambda e: e.tensor_tensor(R[:, 0, :], mag, c, ALU.mult), reads=[T], writes=[R])
        k.op("dve", lambda e: e.tensor_tensor(R[:, 1, :], mag, s, ALU.mult), reads=[T], writes=[R])
        den, rden, ir, ii, nr = dt, xr, mag, s, c
        k.op("dve", lambda e: e.tensor_tensor(den, are, are, ALU.mult), reads=[src, T], writes=[T])
        k.op("dve", lambda e: e.tensor_tensor(m, aim, aim, ALU.mult), reads=[src, T], writes=[T])
        k.op("dve", lambda e: e.tensor_tensor(den, den, m, ALU.add), reads=[T], writes=[T])
        k.op("dve", lambda e: e.reciprocal(rden, den), reads=[T], writes=[T])
        k.op("dve", lambda e: e.tensor_tensor(ir, are, rden, ALU.mult), reads=[src, T], writes=[T])
        k.op("dve", lambda e: e.scalar_tensor_tensor(ii, aim, -1.0, rden, ALU.mult, ALU.mult), reads=[src, T], writes=[T])
        k.op("dve", lambda e: e.tensor_scalar(nr, R[:, 0, :], -1.0, None, ALU.add), reads=[R, T], writes=[T])
        self.cmul("dve", R[:, 2, :], R[:, 3, :], nr, R[:, 1, :], ir, ii, xi, ab, [T, R], [R], [T])
        return R

    def s5_prep(self, l, st):
        k = self.k
        NJ = self.SEG // 8
        self.KcT = k.sb("KcT", [128, 2, 15, 64], BF16, st)
        self.EL = k.sb("EL", [128, 2, 4, 2, 8, 128], BF16, st)
        self.WL = k.sb("WL", [128, 32, 8, 64], BF16, st)
        self.SQ = k.sb("SQ", [128, 2, 12, 32], F32, st)
        self.I2 = k.sb("I2", [128, 64], F32, st)
        self.wglu = k.sb("wglu", [128, 2, 256], BF16, st)
        k.dma("sp", self.I2[:], self.dr["c_I2"].ap(), writes=[self.I2], owner=self.I2)
        k.dma("sp", self.wglu[:], self.dr["b_w_glu"].ap()[l].rearrange("(kc p) n -> p kc n", p=128), writes=[self.wglu], owner=self.wglu)
        k.op("pool", lambda e: e.memset(self.KcT[:], 0.0), writes=[self.KcT])
        k.op("pool", lambda e: e.memset(self.WL[:], 0.0), writes=[self.WL])
        with contextlib.ExitStack() as ps:
            sA = k.sb("sA", [128, 3, 32], F32, ps)
            bA = k.sb("bA", [128, 2, 32, 16], F32, ps)
            cA = k.sb("cA", [128, 2, 32, 16], F32, ps)
            k.dma("sp", sA[:], self.dr["s5A_a"].ap()[l], writes=[sA], owner=sA)
            k.dma("sp", bA[:], self.dr["s5A_b"].ap()[l], writes=[bA], owner=bA)
            k.dma("sp", cA[:], self.dr["s5A_c"].ap()[l], writes=[cA], owner=cA)
            RA = self.abar_calc(ps, "A", sA, 32)
            PW = k.sb("PWA", [128, 2, 9, 32], F32, ps)
            tA = k.sb("tA", [128, 2, 32], F32, ps)
            k.op("dve", lambda e: e.memset(PW[:, 0, 0, :], 1.0), writes=[PW])
            k.op("dve", lambda e: e.memset(PW[:, 1, 0, :], 0.0), writes=[PW])
            for t in range(1, 9):
                self.cmul("dve", PW[:, 0, t, :], PW[:, 1, t, :], PW[:, 0, t - 1, :], PW[:, 1, t - 1, :], RA[:, 0, :], RA[:, 1, :],
                          tA[:, 0, :], tA[:, 1, :], [PW, RA], [PW], [tA])
            SQ = self.SQ
            SQf = k.sb("SQf", [128, 2, 12, 32], F32, ps)
            k.op("dve", lambda e: e.tensor_copy(SQf[:, :, 0, :], PW[:, :, 8, :]), reads=[PW], writes=[SQf])
            for m in range(1, 12):
                self.cmul("dve", SQf[:, 0, m, :], SQf[:, 1, m, :], SQf[:, 0, m - 1, :], SQf[:, 1, m - 1, :],
                          SQf[:, 0, m - 1, :], SQf[:, 1, m - 1, :], tA[:, 0, :], tA[:, 1, :], [SQf], [SQf], [tA])
            k.op("dve", lambda e: e.tensor_copy(SQ[:, 0, :, :], SQf[:, 0, :, :]), reads=[SQf], writes=[SQ])
            k.op("dve", lambda e: e.tensor_copy(SQ[0:64, 1, :, :], SQf[0:64, 1, :, :]), reads=[SQf], writes=[SQ])
            k.op("dve", lambda e: e.tensor_scalar(SQ[64:128, 1, :, :], SQf[64:128, 1, :, :], -1.0, None, ALU.mult), reads=[SQf], writes=[SQ])
            nst = 0
            dd_ = 1
            while dd_ < self.SEG // 8 + 1:
                nst += 1
                dd_ *= 2
            self.nst = nst
            RT = [k.sb(f"RT{i}", [128, 16, 128], BF16, ps) for i in range(2)]
            self.rotdep = Dep("rotd")
            ri = 0
            for dirn in range(2):
                for mi in range(nst):
                    R = RT[ri % 2]
                    ri += 1
                    ar = SQ[:, 0, mi, dirn * 16:(dirn + 1) * 16]
                    ai = SQ[:, 1, mi, dirn * 16:(dirn + 1) * 16]
                    for (r0, c0, src) in ((0, 0, ar), (0, 64, ai), (64, 0, ai), (64, 64, ar)):
                        k.op("dve", lambda e, r0=r0, c0=c0, src=src, R=R: e.tensor_tensor(
                            R[r0:r0 + 64, :, c0:c0 + 64], self.I2[r0:r0 + 64, :].unsqueeze(1).broadcast_to([64, 16, 64]),
                            src[r0:r0 + 64, :].unsqueeze(2).broadcast_to([64, 16, 64]), ALU.mult), reads=[self.I2, SQ], writes=[R])
                    k.dma("sp", self.dr["rotd"].ap()[dirn, mi], R[:], reads=[R], writes=[self.rotdep], owner=R)
            bbA = k.sb("bbA", [128, 2, 32, 16], F32, ps)
            tB = k.sb("tB", [128, 2, 32, 16], F32, ps)
            cr = RA[:, 2, :].unsqueeze(2).broadcast_to([128, 32, 16])
            ci = RA[:, 3, :].unsqueeze(2).broadcast_to([128, 32, 16])
            self.cmul("dve", bbA[:, 0], bbA[:, 1], cr, ci, bA[:, 0], bA[:, 1], tB[:, 0], tB[:, 1], [RA, bA], [bbA], [tB])
            CAf = k.sb("CAf", [128, 2, 16, 9, 16], F32, ps)
            tC = k.sb("tC", [128, 2, 16, 9, 16], F32, ps)
            CAS = k.sb("CAS", [128, 32, 9, 16], F32, ps)
            shp = [128, 16, 9, 16]
            for dd_ in range(2):
                gs_ = slice(dd_ * 16, dd_ * 16 + 16)
                pr = PW[:, 0, :, gs_].rearrange("p t g -> p g t").unsqueeze(3).broadcast_to(shp)
                pi_ = PW[:, 1, :, gs_].rearrange("p t g -> p g t").unsqueeze(3).broadcast_to(shp)
                c_r = cA[:, 0, gs_, :].unsqueeze(2).broadcast_to(shp)
                c_i = cA[:, 1, gs_, :].unsqueeze(2).broadcast_to(shp)
                self.cmul("dve", CAf[:, 0], CAf[:, 1], c_r, c_i, pr, pi_, tC[:, 0], tC[:, 1], [PW, cA], [CAf], [tC])
                k.op("dve", lambda e, gs_=gs_: e.tensor_copy(CAS[0:64, gs_], CAf[0:64, 0]), reads=[CAf], writes=[CAS])
                k.op("dve", lambda e, gs_=gs_: e.tensor_scalar(CAS[64:128, gs_], CAf[64:128, 1], -1.0, None, ALU.mult), reads=[CAf], writes=[CAS])
            BBS = k.sb("BBS", [128, 32, 16], F32, ps)
            k.op("dve", lambda e: e.tensor_copy(BBS[0:64], bbA[0:64, 0]), reads=[bbA], writes=[BBS])
            k.op("dve", lambda e: e.tensor_copy(BBS[64:128], bbA[64:128, 1]), reads=[bbA], writes=[BBS])
            WLv = self.WL[:].rearrange("p (d h e) t (f c) -> p d h e t f c", d=2, h=4, e=4, f=4)
            CASv = CAS[:].rearrange("p (d h e) t c -> p d h e t c", d=2, h=4, e=4)
            for e_ in range(4):
                k.op("dve", lambda e, e_=e_: e.tensor_copy(WLv[:, 0, :, e_, :, e_, :], CASv[:, 0, :, e_, 1:9, :]), reads=[CAS], writes=[self.WL])
                for t in range(8):
                    k.op("pool", lambda e, e_=e_, t=t: e.tensor_copy(WLv[:, 1, :, e_, t, e_, :], CASv[:, 1, :, e_, 8 - t, :]), reads=[CAS], writes=[self.WL])
            Ksb = k.sb("Ksb", [16, 32, 8, 16], F32, ps)
            for q in range(8):
                bk = self.nextbank()
                for j in range(4):
                    kg = q * 4 + j
                    k.op("pe", lambda e, bk=bk, kg=kg, j=j: e.matmul(
                        self.bank(bk)[0:16, j * 128:(j + 1) * 128], lhsT=BBS[:, kg, :], rhs=CAS[:, kg, 0:8, :],
                        start=True, stop=True), reads=[BBS, CAS], writes=[self.bdep[bk]])
                k.op("act", lambda e, bk=bk, q=q: e.activation(Ksb[:, q * 4:(q + 1) * 4, :, :], self.bank(bk)[0:16, :].rearrange("p (a b c) -> p a b c", a=4, b=8), AF.Copy),
                     reads=[self.bdep[bk]], writes=[Ksb])
            Kcs = k.sb("Kcs", [16, 16, 15, 16], F32, ps)
            dT = k.sb("dT", [16, 16], F32, ps)
            I16 = k.sb("I16", [16, 16], F32, ps)
            dI = k.sb("dI", [16, 16, 16], F32, ps)
            k.dma("sp", dT[:], self.dr["s5_dT"].ap()[l], writes=[dT], owner=dT)
            k.dma("sp", I16[:], self.dr["c_I16"].ap(), writes=[I16], owner=I16)
            k.op("dve", lambda e: e.tensor_copy(Kcs[:, :, 8:15, :], Ksb[:, 0:16, 1:8, :]), reads=[Ksb], writes=[Kcs])
            for t in range(1, 8):
                k.op("dve", lambda e, t=t: e.tensor_copy(Kcs[:, :, 7 - t, :], Ksb[:, 16:32, t, :]), reads=[Ksb], writes=[Kcs])
            k.op("dve", lambda e: e.tensor_tensor(Kcs[:, :, 7, :], Ksb[:, 0:16, 0, :], Ksb[:, 16:32, 0, :], ALU.add), reads=[Ksb], writes=[Kcs])
            k.op("dve", lambda e: e.tensor_tensor(dI[:], I16[:].unsqueeze(1).broadcast_to([16, 16, 16]),
                                                   dT[:].unsqueeze(2).broadcast_to([16, 16, 16]), ALU.mult), reads=[I16, dT], writes=[dI])
            k.op("dve", lambda e: e.tensor_tensor(Kcs[:, :, 7, :], Kcs[:, :, 7, :], dI[:], ALU.add), reads=[dI, Kcs], writes=[Kcs])
            kdd = Dep("Kd")
            k.dma("sp", self.dr["Kd"].ap().rearrange("g s c d -> c g s d"), Kcs[:], reads=[Kcs], writes=[kdd], owner=Kcs)
            for g in range(16):
                tile_, hh, e_ = g // 8, (g % 8) // 4, g % 4
                p0 = 64 * hh + 16 * e_
                k.dma("pool", self.KcT[p0:p0 + 16, tile_, :, 16 * e_:16 * e_ + 16], self.dr["Kd"].ap()[g].rearrange("s c d -> c s d"),
                      reads=[kdd], writes=[self.KcT], owner=self.KcT)
            if l == 0:
                self.dbg("RA", RA, [128, 4, 32]); self.dbg("PW", PW, [128, 2, 9, 32]); self.dbg("CAS", CAS, [128, 32, 9, 16])
                self.dbg("BBS", BBS, [128, 32, 16]); self.dbg("Ksb", Ksb, [16, 32, 8, 16]); self.dbg("Kcs", Kcs, [16, 16, 15, 16])
                self.dbg("SQ", self.SQ, [128, 2, 12, 32])
            k.flush()
        with contextlib.ExitStack() as ps:
            sB = k.sb("sB", [128, 3, 256], F32, ps)
            bB = k.sb("bB", [128, 2, 256], F32, ps)
            EM = k.sb("EM", [128, 4], F32, ps)
            k.dma("sp", sB[:], self.dr["s5B_a"].ap()[l].rearrange("p a t d n -> p a (t d n)"), writes=[sB], owner=sB)
            k.dma("sp", bB[:], self.dr["s5B_b"].ap()[l].rearrange("p a t d n -> p a (t d n)"), writes=[bB], owner=bB)
            k.dma("sp", EM[:], self.dr["c_EMASK"].ap(), writes=[EM], owner=EM)
            RB = self.abar_calc(ps, "B", sB, 256)
            PB = k.sb("PB", [128, 2, 8, 256], F32, ps)
            tD = k.sb("tD", [128, 2, 256], F32, ps)
            self.cmul("dve", PB[:, 0, 0, :], PB[:, 1, 0, :], RB[:, 2, :], RB[:, 3, :], bB[:, 0, :], bB[:, 1, :], tD[:, 0, :], tD[:, 1, :], [RB, bB], [PB], [tD])
            for t in range(1, 8):
                self.cmul("dve", PB[:, 0, t, :], PB[:, 1, t, :], PB[:, 0, t - 1, :], PB[:, 1, t - 1, :], RB[:, 0, :], RB[:, 1, :],
                          tD[:, 0, :], tD[:, 1, :], [PB, RB], [PB], [tD])
            PBv = PB[:].rearrange("p r k (t d n) -> p r k t d n", t=2, d=2)
            for e_ in range(4):
                for d_ in range(2):
                    for t2 in range(8):
                        pw = 7 - t2 if d_ == 0 else t2
                        for r in range(2):
                            eng = "dve" if (t2 + r) % 2 == 0 else "pool"
                            k.op(eng, lambda e, e_=e_, d_=d_, t2=t2, pw=pw, r=r: e.tensor_scalar(
                                self.EL[:, :, e_, d_, t2, r * 64:(r + 1) * 64], PBv[:, r, pw, :, d_, :], EM[:, e_:e_ + 1], None, ALU.mult),
                                reads=[PB, EM], writes=[self.EL])
            if l == 0:
                self.dbg("RB", RB, [128, 4, 256]); self.dbg("PB", PB, [128, 2, 8, 256])
                self.dbg("EL", self.EL, [128, 2, 4, 2, 8, 128], BF16); self.dbg("WL", self.WL, [128, 32, 8, 64], BF16)
                self.dbg("KcT", self.KcT, [128, 2, 15, 64], BF16)
            k.flush()

    def s5_seq(self, l, si, st_outer):
        k = self.k
        L = self.seqs[si]["L"]
        SEG = self.SEG
        NJ = SEG // 8
        NC1 = NJ + 1
        nseg = L // SEG
        steps = []
        d_ = 1
        while d_ < NC1:
            steps.append(d_)
            d_ *= 2
        with contextlib.ExitStack() as st:
            uT = [k.sb(f"uT{i}", [128, 2, SEG], BF16, st) for i in range(2)]
            NCP = NC1 + (NC1 % 2)
            Sm = k.sb("Sm", [128, 16, NCP], F32, st)
            Sbf = [k.sb(f"Sbf{i}", [128, 16, NCP], BF16, st) for i in range(2)]
            k.op("dve", lambda e: e.memset(Sm[:], 0.0), writes=[Sm])
            Rot = [k.sb(f"Rot{i}", [128, 16, 128], BF16, st) for i in range(2)]
            carry = k.sb("carry", [128, 2, 16], F32, st)
            gF = k.sb("gF", [128, 2, SEG], F32, st)
            gB = k.sb("gB", [128, 2, SEG], BF16, st)
            sg = [k.sb(f"sg{i}", [128, 512], F32, st) for i in range(2)]
            so = [k.sb(f"so{i}", [128, 2, 512], BF16, st) for i in range(2)]
            hsd = self.dr[f"hsT{si}"].ap().rearrange("(c p) t -> p c t", p=128)
            sbst = self.dr[f"sbst{si}"]
            sbdep = [Dep(f"sbst{q}") for q in range(nseg)]
            ui = [0]

            def load_u(q):
                u = uT[ui[0] % 2]
                ui[0] += 1
                k.dma("pool", u[:], hsd[:, :, q * SEG:(q + 1) * SEG], writes=[u], owner=u)
                return u

            smd = [Dep(f"sm{g}") for g in range(16)]
            sbd = [[Dep(f"sb{d2}_{g}") for g in range(16)] for d2 in range(2)]

            def scan(dirn, u, first):
                S = Sbf[dirn]
                sd = sbd[dirn]
                c_off = 1 if dirn == 0 else 0
                ccol = 0 if dirn == 0 else NJ
                for gp in range(8):
                    bk = self.nextbank()
                    for j in range(2):
                        g = gp * 2 + j
                        tile_, hh, e_ = g // 8, (g % 8) // 4, g % 4
                        for t2 in range(8):
                            k.op("pe", lambda e, bk=bk, j=j, tile_=tile_, hh=hh, e_=e_, t2=t2, u=u: e.matmul(
                                self.bank(bk)[:, j * NJ:(j + 1) * NJ], lhsT=self.EL[64 * hh:64 * hh + 64, tile_, e_, dirn, t2, :],
                                rhs=u[64 * hh:64 * hh + 64, tile_, t2:SEG:8], start=(t2 == 0), stop=(t2 == 7)),
                                reads=[self.EL, u], writes=[self.bdep[bk]])
                    g2 = slice(gp * 2, gp * 2 + 2)
                    pv = self.bank(bk)[:, 0:2 * NJ].rearrange("p (a b) -> p a b", a=2)
                    k.op("act", lambda e, pv=pv, g2=g2: e.activation(Sm[:, g2, c_off:c_off + NJ], pv, AF.Copy),
                         reads=[self.bdep[bk], Sm], writes=[smd[gp * 2], smd[gp * 2 + 1]])
                if first:
                    k.op("dve", lambda e: e.memset(Sm[:, :, ccol:ccol + 1], 0.0), reads=[Sm], writes=smd)
                else:
                    k.op("dve", lambda e: e.tensor_copy(Sm[:, :, ccol:ccol + 1], carry[:, dirn, :].unsqueeze(2)), reads=[carry, Sm], writes=smd)
                for g in range(16):
                    if g % 2 == 0:
                        k.op("act", lambda e, g=g: e.activation(S[:, g, :], Sm[:, g, :], AF.Copy), reads=[smd[g]], writes=[sd[g]])
                    else:
                        k.op("pool", lambda e, g=g: e.tensor_copy(S[:, g, :], Sm[:, g, :]), reads=[smd[g]], writes=[sd[g]])
                for mi, dd in enumerate(steps):
                    R = Rot[mi % 2]
                    if self.cfg.get("rot_dma", False):
                        k.dma("sp", R[:], self.dr["rotd"].ap()[dirn, mi], reads=[self.rotdep], writes=[R], owner=R)
                    else:
                        ar = self.SQ[:, 0, mi, dirn * 16:(dirn + 1) * 16]
                        ai = self.SQ[:, 1, mi, dirn * 16:(dirn + 1) * 16]
                        I2 = self.I2
                        for (r0, c0, src) in ((0, 0, ar), (0, 64, ai), (64, 0, ai), (64, 64, ar)):
                            k.op("dve", lambda e, r0=r0, c0=c0, src=src, R=R: e.tensor_tensor(
                                R[r0:r0 + 64, :, c0:c0 + 64], I2[r0:r0 + 64, :].unsqueeze(1).broadcast_to([64, 16, 64]),
                                src[r0:r0 + 64, :].unsqueeze(2).broadcast_to([64, 16, 64]), ALU.mult), reads=[I2, self.SQ], writes=[R])
                    n = NC1 - dd
                    if dirn == 0:
                        srcs, dsts = slice(0, n), slice(dd, NC1)
                    else:
                        srcs, dsts = slice(dd, NC1), slice(0, n)
                    per = 2 if n <= 256 else 1
                    for g0 in range(0, 16, per):
                        bk = self.nextbank()
                        for j in range(per):
                            g = g0 + j
                            k.op("pe", lambda e, bk=bk, j=j, g=g, R=R, n=n, srcs=srcs: e.matmul(
                                self.bank(bk)[:, j * 256:j * 256 + n], lhsT=R[:, g, :], rhs=S[:, g, srcs], start=True, stop=True),
                                reads=[R, sd[g]], writes=[self.bdep[bk]])
                        for j in range(per):
                            g = g0 + j
                            k.op("dve", lambda e, bk=bk, j=j, g=g, n=n, dsts=dsts: e.tensor_tensor(
                                Sm[:, g, dsts], self.bank(bk)[:, j * 256:j * 256 + n], Sm[:, g, dsts], ALU.add),
                                reads=[self.bdep[bk], smd[g]], writes=[smd[g]])
                            ceng = "act" if g % 2 == 0 else "pool"
                            if ceng == "act":
                                k.op("act", lambda e, g=g: e.activation(S[:, g, :], Sm[:, g, :], AF.Copy), reads=[smd[g]], writes=[sd[g]])
                            else:
                                k.op("pool", lambda e, g=g: e.tensor_copy(S[:, g, :], Sm[:, g, :]), reads=[smd[g]], writes=[sd[g]])
                ocol = NJ if dirn == 0 else 0
                k.op("dve", lambda e: e.tensor_copy(carry[:, dirn, :].unsqueeze(2), Sm[:, :, ocol:ocol + 1]), reads=smd, writes=[carry])

            for q in range(nseg - 1, -1, -1):
                u = load_u(q)
                scan(1, u, q == nseg - 1)
                k.dma("sp", sbst.ap()[q], Sbf[1][:], reads=sbd[1], writes=[sbdep[q]], owner=Sbf[1])
            for q in range(nseg):
                u = load_u(q)
                scan(0, u, q == 0)
                k.dma("sp", Sbf[1][:], sbst.ap()[q], reads=[sbdep[q]], writes=sbd[1] + [Sbf[1]], owner=Sbf[1])
                for h4 in range(4):
                    tile_, hh = h4 // 2, h4 % 2
                    pr = slice(64 * hh, 64 * hh + 64)
                    for tp in range(4):
                        bk = self.nextbank()
                        for j in range(2):
                            t = tp * 2 + j
                            oc = slice(j * NJ, (j + 1) * NJ)
                            for t2 in range(8):
                                k.op("pe", lambda e, bk=bk, oc=oc, t=t, t2=t2, tile_=tile_, pr=pr, u=u: e.matmul(
                                    self.bank(bk)[pr, oc], lhsT=self.KcT[pr, tile_, t - t2 + 7, :], rhs=u[pr, tile_, t2:SEG:8],
                                    start=(t2 == 0), stop=False), reads=[self.KcT, u], writes=[self.bdep[bk]])
                            for e_ in range(4):
                                g = h4 * 4 + e_
                                k.op("pe", lambda e, bk=bk, oc=oc, t=t, g=g, pr=pr: e.matmul(
                                    self.bank(bk)[pr, oc], lhsT=self.WL[:, g, t, :], rhs=Sbf[0][:, g, 0:NJ],
                                    start=False, stop=False), reads=[self.WL, sbd[0][g]], writes=[self.bdep[bk]])
                                k.op("pe", lambda e, bk=bk, oc=oc, t=t, g=g, pr=pr, e_=e_: e.matmul(
                                    self.bank(bk)[pr, oc], lhsT=self.WL[:, 16 + g, t, :], rhs=Sbf[1][:, g, 1:NC1],
                                    start=False, stop=(e_ == 3)), reads=[self.WL, sbd[1][g]], writes=[self.bdep[bk]])
                        k.op("act", lambda e, bk=bk, tp=tp, tile_=tile_, pr=pr: e.activation(
                            gF[pr, tile_, :].rearrange("p (j t) -> p t j", t=8)[:, tp * 2:tp * 2 + 2, :],
                            self.bank(bk)[pr, 0:2 * NJ].rearrange("p (a b) -> p a b", a=2), AF.Gelu_apprx_tanh),
                            reads=[self.bdep[bk]], writes=[gF])
                if l == 0 and si == 0 and q == 0:
                    self.dbg("Sf", Sbf[0], [128, 16, NCP], BF16, sbd[0]); self.dbg("Sb", Sbf[1], [128, 16, NCP], BF16, sbd[1])
                    self.dbg("gF", gF, [128, 2, SEG]); self.dbg("uT", u, [128, 2, SEG], BF16)
                k.op("pool", lambda e: e.tensor_copy(gB[:], gF[:]), reads=[gF], writes=[gB])
                GB = min(512, SEG)
                for b in range(SEG // GB):
                    ts = slice(b * GB, (b + 1) * GB)
                    sob = so[b % 2]
                    for cc in range(2):
                        bk = self.nextbank()
                        for kc in range(2):
                            k.op("pe", lambda e, bk=bk, cc=cc, kc=kc, ts=ts: e.matmul(
                                self.bank(bk)[:, 0:GB], lhsT=self.wglu[:, kc, cc * 128:(cc + 1) * 128], rhs=gB[:, kc, ts],
                                start=(kc == 0), stop=(kc == 1)), reads=[self.wglu, gB], writes=[self.bdep[bk]])
                        sgt = sg[cc]
                        k.op("act", lambda e, bk=bk, sgt=sgt: e.activation(sgt[:, 0:GB], self.bank(bk)[:, 0:GB], AF.Sigmoid), reads=[self.bdep[bk]], writes=[sgt])
                        k.op("dve", lambda e, cc=cc, ts=ts, sgt=sgt, sob=sob: e.tensor_tensor(sob[:, cc, 0:GB], gF[:, cc, ts], sgt[:, 0:GB], ALU.mult),
                             reads=[gF, sgt], writes=[sob])
                    t0 = q * SEG + b * GB
                    k.dma("sp", self.dr[f"s5T{si}"].ap().rearrange("(c p) t -> p c t", p=128)[:, :, t0:t0 + GB], sob[:, :, 0:GB], reads=[sob], owner=sob)
            k.flush()

    def fourier(self, l, si, st):
        k = self.k
        s = self.seqs[si]
        L, N1, N2 = s["L"], s["N1"], s["N2"]
        FA = k.sb("FA", [N1, 2 * N1], BF16, st)
        FC1 = k.sb("FC1", [N2, 2 * N2], BF16, st)
        FC2 = k.sb("FC2", [N2, 2 * N2], BF16, st)
        TW = k.sb("TW", [N2, 2, N1], F32, st)
        k.dma("pool", FA[:], self.dr[f"c{si}_FA"].ap(), writes=[FA], owner=FA)
        k.dma("pool", FC1[:], self.dr[f"c{si}_FC1"].ap(), writes=[FC1], owner=FC1)
        k.dma("pool", FC2[:], self.dr[f"c{si}_FC2"].ap(), writes=[FC2], owner=FC2)
        k.dma("sp", TW[:], self.dr[f"c{si}_TW"].ap(), writes=[TW], owner=TW)
        xg = [k.sb(f"xg{i}", [N1, N2, 64], BF16, st) for i in range(2)]
        AR = k.sb("AR", [N2, N1, 64], BF16, st)
        AI = k.sb("AI", [N2, N1, 64], BF16, st)
        XT = k.sb("XT", [64, 2, L], BF16, st)
        nbA = 512 // (2 * N1)
        tmp = [k.sb(f"ftmp{i}", [N2, nbA, N1], F32, st) for i in range(4)]
        nbC = 512 // (2 * N2)
        hfd = self.dr[f"hf{si}"].ap()
        for g in range(4):
            x = xg[g % 2]
            k.dma("pool", x[:], hfd[g].rearrange("(a b) c -> a b c", a=N1), writes=[x], owner=x)
            for c0 in range(0, 64, nbA):
                bk = self.nextbank()
                for j in range(nbA):
                    ch = c0 + j
                    k.op("pe", lambda e, bk=bk, j=j, ch=ch, x=x: e.matmul(
                        self.bank(bk)[0:N2, j * 2 * N1:(j + 1) * 2 * N1], lhsT=x[:, :, ch], rhs=FA[:], start=True, stop=True),
                        reads=[x, FA], writes=[self.bdep[bk]])
                pv = self.bank(bk)[0:N2, 0:nbA * 2 * N1].rearrange("p (c r k) -> p c r k", c=nbA, r=2)
                Ar, Ai = pv[:, :, 0, :], pv[:, :, 1, :]
                twr = TW[:, 0, :].unsqueeze(1).broadcast_to([N2, nbA, N1])
                twi = TW[:, 1, :].unsqueeze(1).broadcast_to([N2, nbA, N1])
                bd = self.bdep[bk]
                k.op("dve", lambda e, Ar=Ar, twr=twr: e.tensor_tensor(tmp[0][:], Ar, twr, ALU.mult), reads=[bd, TW], writes=[tmp[0]])
                k.op("dve", lambda e, Ai=Ai, twi=twi: e.tensor_tensor(tmp[1][:], Ai, twi, ALU.mult), reads=[bd, TW], writes=[tmp[1]])
                k.op("dve", lambda e, Ar=Ar, twi=twi: e.tensor_tensor(tmp[2][:], Ar, twi, ALU.mult), reads=[bd, TW], writes=[tmp[2]])
                k.op("dve", lambda e, Ai=Ai, twr=twr: e.tensor_tensor(tmp[3][:], Ai, twr, ALU.mult), reads=[bd, TW], writes=[tmp[3]])
                k.op("pool", lambda e, c0=c0: e.tensor_tensor(AR[:, :, c0:c0 + nbA].rearrange("p k c -> p c k"), tmp[0][:], tmp[1][:], ALU.subtract),
                     reads=[tmp[0], tmp[1]], writes=[AR])
                k.op("pool", lambda e, c0=c0: e.tensor_tensor(AI[:, :, c0:c0 + nbA].rearrange("p k c -> p c k"), tmp[2][:], tmp[3][:], ALU.add),
                     reads=[tmp[2], tmp[3]], writes=[AI])
            for k10 in range(0, N1, nbC):
                bk = self.nextbank()
                for j in range(nbC):
                    k1 = k10 + j
                    oc = slice(j * 2 * N2, (j + 1) * 2 * N2)
                    k.op("pe", lambda e, bk=bk, oc=oc, k1=k1: e.matmul(self.bank(bk)[0:64, oc], lhsT=AR[:, k1, :], rhs=FC1[:], start=True, stop=False),
                         reads=[AR, FC1], writes=[self.bdep[bk]])
                    k.op("pe", lambda e, bk=bk, oc=oc, k1=k1: e.matmul(self.bank(bk)[0:64, oc], lhsT=AI[:, k1, :], rhs=FC2[:], start=False, stop=True),
                         reads=[AI, FC2], writes=[self.bdep[bk]])
                pv = self.bank(bk)[0:64, 0:nbC * 2 * N2].rearrange("p (i r k) -> p r k i", i=nbC, r=2)
                for r in range(2):
                    eng = "act" if r == 0 else "dve"
                    outv = XT[:, r, :].rearrange("p (k2 k1) -> p k2 k1", k1=N1)[:, :, k10:k10 + nbC]
                    if eng == "act":
                        k.op("act", lambda e, outv=outv, pv=pv, r=r: e.activation(outv, pv[:, r, :, :], AF.Copy), reads=[self.bdep[bk]], writes=[XT])
                    else:
                        k.op("dve", lambda e, outv=outv, pv=pv, r=r: e.tensor_copy(outv, pv[:, r, :, :]), reads=[self.bdep[bk]], writes=[XT])
            for r in range(2):
                k.dma("sp", self.dr[f"fT{si}"].ap()[r, g * 64:(g + 1) * 64, :], XT[:, r, :], reads=[XT], owner=XT)

    def mla_pre(self, l, si, st):
        k = self.k
        L = self.seqs[si]["L"]
        wq = k.sb("wq", [128, 2, 768], BF16, st)
        wqs = k.sb("wqs", [128, 2, 768], BF16, st)
        wk = k.sb("wk", [128, 512], BF16, st)
        wv = k.sb("wv", [128, 512], BF16, st)
        vec = k.sb("vec", [128, 59], F32, st)
        k.dma("sp", wq[:], self.dr["b_w_uq"].ap()[l].rearrange("(kc p) n -> p kc n", p=128), writes=[wq], owner=wq)
        k.dma("sp", wqs[:], self.dr["b_w_uq_sw"].ap()[l].rearrange("(kc p) n -> p kc n", p=128), writes=[wqs], owner=wqs)
        k.dma("sp", wk[:], self.dr["b_w_uk"].ap()[l], writes=[wk], owner=wk)
        k.dma("sp", wv[:], self.dr["b_w_uv"].ap()[l], writes=[wv], owner=wv)
        k.dma("sp", vec[:], self.dr["vecs"].ap()[l], writes=[vec], owner=vec)
        cq = [k.sb(f"cq{i}", [128, 2, 512], F32, st) for i in range(2)]
        ckv = [k.sb(f"ckv{i}", [128, 512], F32, st) for i in range(2)]
        kpa = [k.sb(f"kpa{i}", [96, 512], F32, st) for i in range(2)]
        kpb = [k.sb(f"kpb{i}", [96, 512], F32, st) for i in range(2)]
        rope = [k.sb(f"rope{i}", [96, 2, 2, 512], F32, st) for i in range(2)]
        sq = k.sb("sq", [128, 3, 512], F32, st)
        rs = k.sb("rs", [128, 2, 512], F32, st)
        cqn = k.sb("cqn", [128, 2, 512], BF16, st)
        ckvn = k.sb("ckvn", [128, 512], BF16, st)
        rt = k.sb("rt", [96, 3, 512], F32, st)
        Qst = [k.sb(f"Qst{i}", [96, H, 512], BF16, st) for i in range(2)]
        Kst = [k.sb(f"Kst{i}", [96, H, 512], BF16, st) for i in range(2)]
        Vst = [k.sb(f"Vst{i}", [128, 4, 512], BF16, st) for i in range(2)]
        cqd = self.dr[f"cqT{si}"].ap().rearrange("(c p) t -> p c t", p=128)
        rp = self.dr[f"c{si}_ROPE"].ap()
        for b in range(L // 512):
            i = b % 2
            ts = slice(b * 512, (b + 1) * 512)
            k.dma("sp", cq[i][:], cqd[:, :, ts], writes=[cq[i]], owner=cq[i])
            k.dma("sp", ckv[i][:], self.dr[f"ckvT{si}"].ap()[:, ts], writes=[ckv[i]], owner=ckv[i])
            k.dma("sp", kpa[i][64:96, :], self.dr[f"kpeT{si}"].ap()[0:32, ts], writes=[kpa[i]], owner=kpa[i])
            k.dma("sp", kpb[i][64:96, :], self.dr[f"kpeT{si}"].ap()[32:64, ts], writes=[kpb[i]], owner=kpb[i])
            k.dma("sp", rope[i][64:96], rp[:, :, :, ts].rearrange("a b r t -> r a b t"), writes=[rope[i]], owner=rope[i])
            k.op("act", lambda e, i=i: e.activation(sq[:, 0:2, :], cq[i][:], AF.Square), reads=[cq[i]], writes=[sq])
            k.op("act", lambda e, i=i: e.activation(sq[:, 2, :], ckv[i][:], AF.Square), reads=[ckv[i]], writes=[sq])
            bq = self.nextbank()
            for kc in range(2):
                k.op("pe", lambda e, bq=bq, kc=kc: e.matmul(self.bank(bq), lhsT=self.ones[:], rhs=sq[:, kc, :], start=(kc == 0), stop=(kc == 1)),
                     reads=[self.ones, sq], writes=[self.bdep[bq]])
            bkv = self.nextbank()
            k.op("pe", lambda e, bkv=bkv: e.matmul(self.bank(bkv), lhsT=self.ones[:], rhs=sq[:, 2, :], start=True, stop=True),
                 reads=[self.ones, sq], writes=[self.bdep[bkv]])
            k.op("act", lambda e, bq=bq: e.activation(rs[:, 0, :], self.bank(bq), AF.Sqrt, scale=1.0 / 256, bias=EPS), reads=[self.bdep[bq]], writes=[rs])
            k.op("act", lambda e, bkv=bkv: e.activation(rs[:, 1, :], self.bank(bkv), AF.Sqrt, scale=1.0 / 128, bias=EPS), reads=[self.bdep[bkv]], writes=[rs])
            k.op("dve", lambda e: e.reciprocal(rs[:], rs[:]), reads=[rs], writes=[rs])
            for kc in range(2):
                k.op("dve", lambda e, kc=kc, i=i: e.scalar_tensor_tensor(cqn[:, kc, :], cq[i][:, kc, :], vec[:, 56 + kc:57 + kc], rs[:, 0, :], ALU.mult, ALU.mult),
                     reads=[cq[i], vec, rs], writes=[cqn])
            k.op("dve", lambda e, i=i: e.scalar_tensor_tensor(ckvn[:], ckv[i][:], vec[:, 58:59], rs[:, 1, :], ALU.mult, ALU.mult),
                 reads=[ckv[i], vec, rs], writes=[ckvn])
            R = rope[i]
            k.op("pool", lambda e, i=i, R=R: e.tensor_tensor(rt[64:96, 0, :], kpa[i][64:96, :], R[64:96, 1, 0, :], ALU.mult), reads=[kpa[i], R], writes=[rt])
            k.op("pool", lambda e, i=i, R=R: e.tensor_tensor(rt[64:96, 1, :], kpb[i][64:96, :], R[64:96, 1, 1, :], ALU.mult), reads=[kpb[i], R], writes=[rt])
            k.op("pool", lambda e: e.tensor_tensor(rt[64:96, 2, :], rt[64:96, 0, :], rt[64:96, 1, :], ALU.add), reads=[rt], writes=[rt])
            Q, K_, V = Qst[i], Kst[i], Vst[i]
            k.op("pool", lambda e, K_=K_: e.tensor_copy(K_[64:96, :, :], rt[64:96, 2, :].unsqueeze(1).broadcast_to([32, H, 512])), reads=[rt], writes=[K_])
            for h in range(H):
                ba = self.nextbank()
                bb = self.nextbank()
                for kc in range(2):
                    k.op("pe", lambda e, ba=ba, kc=kc, h=h: e.matmul(self.bank(ba)[0:96, :], lhsT=wq[:, kc, h * 96:(h + 1) * 96], rhs=cqn[:, kc, :],
                                                                    start=(kc == 0), stop=(kc == 1)), reads=[wq, cqn], writes=[self.bdep[ba]])
                for kc in range(2):
                    k.op("pe", lambda e, bb=bb, kc=kc, h=h: e.matmul(self.bank(bb)[0:96, :], lhsT=wqs[:, kc, h * 96:(h + 1) * 96], rhs=cqn[:, kc, :],
                                                                    start=(kc == 0), stop=(kc == 1)), reads=[wqs, cqn], writes=[self.bdep[bb]])
                k.op("act", lambda e, ba=ba, h=h, Q=Q: e.activation(Q[0:64, h, :], self.bank(ba)[0:64, :], AF.Copy, scale=96 ** -0.5),
                     reads=[self.bdep[ba]], writes=[Q])
                k.op("dve", lambda e, ba=ba, R=R: e.tensor_tensor(rt[64:96, 0, :], self.bank(ba)[64:96, :], R[64:96, 0, 0, :], ALU.mult),
                     reads=[self.bdep[ba], R], writes=[rt])
                k.op("dve", lambda e, bb=bb, R=R: e.tensor_tensor(rt[64:96, 1, :], self.bank(bb)[64:96, :], R[64:96, 0, 1, :], ALU.mult),
                     reads=[self.bdep[bb], R], writes=[rt])
                k.op("dve", lambda e, h=h, Q=Q: e.tensor_tensor(Q[64:96, h, :], rt[64:96, 0, :], rt[64:96, 1, :], ALU.add), reads=[rt], writes=[Q])
                bk = self.nextbank()
                k.op("pe", lambda e, bk=bk, h=h: e.matmul(self.bank(bk)[0:64, :], lhsT=wk[:, h * 64:(h + 1) * 64], rhs=ckvn[:], start=True, stop=True),
                     reads=[wk, ckvn], writes=[self.bdep[bk]])
                k.op("act", lambda e, bk=bk, h=h, K_=K_: e.activation(K_[0:64, h, :], self.bank(bk)[0:64, :], AF.Copy), reads=[self.bdep[bk]], writes=[K_])
            for sub in range(4):
                bk = self.nextbank()
                k.op("pe", lambda e, bk=bk, sub=sub: e.matmul(self.bank(bk), lhsT=ckvn[:, sub * 128:(sub + 1) * 128], rhs=wv[:], start=True, stop=True),
                     reads=[wv, ckvn], writes=[self.bdep[bk]])
                k.op("act", lambda e, bk=bk, sub=sub, V=V: e.activation(V[:, sub, :], self.bank(bk), AF.Copy), reads=[self.bdep[bk]], writes=[V])
            k.dma("sp", self.dr[f"qT{si}"].ap()[:, :, ts].rearrange("h r t -> r h t"), Q[:], reads=[Q], owner=Q)
            k.dma("sp", self.dr[f"kT{si}"].ap()[:, :, ts].rearrange("h r t -> r h t"), K_[:], reads=[K_], owner=K_)
            k.dma("sp", self.dr[f"v{si}"].ap()[ts, :].rearrange("(s p) c -> p s c", p=128), V[:], reads=[V], owner=V)

    def attn(self, l, si, st):
        k = self.k
        L = self.seqs[si]["L"]
        split = self.seqs[si].get("split", False)
        Lq = L // 4 if split else L
        QB = min(1024, Lq)
        NKT = L // 128
        NH = QB // 512
        KT = [k.sb(f"KT{i}", [96, L], BF16, st) for i in range(2)]
        VT = [k.sb(f"VT{i}", [128, NKT, 128], BF16, st) for i in range(2)]
        Qb = [k.sb(f"Qb{i}", [96, QB], BF16, st) for i in range(2)]
        Pt = [k.sb(f"Pt{i}", [128, QB], BF16, st) for i in range(3)]
        rden = k.sb("rden", [128, QB], F32, st)
        ost = [k.sb(f"ost{i}", [64, QB], BF16, st) for i in range(2)]
        for i in range(2):
            k.op("pool", lambda e, i=i: e.memset(VT[i][:, :, 64:128], 1.0), writes=[VT[i]])
        sdep = [Dep("S0"), Dep("S1")]
        odep = [Dep("O0"), Dep("O1")]
        qd = self.dr[f"qT{si}"].ap()
        kd = self.dr[f"kT{si}"].ap()
        vd = self.dr[f"v{si}"].ap()
        ad = self.dr[f"attnq{si}"].ap() if split else self.dr[f"attnT{si}"].ap()
        cnt = 0
        pcnt = 0
        if split:
            qmd = Dep("qm")
            k.dma("sp", self.dr[f"qm{si}"].ap(), lambda e, qsrc=qd: qsrc[:, :, bass.ds(k.rbase(e, "sp", Lq), Lq)], writes=[qmd], owner=qmd)
            qd = self.dr[f"qm{si}"].ap()
        for h in range(H):
            Kt, Vt = KT[h % 2], VT[h % 2]
            k.dma("sp", Kt[:], kd[h], writes=[Kt], owner=Kt)
            vsrc = vd[:, h * 64:(h + 1) * 64].rearrange("(t p) c -> p t c", p=128)
            nsp = max(1, NKT // 32)
            for sp_ in range(nsp):
                tsl = slice(sp_ * NKT // nsp, (sp_ + 1) * NKT // nsp)
                k.dma("pool", Vt[:, tsl, 0:64], vsrc[:, tsl, :], writes=[Vt], owner=Vt)
            for qb in range(Lq // QB):
                Q = Qb[cnt % 2]
                O = self.pst[2 + cnt % 2]
                od = odep[cnt % 2]
                osb = ost[cnt % 2]
                cnt += 1
                k.dma("sp", Q[:], qd[h, :, qb * QB:(qb + 1) * QB], reads=([qmd] if split else []), writes=[Q], owner=Q)

                def smm(kt, Q=Q, Kt=Kt):
                    S = self.pst[kt % 2]
                    for hf in range(NH):
                        k.op("pe", lambda e, S=S, hf=hf, kt=kt: e.matmul(S[:, hf * 512:(hf + 1) * 512], lhsT=Kt[:, kt * 128:(kt + 1) * 128],
                                                                          rhs=Q[:, hf * 512:(hf + 1) * 512], start=True, stop=True),
                             reads=[Kt, Q], writes=[sdep[kt % 2]])
                smm(0)
                for kt in range(NKT):
                    if kt + 1 < NKT:
                        smm(kt + 1)
                    S = self.pst[kt % 2]
                    P = Pt[pcnt % 3]
                    pcnt += 1
                    k.op("act", lambda e, S=S, P=P: e.activation(P[:], S[:, 0:QB], AF.Exp), reads=[sdep[kt % 2]], writes=[P])
                    for hf in range(NH):
                        k.op("pe", lambda e, O=O, hf=hf, kt=kt, P=P, Vt=Vt: e.matmul(O[:, hf * 512:(hf + 1) * 512], lhsT=Vt[:, kt, :],
                                                                                    rhs=P[:, hf * 512:(hf + 1) * 512], start=(kt == 0), stop=(kt == NKT - 1)),
                             reads=[Vt, P], writes=[od])
                k.op("dve", lambda e, O=O: e.reciprocal(rden[64:128, :], O[64:128, 0:QB]), reads=[od], writes=[rden])
                k.op("dve", lambda e, O=O, osb=osb: e.tensor_tensor(osb[:], O[0:64, 0:QB], rden[64:128, :], ALU.mult), reads=[od, rden], writes=[osb])
                k.dma("sp", ad[h * 64:(h + 1) * 64, qb * QB:(qb + 1) * QB], osb[:], reads=[osb], owner=osb)

    def p3_setup(self, l, st):
        k = self.k
        self.vec3 = k.sb("vec3", [128, 59], F32, st)
        k.dma("sp", self.vec3[:], self.dr["vecs"].ap()[l], writes=[self.vec3], owner=self.vec3)
        self.cs64 = k.sb("cs64", [128, 2, 128], BF16, st)
        k.dma("pool", self.cs64[:], self.dr["c_CS64"].ap(), writes=[self.cs64], owner=self.cs64)
        self.wring = [k.sb(f"wr{i}", [128, 4096], BF16, st) for i in range(4)]
        self.wri = 0
        self.p3x = [k.sb(f"p3x{i}", [128, 8, 512], F32, st) for i in range(2)]
        self.p3in = [k.sb(f"p3in{i}", [128, 12, 512], BF16, st) for i in range(2)]
        self.gs = [k.sb(f"gs{i}", [128, 512], F32, st) for i in range(6)]
        self.mt = [k.sb(f"mt{i}", [128, 512], F32, st) for i in range(3)]
        self.mg = k.sb("mg", [128, 8, 512], BF16, st)
        self.z = k.sb("z", [128, 8, 512], F32, st)
        self.x1 = k.sb("x1", [128, 8, 512], F32, st)
        self.x1b = k.sb("x1b", [128, 8, 512], BF16, st)
        self.xbf = k.sb("xbf", [128, 8, 512], BF16, st)
        self.hb = k.sb("hb", [128, NFF, 512], BF16, st)
        self.lsq = [k.sb(f"lsq{i}", [128, 512], F32, st) for i in range(2)]
        self.lst = k.sb("lst", [128, 4, 512], F32, st)
        self.p3i = 0

    def wload(self, src_ap, shape_str, **kw):
        k = self.k
        w = self.wring[self.wri % 4]
        self.wri += 1
        n = 1
        for d_ in src_ap.shape[1:]:
            n *= d_
        view = w[:, 0:n].rearrange(shape_str, **kw) if shape_str else w[:, 0:n]
        k.dma("sp", view, src_ap, writes=[w], owner=w)
        return w, view

    def layernorm(self, zin, gcol, bcol, out32, out16):
        k = self.k
        vec = self.vec3
        b1 = self.nextbank()
        for kc in range(8):
            k.op("pe", lambda e, kc=kc: e.matmul(self.bank(b1), lhsT=self.ones[:], rhs=zin[:, kc, :], start=(kc == 0), stop=(kc == 7)),
                 reads=[self.ones, zin], writes=[self.bdep[b1]])
        b2 = self.nextbank()
        for kc in range(8):
            sq = self.lsq[kc % 2]
            k.op("act", lambda e, kc=kc, sq=sq: e.activation(sq[:], zin[:, kc, :], AF.Square), reads=[zin], writes=[sq])
            k.op("pe", lambda e, kc=kc, sq=sq: e.matmul(self.bank(b2), lhsT=self.ones[:], rhs=sq[:], start=(kc == 0), stop=(kc == 7)),
                 reads=[self.ones, sq], writes=[self.bdep[b2]])
        lst = self.lst
        mean, msq, var, rstd = lst[:, 0, :], lst[:, 1, :], lst[:, 2, :], lst[:, 3, :]
        k.op("dve", lambda e: e.tensor_scalar(mean, self.bank(b1), 1.0 / D, None, ALU.mult), reads=[self.bdep[b1]], writes=[lst])
        k.op("dve", lambda e: e.tensor_tensor(msq, mean, mean, ALU.mult), reads=[lst], writes=[lst])
        k.op("dve", lambda e: e.scalar_tensor_tensor(var, self.bank(b2), 1.0 / D, msq, ALU.mult, ALU.subtract), reads=[self.bdep[b2], lst], writes=[lst])
        k.op("act", lambda e: e.activation(rstd, var, AF.Sqrt, bias=EPS), reads=[lst], writes=[lst])
        k.op("dve", lambda e: e.reciprocal(rstd, rstd), reads=[lst], writes=[lst])
        for kc in range(8):
            o32 = out32[:, kc, :]
            k.op("dve", lambda e, kc=kc, o32=o32: e.tensor_tensor(o32, zin[:, kc, :], mean, ALU.subtract), reads=[zin, lst], writes=[out32])
            k.op("dve", lambda e, kc=kc, o32=o32: e.tensor_tensor(o32, o32, rstd, ALU.mult), reads=[out32, lst], writes=[out32])
            k.op("dve", lambda e, kc=kc, o32=o32: e.tensor_scalar(o32, o32, vec[:, gcol + kc:gcol + kc + 1], vec[:, bcol + kc:bcol + kc + 1], ALU.mult, ALU.add),
                 reads=[out32, vec], writes=[out32])
            if out16 is not None:
                k.op("pool", lambda e, kc=kc, o32=o32: e.tensor_copy(out16[:, kc, :], o32), reads=[out32], writes=[out16])

    def p3(self, l, si, st):
        k = self.k
        L = self.seqs[si]["L"]
        if not self.seqs[si].get("split"):
            xs = self.xsrc(l, si).ap().rearrange("(kc p) t -> p kc t", p=128)
            xd = self.xdst(l, si).ap().rearrange("(kc p) t -> p kc t", p=128)
        fT = self.dr[f"fT{si}"].ap()
        vec = self.vec3
        wb = lambda n: self.dr["b_" + n].ap()[l]
        fscale = 1.0 / math.sqrt(L * 64.0)
        split = self.seqs[si].get("split", False)
        Lq = L // 4 if split else L
        if split:
            last = (l == self.depth - 1)
            xs = self.dr[f"xT{si}"].ap().rearrange("(kc p) t -> p kc t", p=128)
            xqv = self.dr[f"xq{si}"].ap().rearrange("c (kc p) t -> c p kc t", p=128)
            xd = self.dr[f"yq{si}"].ap().rearrange("(kc p) t -> p kc t", p=128) if last else None
        s5d = self.dr[f"s5T{si}"].ap().rearrange("(c p) t -> p c t", p=128)
        for b in range(Lq // 512):
            i = self.p3i % 2
            self.p3i += 1
            ts = slice(b * 512, (b + 1) * 512)
            x, pin = self.p3x[i], self.p3in[i]
            xbf = self.xbf
            if split:
                if b == 0:
                    md = Dep("mine")
                    dynq = lambda e: bass.ds(k.rbase(e, "sp", Lq), Lq)
                    k.dma("sp", self.dr[f"fm{si}"].ap(), lambda e: fT[:, :, dynq(e)], writes=[md], owner=md)
                    k.dma("sp", self.dr[f"s5m{si}"].ap(), lambda e: self.dr[f"s5T{si}"].ap()[:, dynq(e)], writes=[md], owner=md)
                    if l == 0:
                        k.dma("sp", self.dr[f"xm{si}"].ap(), lambda e: self.dr[f"xT{si}"].ap()[:, dynq(e)], writes=[md], owner=md)
                    self._md = md
                md = self._md
                if l == 0:
                    k.dma("sp", x[:], self.dr[f"xm{si}"].ap().rearrange("(kc p) t -> p kc t", p=128)[:, :, ts], reads=[md], writes=[x], owner=x)
                else:
                    for h_ in range(2):
                        k.dma("sp", x[:, :, h_ * 256:(h_ + 1) * 256], xqv[2 * b + h_], writes=[x], owner=x)
                k.dma("sp", pin[:, 0:4, :], self.dr[f"fm{si}"].ap()[:, :, ts].rearrange("r (c p) t -> p (r c) t", p=128), reads=[md], writes=[pin], owner=pin)
                k.dma("sp", pin[:, 4:6, :], self.dr[f"s5m{si}"].ap().rearrange("(c p) t -> p c t", p=128)[:, :, ts], reads=[md], writes=[pin], owner=pin)
                k.dma("sp", pin[:, 6:10, :], self.dr[f"attnq{si}"].ap().rearrange("(c p) t -> p c t", p=128)[:, :, ts], writes=[pin], owner=pin)
            else:
                k.dma("pool", x[:], xs[:, :, ts], writes=[x], owner=x)
                k.dma("pool", pin[:, 0:4, :], fT[:, :, ts].rearrange("r (c p) t -> p (r c) t", p=128), writes=[pin], owner=pin)
                k.dma("pool", pin[:, 4:6, :], s5d[:, :, ts], writes=[pin], owner=pin)
                k.dma("pool", pin[:, 6:10, :], self.dr[f"attnT{si}"].ap().rearrange("(c p) t -> p c t", p=128)[:, :, ts], writes=[pin], owner=pin)
            k.op("act", lambda e, x=x: e.activation(xbf[:, 0:4, :], x[:, 0:4, :], AF.Copy), reads=[x], writes=[xbf])
            k.op("pool", lambda e, x=x: e.tensor_copy(xbf[:, 4:8, :], x[:, 4:8, :]), reads=[x], writes=[xbf])
            for cc in range(2):
                bk = self.nextbank()
                k.op("pe", lambda e, bk=bk, cc=cc, pin=pin: e.matmul(self.bank(bk), lhsT=self.cs64[:, 0, :], rhs=pin[:, cc, :], start=True, stop=False),
                     reads=[self.cs64, pin], writes=[self.bdep[bk]])
                k.op("pe", lambda e, bk=bk, cc=cc, pin=pin: e.matmul(self.bank(bk), lhsT=self.cs64[:, 1, :], rhs=pin[:, 2 + cc, :], start=False, stop=True),
                     reads=[self.cs64, pin], writes=[self.bdep[bk]])
                k.op("act", lambda e, bk=bk, cc=cc, pin=pin: e.activation(pin[:, 10 + cc, :], self.bank(bk), AF.Copy, scale=fscale),
                     reads=[self.bdep[bk]], writes=[pin])
            brk = [(0, (10, 11)), (2, (4, 5)), (4, (6, 7, 8, 9))]
            for G in range(2):
                gw = []
                for br in range(3):
                    c0 = 928 + br * 1024 + G * 512
                    gw.append(self.wload(wb("w_in").rearrange("(kc p) n -> p kc n", p=128)[:, :, c0:c0 + 512], "p (kc n) -> p kc n", kc=8))
                bw = self.wring[self.wri % 4]
                self.wri += 1
                bwv = bw[:, 0:4096].rearrange("p (kc n) -> p kc n", kc=8)
                cs_ = slice(G * 512, (G + 1) * 512)
                k.dma("sp", bwv[:, 0:2, :], wb("w_br_f").rearrange("(kc p) n -> p kc n", p=128)[:, :, cs_], writes=[bw], owner=bw)
                k.dma("sp", bwv[:, 2:4, :], wb("w_br_s").rearrange("(kc p) n -> p kc n", p=128)[:, :, cs_], writes=[bw], owner=bw)
                k.dma("sp", bwv[:, 4:8, :], wb("w_br_a").rearrange("(kc p) n -> p kc n", p=128)[:, :, cs_], writes=[bw], owner=bw)
                for c4 in range(4):
                    cc = G * 4 + c4
                    ws = slice(c4 * 128, (c4 + 1) * 128)
                    gsl = []
                    for br in range(3):
                        bk = self.nextbank()
                        wt, wv_ = gw[br]
                        for kc in range(8):
                            k.op("pe", lambda e, bk=bk, kc=kc, wv_=wv_, ws=ws: e.matmul(self.bank(bk), lhsT=wv_[:, kc, ws], rhs=xbf[:, kc, :],
                                                                                      start=(kc == 0), stop=(kc == 7)),
                                 reads=[wt, xbf], writes=[self.bdep[bk]])
                        g = self.gs[(cc * 3 + br) % 6]
                        col = 32 + br * 8 + cc
                        k.op("act", lambda e, bk=bk, g=g, col=col: e.activation(g[:], self.bank(bk), AF.Sigmoid, bias=vec[:, col:col + 1]),
                             reads=[self.bdep[bk], vec], writes=[g])
                        gsl.append(g)
                    for br in range(3):
                        bk = self.nextbank()
                        k0, srcs = brk[br]
                        n = len(srcs)
                        for j, sidx in enumerate(srcs):
                            k.op("pe", lambda e, bk=bk, j=j, sidx=sidx, k0=k0, n=n, ws=ws, bwv=bwv, pin=pin: e.matmul(
                                self.bank(bk), lhsT=bwv[:, k0 + j, ws], rhs=pin[:, sidx, :], start=(j == 0), stop=(j == n - 1)),
                                reads=[bw, pin], writes=[self.bdep[bk]])
                        m = self.mt[br]
                        g = gsl[br]
                        k.op("dve", lambda e, bk=bk, m=m, g=g: e.tensor_tensor(m[:], self.bank(bk), g[:], ALU.mult), reads=[self.bdep[bk], g], writes=[m])
                    k.op("pool", lambda e: e.tensor_tensor(self.mt[0][:], self.mt[0][:], self.mt[1][:], ALU.add), reads=[self.mt[0], self.mt[1]], writes=[self.mt[0]])
                    k.op("pool", lambda e, cc=cc: e.tensor_tensor(self.mg[:, cc, :], self.mt[0][:], self.mt[2][:], ALU.add), reads=[self.mt[0], self.mt[2]], writes=[self.mg])
            if split and not last:
                self._p3_rest(l, si, b, ts, x, [xqv[2 * b], xqv[2 * b + 1]])
            else:
                self._p3_rest(l, si, b, ts, x, xd)

    def _p3_rest(self, l, si, b, ts, x, xd):
        k = self.k
        vec = self.vec3
        wb = lambda n: self.dr["b_" + n].ap()[l]
        for G in range(2):
            wt, wv_ = self.wload(wb("w_o").rearrange("(kc p) n -> p kc n", p=128)[:, :, G * 512:(G + 1) * 512], "p (kc n) -> p kc n", kc=8)
            for c4 in range(4):
                cc = G * 4 + c4
                bk = self.nextbank()
                for kc in range(8):
                    k.op("pe", lambda e, bk=bk, kc=kc, wv_=wv_, c4=c4: e.matmul(self.bank(bk), lhsT=wv_[:, kc, c4 * 128:(c4 + 1) * 128], rhs=self.mg[:, kc, :],
                                                                              start=(kc == 0), stop=(kc == 7)), reads=[wt, self.mg], writes=[self.bdep[bk]])
                k.op("dve", lambda e, bk=bk, cc=cc: e.scalar_tensor_tensor(self.z[:, cc, :], x[:, cc, :], ALPHA, self.bank(bk), ALU.mult, ALU.add),
                     reads=[x, self.bdep[bk]], writes=[self.z])
        self.layernorm(self.z, 0, 8, self.x1, self.x1b)
        wgu = wb("w_gu").rearrange("(kc p) (two f) -> p kc two f", p=128, two=2)
        for f0 in range(0, NFF, 2):
            wt = self.wring[self.wri % 4]
            self.wri += 1
            wv_ = wt[:, 0:4096].rearrange("p (kc two f) -> p kc two f", kc=8, two=2)
            for two in range(2):
                k.dma("sp", wv_[:, :, two, :], wgu[:, :, two, f0 * 128:(f0 + 2) * 128], writes=[wt], owner=wt)
            for j in range(2):
                f = f0 + j
                bg = self.nextbank()
                bu = self.nextbank()
                for two, bk in ((0, bg), (1, bu)):
                    for kc in range(8):
                        k.op("pe", lambda e, bk=bk, kc=kc, wv_=wv_, two=two, j=j: e.matmul(self.bank(bk), lhsT=wv_[:, kc, two, j * 128:(j + 1) * 128],
                                                                                         rhs=self.x1b[:, kc, :], start=(kc == 0), stop=(kc == 7)),
                             reads=[wt, self.x1b], writes=[self.bdep[bk]])
                sgt = self.gs[f % 6]
                k.op("act", lambda e, bg=bg, sgt=sgt: e.activation(sgt[:], self.bank(bg), AF.Silu), reads=[self.bdep[bg]], writes=[sgt])
                k.op("dve", lambda e, bu=bu, sgt=sgt, f=f: e.tensor_tensor(self.hb[:, f, :], self.bank(bu), sgt[:], ALU.mult),
                     reads=[self.bdep[bu], sgt], writes=[self.hb])
        wdn = wb("w_down").rearrange("(kc p) n -> p kc n", p=128)
        for cc in range(8):
            wt, wv_ = self.wload(wdn[:, :, cc * 128:(cc + 1) * 128], "p (kc n) -> p kc n", kc=NFF)
            bk = self.nextbank()
            for kc in range(NFF):
                k.op("pe", lambda e, bk=bk, kc=kc, wv_=wv_: e.matmul(self.bank(bk), lhsT=wv_[:, kc, :], rhs=self.hb[:, kc, :],
                                                                    start=(kc == 0), stop=(kc == NFF - 1)), reads=[wt, self.hb], writes=[self.bdep[bk]])
            k.op("dve", lambda e, bk=bk, cc=cc: e.scalar_tensor_tensor(self.z[:, cc, :], self.x1[:, cc, :], ALPHA, self.bank(bk), ALU.mult, ALU.add),
                 reads=[self.x1, self.bdep[bk]], writes=[self.z])
        self.layernorm(self.z, 16, 24, self.x1, None)
        if isinstance(xd, list):
            for h_ in range(2):
                k.dma("pool", xd[h_], self.x1[:, :, h_ * 256:(h_ + 1) * 256], reads=[self.x1], owner=self.x1)
        else:
            k.dma("pool", xd[:, :, ts], self.x1[:], reads=[self.x1], owner=self.x1)


FULL_CFG = dict(depth=4, SEG=2048, seqs=[dict(L=4096, N1=64, N2=64), dict(L=16384, N1=128, N2=128, split=True)])


def build_program(cfg):
    p = Prog(cfg)
    nc = p.k.finish()
    return p, nc


def make_in_maps(cfg, inputs, n_cores, x_per_core):
    depth = cfg["depth"]
    par = _stage_params(inputs, depth)
    base = dict(par)
    for n, v in _global_consts().items():
        base["c_" + n] = v
    for si, s in enumerate(cfg["seqs"]):
        for n, v in _consts_for_seq(s["L"], s["N1"], s["N2"]).items():
            base[f"c{si}_{n}"] = v
    maps = []
    for c in range(n_cores):
        m = dict(base)
        for si in range(len(cfg["seqs"])):
            m[f"xT{si}"] = np.ascontiguousarray(np.asarray(x_per_core[c][si], dtype=np.float32).T)
        maps.append(m)
    return maps


def kernel(**inputs):
    cfg = FULL_CFG
    xp = np.asarray(inputs["x_prompt"], dtype=np.float32)
    xs = np.asarray(inputs["x_sample"], dtype=np.float32)
    n = 8
    x_per_core = [[xp[c], xs[c // 4]] for c in range(n)]
    _, nc = build_program(cfg)
    maps = make_in_maps(cfg, inputs, n, x_per_core)
    res = run_bass_kernel_spmd(nc, maps, core_ids=list(range(n)))
    y_prompt = np.stack([np.ascontiguousarray(res.results[c]["yT0"].T) for c in range(n)], axis=0).astype(np.float32)
    y_sample = np.stack([np.concatenate([res.results[4 * s + r]["yq1"].T for r in range(4)], axis=0) for s in range(2)], axis=0).astype(np.float32)
    return (y_prompt, y_sample)
```

```python
import contextlib
import math
import numpy as np
import concourse.bass as bass
import concourse.mybir as mybir
from concourse.bass_utils import run_bass_kernel_spmd

F32 = mybir.dt.float32
BF16 = mybir.dt.bfloat16
AF = mybir.ActivationFunctionType
ALU = mybir.AluOpType

D = 1024
H = 8
DFF = 2816
NFF = DFF // 128
ALPHA = (2 * 4) ** 0.25
EPS = 1e-5
PI = math.pi


class Dep:
    __slots__ = ("name", "w", "r", "sems", "wdma")

    def __init__(self, name=""):
        self.name = name
        self.w = None
        self.r = {}
        self.sems = {}
        self.wdma = None


class Tile:
    def __init__(self, t, name):
        self.t = t
        self.dep = Dep(name)
        self.name = name

    def __getitem__(self, key):
        return self.t[key]


class KB:
    ENG = ("pe", "act", "dve", "pool", "sp")

    def __init__(self):
        self.nc = bass.Bass("TRN2", target_bir_lowering=False)
        self.es = contextlib.ExitStack()
        self.streams = {e: [] for e in self.ENG}
        self.cnt = {e: 0 for e in self.ENG}
        self.seen = {e: {} for e in self.ENG}
        self.esem = {}
        for e in self.ENG:
            self.esem[e] = self.es.enter_context(self.nc.semaphore("s_" + e))
        self.nsem = 0
        self.dma_sems = []
        self.free_recs = {"sw": [], "hw": []}
        self.active = []
        self.semowner = {}
        self.uid = 0
        self.ninst = 0
        self.ccsem = None
        self.cccnt = 0
        self.blockno = 0
        self.rcache = {}

    def sb(self, name, shape, dtype, stack=None):
        self.uid += 1
        st = stack if stack is not None else self.es
        t = st.enter_context(self.nc.sbuf_tensor(f"{name}_{self.uid}", list(shape), dtype))
        return Tile(t, name)

    def ps(self, name, shape, dtype=F32, stack=None):
        self.uid += 1
        st = stack if stack is not None else self.es
        t = st.enter_context(self.nc.psum_tensor(f"{name}_{self.uid}", list(shape), dtype))
        return Tile(t, name)

    def dram(self, name, shape, dtype, kind="Internal"):
        return self.nc.dram_tensor(name, list(shape), dtype, kind=kind)

    def _dsem(self, dep, q):
        if q not in dep.sems:
            if self.free_recs[q]:
                rec = self.free_recs[q].pop()
                rec[2] = False
            else:
                self.nsem += 1
                sem = self.es.enter_context(self.nc.semaphore(f"d{self.nsem}"))
                rec = [sem, 0, False]
                self.dma_sems.append(rec)
                self.semowner[id(sem)] = rec
            dep.sems[q] = rec
            self.active.append((dep, q))
        return dep.sems[q]

    def _need(self, eng, waits, ev):
        if ev is None:
            return
        sem, val = ev
        key = id(sem)
        ow = self.semowner.get(key)
        if ow is not None:
            ow[2] = True
        if self.seen[eng].get(key, 0) >= val:
            return
        if sem is self.esem.get(eng) and eng in ("pe", "sp"):
            return
        self.seen[eng][key] = val
        waits[key] = (sem, max(val, waits.get(key, (sem, 0))[1]))

    def _deps(self, eng, reads, writes, same_gen=None):
        waits = {}
        for d in reads:
            self._need(eng, waits, d.w)
        for d in writes:
            if not (same_gen is not None and d.wdma is same_gen):
                self._need(eng, waits, d.w)
            for key, (sem, val) in d.r.items():
                self._need(eng, waits, (sem, val))
        return list(waits.values())

    def _commit(self, ev, reads, writes):
        sem, val = ev
        k = id(sem)
        for d in reads:
            if d.r.get(k, (sem, 0))[1] < val:
                d.r[k] = (sem, val)
            d.wdma = None
        for d in writes:
            d.w = ev
            d.r = {}
            d.wdma = None

    @staticmethod
    def _d(x):
        out = []
        for v in x:
            if v is None:
                continue
            out.append(v.dep if isinstance(v, Tile) else v)
        return out

    def op(self, eng, fn, reads=(), writes=()):
        reads = self._d(reads)
        writes = self._d(writes)
        waits = self._deps(eng, reads, writes)
        self.cnt[eng] += 1
        ev = (self.esem[eng], self.cnt[eng])
        self.streams[eng].append((waits, fn, self.esem[eng], 1))
        self._commit(ev, reads, writes)
        self.ninst += 1
        return ev

    def dma(self, eng, out, in_, reads=(), writes=(), owner=None, **kw):
        reads = self._d(reads)
        writes = self._d(writes)
        owner = owner.dep if isinstance(owner, Tile) else owner
        rec = self._dsem(owner, "sw" if eng == "pool" else "hw")
        sem = rec[0]
        waits = self._deps(eng, reads, writes, same_gen=sem)
        if rec[2] and rec[1] > 0:
            w2 = {}
            self._need(eng, w2, (sem, rec[1]))
            if w2:
                waits = [x for x in waits if x[0] is not sem] + [(sem, rec[1])]
        rec[2] = False
        rec[1] += 16
        ev = (sem, rec[1])

        def fn(e, out=out, in_=in_, kw=kw):
            o_ = out(e) if callable(out) else out
            i_ = in_(e) if callable(in_) else in_
            return e.dma_start(out=o_, in_=i_, **kw)
        self.streams[eng].append((waits, fn, sem, 16))
        self._commit(ev, reads, writes)
        if owner in writes:
            owner.wdma = sem
        self.ninst += 1
        return ev

    def collective(self, ins_ap, outs_ap, groups, reads=(), writes=()):
        reads = self._d(reads)
        writes = self._d(writes)
        if self.ccsem is None:
            self.ccsem = self.es.enter_context(self.nc.semaphore("ccsem"))
        waits = self._deps("pool", reads, writes)
        self.cccnt += 1
        ev = (self.ccsem, self.cccnt)

        def fn(e):
            return e.collective_compute("AllGather", mybir.AluOpType.bypass, replica_groups=groups,
                                        ins=[ins_ap.opt()], outs=[outs_ap.opt()])
        self.streams["pool"].append((waits, fn, self.ccsem, 1))
        self._commit(ev, reads, writes)
        return ev

    def rbase(self, e, q, mult):
        key = (q, mult)
        if key not in self.rcache:
            self.rcache[key] = (e.partition_id() % 4) * mult
        return self.rcache[key]

    def barrier(self):
        evs = [(self.esem[e], self.cnt[e]) for e in self.ENG if self.cnt[e] > 0]
        if self.cccnt > 0:
            evs.append((self.ccsem, self.cccnt))
        for rec in self.dma_sems:
            if rec[1] > 0:
                evs.append((rec[0], rec[1]))
        for e in self.ENG:
            waits = {}
            for ev in evs:
                self._need(e, waits, ev)
            if waits:
                self.streams[e].append((list(waits.values()), None, None, 0))

    def flush(self):
        self.barrier()
        nc = self.nc
        self.blockno += 1
        hmap = {"pe": "tensor", "act": "scalar", "dve": "vector", "pool": "gpsimd", "sp": "sync"}
        with nc.Block() as block:
            for e in self.ENG:
                stream = self.streams[e]

                def body(eng, stream=stream):
                    for waits, fn, sem, inc in stream:
                        for (s, v) in waits:
                            eng.wait_ge(s, v)
                        if fn is not None:
                            fn(eng).then_inc(sem, inc)
                getattr(block, hmap[e])(body)
        self.streams = {e: [] for e in self.ENG}
        for dep, q in self.active:
            self.free_recs[q].append(dep.sems.pop(q))
        self.active = []

    def finish(self):
        self.flush()
        self.es.close()
        return self.nc


def _consts_for_seq(L, N1, N2):
    c = {}
    n1 = np.arange(N1)[:, None]
    k1 = np.arange(N1)[None, :]
    ang = 2 * np.pi * n1 * k1 / N1
    c["FA"] = np.concatenate([np.cos(ang), -np.sin(ang)], axis=1).astype(np.float32)
    n2 = np.arange(N2)[:, None]
    ang = 2 * np.pi * n2 * np.arange(N1)[None, :] / L
    c["TW"] = np.stack([np.cos(ang), -np.sin(ang)], axis=1).astype(np.float32)
    k2 = np.arange(N2)[None, :]
    ang = 2 * np.pi * n2 * k2 / N2
    Cm, Sm = np.cos(ang), np.sin(ang)
    c["FC1"] = np.concatenate([Cm, -Sm], axis=1).astype(np.float32)
    c["FC2"] = np.concatenate([Sm, Cm], axis=1).astype(np.float32)
    pos = np.arange(L, dtype=np.float32)
    inv = (10000.0 ** (-np.arange(0, 32, 2, dtype=np.float32) / 32)).astype(np.float32)
    a = (pos[None, :] * inv[:, None]).astype(np.float32)
    cos2 = np.concatenate([np.cos(a), np.cos(a)], axis=0)
    sin2 = np.concatenate([-np.sin(a), np.sin(a)], axis=0)
    scale = 96 ** -0.5
    c["ROPE"] = np.stack([np.stack([cos2 * scale, sin2 * scale]), np.stack([cos2, sin2])]).astype(np.float32)
    return c


def _global_consts():
    c = {}
    c["ONES"] = np.ones((128, 128), np.float32)
    j = np.arange(64)
    ang = 2 * np.pi * j[:, None] * j[None, :] / 64
    C64 = np.zeros((128, 128), np.float32)
    S64 = np.zeros((128, 128), np.float32)
    for b in range(2):
        C64[b * 64:(b + 1) * 64, b * 64:(b + 1) * 64] = np.cos(ang)
        S64[b * 64:(b + 1) * 64, b * 64:(b + 1) * 64] = np.sin(ang)
    c["CS64"] = np.stack([C64, S64], axis=1).astype(np.float32)
    I2 = np.zeros((128, 64), np.float32)
    I2[np.arange(128), np.arange(128) % 64] = 1.0
    c["I2"] = I2
    I16 = np.zeros((16, 16), np.float32)
    I16[np.arange(16), np.arange(16)] = 1.0
    c["I16"] = I16
    m = np.zeros((128, 4), np.float32)
    for p in range(128):
        m[p, (p % 64) // 16] = 1.0
    c["EMASK"] = m
    return c


def _stage_params(inp, depth):
    o = {}
    f = lambda a: np.ascontiguousarray(np.asarray(a, dtype=np.float32))
    w_in = f(inp["w_in"])
    o["w_in"] = w_in
    kpe = w_in[:, :, 896:928]
    o["w_kpesw"] = f(np.concatenate([kpe[:, :, 16:32], kpe[:, :, 0:16]], axis=2))
    w_uq = f(inp["w_uq"]).reshape(depth, 256, H, 96)
    o["w_uq"] = f(w_uq.reshape(depth, 256, 768))
    sw = np.concatenate([w_uq[..., 0:64], w_uq[..., 80:96], w_uq[..., 64:80]], axis=-1)
    o["w_uq_sw"] = f(sw.reshape(depth, 256, 768))
    w_ukv = f(inp["w_ukv"]).reshape(depth, 128, H, 128)
    o["w_uk"] = f(w_ukv[..., 0:64].reshape(depth, 128, 512))
    o["w_uv"] = f(w_ukv[..., 64:128].reshape(depth, 128, 512))
    for n in ("w_glu", "w_br_f", "w_br_s", "w_br_a", "w_o", "w_gu", "w_down"):
        o[n] = f(inp[n])
    def pk(v, kc):
        return f(v).reshape(depth, kc, 128).transpose(0, 2, 1)
    bg = f(inp["b_gate"]).reshape(depth, 3 * 8, 128).transpose(0, 2, 1)
    o["vecs"] = f(np.concatenate([pk(inp["ln1_g"], 8), pk(inp["ln1_b"], 8), pk(inp["ln2_g"], 8), pk(inp["ln2_b"], 8),
                                  bg, pk(inp["q_norm_g"], 2), pk(inp["kv_norm_g"], 1)], axis=2))
    are, aim, ldt = f(inp["s5_a_re"]), f(inp["s5_a_im"]), f(inp["s5_log_dt"])
    def layA(a):
        t = a.reshape(depth, 32, 64).transpose(0, 2, 1)
        return np.concatenate([t, t], axis=1)
    ldt_full = np.broadcast_to(ldt[..., None], (depth, 2, 16, 64))
    o["s5A_a"] = f(np.stack([layA(are), layA(aim), layA(ldt_full)], axis=2))
    def layA3(b):
        t = b.reshape(depth, 32, 64, 16).transpose(0, 2, 1, 3)
        return np.concatenate([t, t], axis=1)
    o["s5A_b"] = f(np.stack([layA3(f(inp["s5_b_re"])), layA3(f(inp["s5_b_im"]))], axis=2))
    cre = f(inp["s5_c_re"]).transpose(0, 1, 2, 4, 3)
    cim = f(inp["s5_c_im"]).transpose(0, 1, 2, 4, 3)
    o["s5A_c"] = f(np.stack([layA3(cre), layA3(cim)], axis=2))
    def layB(a):
        t = np.repeat(a[:, :, :, None, :], 16, axis=3).reshape(depth, 2, 256, 64)
        t = t.reshape(depth, 2, 2, 128, 64).transpose(0, 3, 2, 1, 4)
        return t
    o["s5B_a"] = f(np.stack([layB(are), layB(aim), layB(ldt_full)], axis=2))
    def layBb(b):
        t = b.transpose(0, 1, 2, 4, 3).reshape(depth, 2, 256, 64)
        return t.reshape(depth, 2, 2, 128, 64).transpose(0, 3, 2, 1, 4)
    o["s5B_b"] = f(np.stack([layBb(f(inp["s5_b_re"])), layBb(f(inp["s5_b_im"]))], axis=2))
    o["s5_dT"] = f(f(inp["s5_d"]).reshape(depth, 16, 16).transpose(0, 2, 1))
    return o


WSHAPES = {
    "w_in": (1024, 4000), "w_kpesw": (1024, 32), "w_uq": (256, 768), "w_uq_sw": (256, 768),
    "w_uk": (128, 512), "w_uv": (128, 512), "w_glu": (256, 256), "w_br_f": (256, 1024),
    "w_br_s": (256, 1024), "w_br_a": (512, 1024), "w_o": (1024, 1024), "w_gu": (1024, 5632),
    "w_down": (2816, 1024),
}
PSHAPES = {"vecs": (128, 59), "s5A_a": (128, 3, 32), "s5A_b": (128, 2, 32, 16), "s5A_c": (128, 2, 32, 16),
           "s5B_a": (128, 3, 2, 2, 64), "s5B_b": (128, 2, 2, 2, 64), "s5_dT": (16, 16)}


class Prog:
    def __init__(self, cfg):
        self.cfg = cfg
        self.depth = cfg["depth"]
        self.seqs = cfg["seqs"]
        self.SEG = cfg["SEG"]
        self.debug = cfg.get("debug", False)
        self.k = KB()
        self.nc = self.k.nc
        self.dr = {}
        self.build()

    def din(self, name, shape, dtype=F32):
        t = self.k.dram(name, shape, dtype, kind="ExternalInput")
        self.dr[name] = t
        return t

    def dscr(self, name, shape, dtype, out=False):
        kind = "ExternalOutput" if (out or self.debug) else "Internal"
        t = self.k.dram(name, shape, dtype, kind=kind)
        self.dr[name] = t
        return t

    def dbg(self, name, tile, shape, dtype=F32, deps=None):
        if not self.debug:
            return
        t = self.k.dram("dbg_" + name, list(shape), dtype, kind="ExternalOutput")
        self.dr["dbg_" + name] = t
        self.k.dma("sp", t.ap(), tile[:], reads=[tile] + list(deps or []), owner=tile)

    def bank(self, b):
        return self.pst[b // 2][:, (b % 2) * 512:(b % 2) * 512 + 512]

    def nextbank(self):
        b = self.bankrr
        self.bankrr = (self.bankrr + 1) % 8
        return b

    def build(self):
        k = self.k
        depth = self.depth
        for n, (K_, N_) in WSHAPES.items():
            self.din(n, [depth, K_, N_])
            self.dscr("b_" + n, [depth, K_, N_], BF16)
        for n, shp in PSHAPES.items():
            self.din(n, [depth] + list(shp))
        gc = _global_consts()
        for n, v in gc.items():
            self.din("c_" + n, list(v.shape))
        for si, s in enumerate(self.seqs):
            L = s["L"]
            self.din(f"xT{si}", [D, L])
            cs = _consts_for_seq(L, s["N1"], s["N2"])
            for n, v in cs.items():
                self.din(f"c{si}_{n}", list(v.shape))
            if not s.get("split"):
                self.dscr(f"yT{si}", [D, L], F32, out=True)
                self.dscr(f"xb{si}", [D, L], F32)
            self.dscr(f"hf{si}", [4, L, 64], F32)
            self.dscr(f"hsT{si}", [256, L], F32)
            self.dscr(f"cqT{si}", [256, L], F32)
            self.dscr(f"ckvT{si}", [128, L], F32)
            self.dscr(f"kpeT{si}", [64, L], F32)
            self.dscr(f"fT{si}", [2, 256, L], BF16)
            self.dscr(f"s5T{si}", [256, L], BF16)
            self.dscr(f"attnT{si}", [512, L], BF16)
            self.dscr(f"qT{si}", [H, 96, L], BF16)
            self.dscr(f"kT{si}", [H, 96, L], BF16)
            self.dscr(f"v{si}", [L, 512], BF16)
            nseg = L // self.SEG
            self.dscr(f"sbst{si}", [nseg, 128, 16, self.SEG // 8 + 2], BF16)
            if s.get("split"):
                Lq = L // 4
                self.dscr(f"yq{si}", [D, Lq], F32, out=True)
                self.dscr(f"xq{si}", [Lq // 256, D, 256], F32)
                self.dscr(f"xg{si}", [Lq // 256, 4 * D, 256], F32)
                self.dscr(f"attnq{si}", [512, Lq], BF16)
                self.dscr(f"qm{si}", [H, 96, Lq], BF16)
                self.dscr(f"fm{si}", [2, 256, Lq], BF16)
                self.dscr(f"s5m{si}", [256, Lq], BF16)
                self.dscr(f"xm{si}", [D, Lq], F32)
        self.dscr("Kd", [16, 15, 16, 16], F32)
        self.dscr("rotd", [2, 12, 128, 16, 128], BF16)

        self.pst = [k.ps(f"ps{i}", [128, 1024]) for i in range(4)]
        self.bdep = [Dep(f"bank{i}") for i in range(8)]
        self.bankrr = 0
        self.ones = k.sb("ones", [128, 128], F32)
        k.dma("sp", self.ones[:], self.dr["c_ONES"].ap(), writes=[self.ones], owner=self.ones)

        cd = Dep("cast")
        for n, (K_, N_) in WSHAPES.items():
            for l in range(depth):
                rstep = 256
                for r0 in range(0, K_, rstep):
                    r1 = min(K_, r0 + rstep)
                    k.dma("pool", self.dr["b_" + n].ap()[l, r0:r1, :], self.dr[n].ap()[l, r0:r1, :],
                          writes=[cd], owner=cd)
        k.flush()

        for l in range(depth):
            self.layer(l)

    def xsrc(self, l, si):
        return self.dr[f"xT{si}"] if l == 0 else self.dr[f"xb{si}"]

    def xblock(self, l, si, b):
        s = self.seqs[si]
        if s.get("split") and l > 0:
            nbq = (s["L"] // 4) // 512
            r, j = b // nbq, b % nbq
            xg = self.dr[f"xg{si}"].ap()
            return [(slice(h * 256, (h + 1) * 256), xg[2 * j + h, r * D:(r + 1) * D, :].rearrange("(kc p) t -> p kc t", p=128)) for h in range(2)]
        return [(slice(0, 512), self.xsrc(l, si).ap().rearrange("(kc p) t -> p kc t", p=128)[:, :, b * 512:(b + 1) * 512])]

    def xdst(self, l, si):
        return self.dr[f"yT{si}"] if l == self.depth - 1 else self.dr[f"xb{si}"]

    def layer(self, l):
        k = self.k
        phases = self.cfg.get("phases", ("p1", "s5", "fourier", "mla", "p3"))
        if "p1" in phases:
            with contextlib.ExitStack() as st:
                self.p1_setup(l, st)
                for si in range(len(self.seqs)):
                    self.p1(l, si, st)
                k.flush()
        if "s5" in phases:
            with contextlib.ExitStack() as st:
                self.s5_prep(l, st)
                for si in range(len(self.seqs)):
                    self.s5_seq(l, si, st)
                k.flush()
        if "fourier" in phases:
            for si in range(len(self.seqs)):
                with contextlib.ExitStack() as st:
                    self.fourier(l, si, st)
                    k.flush()
        if "mla" in phases:
            for si in range(len(self.seqs)):
                with contextlib.ExitStack() as st:
                    self.mla_pre(l, si, st)
                    k.flush()
                with contextlib.ExitStack() as st:
                    self.attn(l, si, st)
                    k.flush()
        if "p3" in phases:
            with contextlib.ExitStack() as st:
                self.p3_setup(l, st)
                for si in range(len(self.seqs)):
                    self.p3(l, si, st)
                k.flush()
            if l < self.depth - 1:
                for si, s_ in enumerate(self.seqs):
                    if s_.get("split"):
                        for c_ in range((s_["L"] // 4) // 256):
                            k.collective(self.dr[f"xq{si}"].ap()[c_], self.dr[f"xg{si}"].ap()[c_], [[0, 1, 2, 3], [4, 5, 6, 7]])
                        k.flush()

    def p1_setup(self, l, st):
        k = self.k
        self.W1 = k.sb("W1", [128, 8, 960], BF16, st)
        win = self.dr["b_w_in"].ap()[l].rearrange("(kc p) n -> p kc n", p=128)
        k.dma("sp", self.W1[:, :, 0:928], win[:, :, 0:928], writes=[self.W1], owner=self.W1)
        wsw = self.dr["b_w_kpesw"].ap()[l].rearrange("(kc p) n -> p kc n", p=128)
        k.dma("sp", self.W1[:, :, 928:960], wsw, writes=[self.W1], owner=self.W1)
        self.p1_xf = [k.sb(f"xf{i}", [128, 8, 512], F32, st) for i in range(2)]
        self.p1_xb = [k.sb(f"xbf{i}", [128, 8, 512], BF16, st) for i in range(2)]
        self.p1_st = [k.sb(f"p1st{i}", [128, 6, 512], F32, st) for i in range(2)]
        self.p1_hf = [k.sb(f"p1hf{i}", [128, 4, 256], F32, st) for i in range(2)]
        self.p1_i = 0

    def p1(self, l, si, st):
        k = self.k
        L = self.seqs[si]["L"]
        W1 = self.W1
        for b in range(L // 512):
            i = self.p1_i % 2
            self.p1_i += 1
            xf, xb, stg, hfs = self.p1_xf[i], self.p1_xb[i], self.p1_st[i], self.p1_hf[i]
            ts = slice(b * 512, (b + 1) * 512)
            for (dsl, sap) in self.xblock(l, si, b):
                k.dma("sp", xf[:, :, dsl], sap, writes=[xf], owner=xf)
            k.op("act", lambda e, xf=xf, xb=xb: e.activation(xb[:, 0:4, :], xf[:, 0:4, :], AF.Copy), reads=[xf], writes=[xb])
            k.op("pool", lambda e, xf=xf, xb=xb: e.tensor_copy(xb[:, 4:8, :], xf[:, 4:8, :]), reads=[xf], writes=[xb])
            for cc in range(6):
                c0 = 256 + cc * 128
                M = 128 if cc < 5 else 64
                bk = self.nextbank()
                for kc in range(8):
                    k.op("pe", lambda e, bk=bk, kc=kc, c0=c0, M=M, xb=xb: e.matmul(
                        self.bank(bk)[0:M, :], lhsT=W1[:, kc, c0:c0 + M], rhs=xb[:, kc, :],
                        start=(kc == 0), stop=(kc == 7)), reads=[W1, xb], writes=[self.bdep[bk]])
                eng = "act" if cc % 2 == 0 else "dve"
                if eng == "act":
                    k.op("act", lambda e, bk=bk, cc=cc, M=M, stg=stg: e.activation(stg[0:M, cc, :], self.bank(bk)[0:M, :], AF.Copy),
                         reads=[self.bdep[bk]], writes=[stg])
                else:
                    k.op("dve", lambda e, bk=bk, cc=cc, M=M, stg=stg: e.tensor_copy(stg[0:M, cc, :], self.bank(bk)[0:M, :]),
                         reads=[self.bdep[bk]], writes=[stg])
            k.dma("pool", self.dr[f"hsT{si}"].ap().rearrange("(c p) t -> p c t", p=128)[:, :, ts], stg[:, 0:2, :], reads=[stg], owner=stg)
            k.dma("pool", self.dr[f"cqT{si}"].ap().rearrange("(c p) t -> p c t", p=128)[:, :, ts], stg[:, 2:4, :], reads=[stg], owner=stg)
            k.dma("pool", self.dr[f"ckvT{si}"].ap()[:, ts], stg[:, 4, :], reads=[stg], owner=stg)
            k.dma("pool", self.dr[f"kpeT{si}"].ap()[:, ts], stg[0:64, 5, :], reads=[stg], owner=stg)
            for sub in range(4):
                bk = self.nextbank() if sub % 2 == 0 else bk
                o0 = (sub % 2) * 256
                for kc in range(8):
                    k.op("pe", lambda e, bk=bk, kc=kc, o0=o0, sub=sub, xb=xb: e.matmul(
                        self.bank(bk)[:, o0:o0 + 256], lhsT=xb[:, kc, sub * 128:(sub + 1) * 128], rhs=W1[:, kc, 0:256],
                        start=(kc == 0), stop=(kc == 7)), reads=[W1, xb], writes=[self.bdep[bk]])
                if sub % 2 == 1:
                    k.op("act", lambda e, bk=bk, sub=sub, hfs=hfs: e.activation(
                        hfs[:, sub - 1:sub + 1, :], self.bank(bk).rearrange("p (a b) -> p a b", a=2), AF.Copy),
                        reads=[self.bdep[bk]], writes=[hfs])
            hfd = self.dr[f"hf{si}"].ap().rearrange("g (n p) c -> p n g c", p=128)
            for g in range(4):
                k.dma("pool", hfd[:, b * 4:(b + 1) * 4, g, :], hfs[:, :, g * 64:(g + 1) * 64], reads=[hfs], owner=hfs)

    def cmul(self, eng, outr, outi, ar, ai, br, bi, t1, t2, reads, writes, tmps):
        k = self.k
        E = eng
        k.op(E, lambda e: e.tensor_tensor(t1, ar, br, ALU.mult), reads=reads, writes=tmps)
        k.op(E, lambda e: e.tensor_tensor(t2, ai, bi, ALU.mult), reads=reads, writes=tmps)
        k.op(E, lambda e: e.tensor_tensor(outr, t1, t2, ALU.subtract), reads=tmps, writes=writes)
        k.op(E, lambda e: e.tensor_tensor(t1, ar, bi, ALU.mult), reads=reads + writes, writes=tmps)
        k.op(E, lambda e: e.tensor_tensor(t2, ai, br, ALU.mult), reads=reads, writes=tmps)
        k.op(E, lambda e: e.tensor_tensor(outi, t1, t2, ALU.add), reads=tmps, writes=writes)

    def abar_calc(self, st, tag, src, F):
        k = self.k
        R = k.sb("abR" + tag, [128, 4, F], F32, st)
        T = k.sb("abT" + tag, [128, 8, F], F32, st)
        are, aim, ldt = src[:, 0, :], src[:, 1, :], src[:, 2, :]
        dt, xr, xi, mag, s, c, m, ab = [T[:, i, :] for i in range(8)]
        k.op("act", lambda e: e.activation(dt, ldt, AF.Exp), reads=[src], writes=[T])
        k.op("dve", lambda e: e.tensor_tensor(xr, are, dt, ALU.mult), reads=[src, T], writes=[T])
        k.op("dve", lambda e: e.tensor_tensor(xi, aim, dt, ALU.mult), reads=[src, T], writes=[T])
        k.op("act", lambda e: e.activation(mag, xr, AF.Exp), reads=[T], writes=[T])
        for _ in range(4):
            k.op("dve", lambda e: e.tensor_scalar(m, xi, PI, -2 * PI, ALU.is_gt, ALU.mult), reads=[T], writes=[T])
            k.op("dve", lambda e: e.tensor_tensor(xi, xi, m, ALU.add), reads=[T], writes=[T])
        k.op("act", lambda e: e.activation(s, xi, AF.Sin), reads=[T], writes=[T])
        k.op("act", lambda e: e.activation(ab, xi, AF.Abs), reads=[T], writes=[T])
        k.op("act", lambda e: e.activation(c, ab, AF.Sin, scale=-1.0, bias=PI / 2), reads=[T], writes=[T])
        k.op("dve", lambda e: e.tensor_tensor(R[:, 0, :], mag, c, ALU.mult), reads=[T], writes=[R])
        k.op("dve", lambda e: e.tensor_tensor(R[:, 1, :], mag, s, ALU.mult), reads=[T], writes=[R])
        den, rden, ir, ii, nr = dt, xr, mag, s, c
        k.op("dve", lambda e: e.tensor_tensor(den, are, are, ALU.mult), reads=[src, T], writes=[T])
        k.op("dve", lambda e: e.tensor_tensor(m, aim, aim, ALU.mult), reads=[src, T], writes=[T])
        k.op("dve", lambda e: e.tensor_tensor(den, den, m, ALU.add), reads=[T], writes=[T])
        k.op("dve", lambda e: e.reciprocal(rden, den), reads=[T], writes=[T])
        k.op("dve", lambda e: e.tensor_tensor(ir, are, rden, ALU.mult), reads=[src, T], writes=[T])
        k.op("dve", lambda e: e.scalar_tensor_tensor(ii, aim, -1.0, rden, ALU.mult, ALU.mult), reads=[src, T], writes=[T])
        k.op("dve", lambda e: e.tensor_scalar(nr, R[:, 0, :], -1.0, None, ALU.add), reads=[R, T], writes=[T])
        self.cmul("dve", R[:, 2, :], R[:, 3, :], nr, R[:, 1, :], ir, ii, xi, ab, [T, R], [R], [T])
        return R

    def s5_prep(self, l, st):
        k = self.k
        NJ = self.SEG // 8
        self.KcT = k.sb("KcT", [128, 2, 15, 64], BF16, st)
        self.EL = k.sb("EL", [128, 2, 4, 2, 8, 128], BF16, st)
        self.WL = k.sb("WL", [128, 32, 8, 64], BF16, st)
        self.SQ = k.sb("SQ", [128, 2, 12, 32], F32, st)
        self.I2 = k.sb("I2", [128, 64], F32, st)
        self.wglu = k.sb("wglu", [128, 2, 256], BF16, st)
        k.dma("sp", self.I2[:], self.dr["c_I2"].ap(), writes=[self.I2], owner=self.I2)
        k.dma("sp", self.wglu[:], self.dr["b_w_glu"].ap()[l].rearrange("(kc p) n -> p kc n", p=128), writes=[self.wglu], owner=self.wglu)
        k.op("pool", lambda e: e.memset(self.KcT[:], 0.0), writes=[self.KcT])
        k.op("pool", lambda e: e.memset(self.WL[:], 0.0), writes=[self.WL])
        with contextlib.ExitStack() as ps:
            sA = k.sb("sA", [128, 3, 32], F32, ps)
            bA = k.sb("bA", [128, 2, 32, 16], F32, ps)
            cA = k.sb("cA", [128, 2, 32, 16], F32, ps)
            k.dma("sp", sA[:], self.dr["s5A_a"].ap()[l], writes=[sA], owner=sA)
            k.dma("sp", bA[:], self.dr["s5A_b"].ap()[l], writes=[bA], owner=bA)
            k.dma("sp", cA[:], self.dr["s5A_c"].ap()[l], writes=[cA], owner=cA)
            RA = self.abar_calc(ps, "A", sA, 32)
            PW = k.sb("PWA", [128, 2, 9, 32], F32, ps)
            tA = k.sb("tA", [128, 2, 32], F32, ps)
            k.op("dve", lambda e: e.memset(PW[:, 0, 0, :], 1.0), writes=[PW])
            k.op("dve", lambda e: e.memset(PW[:, 1, 0, :], 0.0), writes=[PW])
            for t in range(1, 9):
                self.cmul("dve", PW[:, 0, t, :], PW[:, 1, t, :], PW[:, 0, t - 1, :], PW[:, 1, t - 1, :], RA[:, 0, :], RA[:, 1, :],
                          tA[:, 0, :], tA[:, 1, :], [PW, RA], [PW], [tA])
            SQ = self.SQ
            SQf = k.sb("SQf", [128, 2, 12, 32], F32, ps)
            k.op("dve", lambda e: e.tensor_copy(SQf[:, :, 0, :], PW[:, :, 8, :]), reads=[PW], writes=[SQf])
            for m in range(1, 12):
                self.cmul("dve", SQf[:, 0, m, :], SQf[:, 1, m, :], SQf[:, 0, m - 1, :], SQf[:, 1, m - 1, :],
                          SQf[:, 0, m - 1, :], SQf[:, 1, m - 1, :], tA[:, 0, :], tA[:, 1, :], [SQf], [SQf], [tA])
            k.op("dve", lambda e: e.tensor_copy(SQ[:, 0, :, :], SQf[:, 0, :, :]), reads=[SQf], writes=[SQ])
            k.op("dve", lambda e: e.tensor_copy(SQ[0:64, 1, :, :], SQf[0:64, 1, :, :]), reads=[SQf], writes=[SQ])
            k.op("dve", lambda e: e.tensor_scalar(SQ[64:128, 1, :, :], SQf[64:128, 1, :, :], -1.0, None, ALU.mult), reads=[SQf], writes=[SQ])
            nst = 0
            dd_ = 1
            while dd_ < self.SEG // 8 + 1:
                nst += 1
                dd_ *= 2
            self.nst = nst
            RT = [k.sb(f"RT{i}", [128, 16, 128], BF16, ps) for i in range(2)]
            self.rotdep = Dep("rotd")
            ri = 0
            for dirn in range(2):
                for mi in range(nst):
                    R = RT[ri % 2]
                    ri += 1
                    ar = SQ[:, 0, mi, dirn * 16:(dirn + 1) * 16]
                    ai = SQ[:, 1, mi, dirn * 16:(dirn + 1) * 16]
                    for (r0, c0, src) in ((0, 0, ar), (0, 64, ai), (64, 0, ai), (64, 64, ar)):
                        k.op("dve", lambda e, r0=r0, c0=c0, src=src, R=R: e.tensor_tensor(
                            R[r0:r0 + 64, :, c0:c0 + 64], self.I2[r0:r0 + 64, :].unsqueeze(1).broadcast_to([64, 16, 64]),
                            src[r0:r0 + 64, :].unsqueeze(2).broadcast_to([64, 16, 64]), ALU.mult), reads=[self.I2, SQ], writes=[R])
                    k.dma("sp", self.dr["rotd"].ap()[dirn, mi], R[:], reads=[R], writes=[self.rotdep], owner=R)
            bbA = k.sb("bbA", [128, 2, 32, 16], F32, ps)
            tB = k.sb("tB", [128, 2, 32, 16], F32, ps)
            cr = RA[:, 2, :].unsqueeze(2).broadcast_to([128, 32, 16])
            ci = RA[:, 3, :].unsqueeze(2).broadcast_to([128, 32, 16])
            self.cmul("dve", bbA[:, 0], bbA[:, 1], cr, ci, bA[:, 0], bA[:, 1], tB[:, 0], tB[:, 1], [RA, bA], [bbA], [tB])
            CAf = k.sb("CAf", [128, 2, 16, 9, 16], F32, ps)
            tC = k.sb("tC", [128, 2, 16, 9, 16], F32, ps)
            CAS = k.sb("CAS", [128, 32, 9, 16], F32, ps)
            shp = [128, 16, 9, 16]
            for dd_ in range(2):
                gs_ = slice(dd_ * 16, dd_ * 16 + 16)
                pr = PW[:, 0, :, gs_].rearrange("p t g -> p g t").unsqueeze(3).broadcast_to(shp)
                pi_ = PW[:, 1, :, gs_].rearrange("p t g -> p g t").unsqueeze(3).broadcast_to(shp)
                c_r = cA[:, 0, gs_, :].unsqueeze(2).broadcast_to(shp)
                c_i = cA[:, 1, gs_, :].unsqueeze(2).broadcast_to(shp)
                self.cmul("dve", CAf[:, 0], CAf[:, 1], c_r, c_i, pr, pi_, tC[:, 0], tC[:, 1], [PW, cA], [CAf], [tC])
                k.op("dve", lambda e, gs_=gs_: e.tensor_copy(CAS[0:64, gs_], CAf[0:64, 0]), reads=[CAf], writes=[CAS])
                k.op("dve", lambda e, gs_=gs_: e.tensor_scalar(CAS[64:128, gs_], CAf[64:128, 1], -1.0, None, ALU.mult), reads=[CAf], writes=[CAS])
            BBS = k.sb("BBS", [128, 32, 16], F32, ps)
            k.op("dve", lambda e: e.tensor_copy(BBS[0:64], bbA[0:64, 0]), reads=[bbA], writes=[BBS])
            k.op("dve", lambda e: e.tensor_copy(BBS[64:128], bbA[64:128, 1]), reads=[bbA], writes=[BBS])
            WLv = self.WL[:].rearrange("p (d h e) t (f c) -> p d h e t f c", d=2, h=4, e=4, f=4)
            CASv = CAS[:].rearrange("p (d h e) t c -> p d h e t c", d=2, h=4, e=4)
            for e_ in range(4):
                k.op("dve", lambda e, e_=e_: e.tensor_copy(WLv[:, 0, :, e_, :, e_, :], CASv[:, 0, :, e_, 1:9, :]), reads=[CAS], writes=[self.WL])
                for t in range(8):
                    k.op("pool", lambda e, e_=e_, t=t: e.tensor_copy(WLv[:, 1, :, e_, t, e_, :], CASv[:, 1, :, e_, 8 - t, :]), reads=[CAS], writes=[self.WL])
            Ksb = k.sb("Ksb", [16, 32, 8, 16], F32, ps)
            for q in range(8):
                bk = self.nextbank()
                for j in range(4):
                    kg = q * 4 + j
                    k.op("pe", lambda e, bk=bk, kg=kg, j=j: e.matmul(
                        self.bank(bk)[0:16, j * 128:(j + 1) * 128], lhsT=BBS[:, kg, :], rhs=CAS[:, kg, 0:8, :],
                        start=True, stop=True), reads=[BBS, CAS], writes=[self.bdep[bk]])
                k.op("act", lambda e, bk=bk, q=q: e.activation(Ksb[:, q * 4:(q + 1) * 4, :, :], self.bank(bk)[0:16, :].rearrange("p (a b c) -> p a b c", a=4, b=8), AF.Copy),
                     reads=[self.bdep[bk]], writes=[Ksb])
            Kcs = k.sb("Kcs", [16, 16, 15, 16], F32, ps)
            dT = k.sb("dT", [16, 16], F32, ps)
            I16 = k.sb("I16", [16, 16], F32, ps)
            dI = k.sb("dI", [16, 16, 16], F32, ps)
            k.dma("sp", dT[:], self.dr["s5_dT"].ap()[l], writes=[dT], owner=dT)
            k.dma("sp", I16[:], self.dr["c_I16"].ap(), writes=[I16], owner=I16)
            k.op("dve", lambda e: e.tensor_copy(Kcs[:, :, 8:15, :], Ksb[:, 0:16, 1:8, :]), reads=[Ksb], writes=[Kcs])
            for t in range(1, 8):
                k.op("dve", lambda e, t=t: e.tensor_copy(Kcs[:, :, 7 - t, :], Ksb[:, 16:32, t, :]), reads=[Ksb], writes=[Kcs])
            k.op("dve", lambda e: e.tensor_tensor(Kcs[:, :, 7, :], Ksb[:, 0:16, 0, :], Ksb[:, 16:32, 0, :], ALU.add), reads=[Ksb], writes=[Kcs])
            k.op("dve", lambda e: e.tensor_tensor(dI[:], I16[:].unsqueeze(1).broadcast_to([16, 16, 16]),
                                                   dT[:].unsqueeze(2).broadcast_to([16, 16, 16]), ALU.mult), reads=[I16, dT], writes=[dI])
            k.op("dve", lambda e: e.tensor_tensor(Kcs[:, :, 7, :], Kcs[:, :, 7, :], dI[:], ALU.add), reads=[dI, Kcs], writes=[Kcs])
            kdd = Dep("Kd")
            k.dma("sp", self.dr["Kd"].ap().rearrange("g s c d -> c g s d"), Kcs[:], reads=[Kcs], writes=[kdd], owner=Kcs)
            for g in range(16):
                tile_, hh, e_ = g // 8, (g % 8) // 4, g % 4
                p0 = 64 * hh + 16 * e_
                k.dma("pool", self.KcT[p0:p0 + 16, tile_, :, 16 * e_:16 * e_ + 16], self.dr["Kd"].ap()[g].rearrange("s c d -> c s d"),
                      reads=[kdd], writes=[self.KcT], owner=self.KcT)
            if l == 0:
                self.dbg("RA", RA, [128, 4, 32]); self.dbg("PW", PW, [128, 2, 9, 32]); self.dbg("CAS", CAS, [128, 32, 9, 16])
                self.dbg("BBS", BBS, [128, 32, 16]); self.dbg("Ksb", Ksb, [16, 32, 8, 16]); self.dbg("Kcs", Kcs, [16, 16, 15, 16])
                self.dbg("SQ", self.SQ, [128, 2, 12, 32])
            k.flush()
        with contextlib.ExitStack() as ps:
            sB = k.sb("sB", [128, 3, 256], F32, ps)
            bB = k.sb("bB", [128, 2, 256], F32, ps)
            EM = k.sb("EM", [128, 4], F32, ps)
            k.dma("sp", sB[:], self.dr["s5B_a"].ap()[l].rearrange("p a t d n -> p a (t d n)"), writes=[sB], owner=sB)
            k.dma("sp", bB[:], self.dr["s5B_b"].ap()[l].rearrange("p a t d n -> p a (t d n)"), writes=[bB], owner=bB)
            k.dma("sp", EM[:], self.dr["c_EMASK"].ap(), writes=[EM], owner=EM)
            RB = self.abar_calc(ps, "B", sB, 256)
            PB = k.sb("PB", [128, 2, 8, 256], F32, ps)
            tD = k.sb("tD", [128, 2, 256], F32, ps)
            self.cmul("dve", PB[:, 0, 0, :], PB[:, 1, 0, :], RB[:, 2, :], RB[:, 3, :], bB[:, 0, :], bB[:, 1, :], tD[:, 0, :], tD[:, 1, :], [RB, bB], [PB], [tD])
            for t in range(1, 8):
                self.cmul("dve", PB[:, 0, t, :], PB[:, 1, t, :], PB[:, 0, t - 1, :], PB[:, 1, t - 1, :], RB[:, 0, :], RB[:, 1, :],
                          tD[:, 0, :], tD[:, 1, :], [PB, RB], [PB], [tD])
            PBv = PB[:].rearrange("p r k (t d n) -> p r k t d n", t=2, d=2)
            for e_ in range(4):
                for d_ in range(2):
                    for t2 in range(8):
                        pw = 7 - t2 if d_ == 0 else t2
                        for r in range(2):
                            eng = "dve" if (t2 + r) % 2 == 0 else "pool"
                            k.op(eng, lambda e, e_=e_, d_=d_, t2=t2, pw=pw, r=r: e.tensor_scalar(
                                self.EL[:, :, e_, d_, t2, r * 64:(r + 1) * 64], PBv[:, r, pw, :, d_, :], EM[:, e_:e_ + 1], None, ALU.mult),
                                reads=[PB, EM], writes=[self.EL])
            if l == 0:
                self.dbg("RB", RB, [128, 4, 256]); self.dbg("PB", PB, [128, 2, 8, 256])
                self.dbg("EL", self.EL, [128, 2, 4, 2, 8, 128], BF16); self.dbg("WL", self.WL, [128, 32, 8, 64], BF16)
                self.dbg("KcT", self.KcT, [128, 2, 15, 64], BF16)
            k.flush()

    def s5_seq(self, l, si, st_outer):
        k = self.k
        L = self.seqs[si]["L"]
        SEG = self.SEG
        NJ = SEG // 8
        NC1 = NJ + 1
        nseg = L // SEG
        steps = []
        d_ = 1
        while d_ < NC1:
            steps.append(d_)
            d_ *= 2
        with contextlib.ExitStack() as st:
            uT = [k.sb(f"uT{i}", [128, 2, SEG], BF16, st) for i in range(2)]
            NCP = NC1 + (NC1 % 2)
            Sm = k.sb("Sm", [128, 16, NCP], F32, st)
            Sbf = [k.sb(f"Sbf{i}", [128, 16, NCP], BF16, st) for i in range(2)]
            k.op("dve", lambda e: e.memset(Sm[:], 0.0), writes=[Sm])
            RotAll = k.sb("RotAll", [128, len(steps), 16, 128], BF16, st)

            def gen_rot(dirn):
                I2 = self.I2
                for mi in range(len(steps)):
                    ar = self.SQ[:, 0, mi, dirn * 16:(dirn + 1) * 16]
                    ai = self.SQ[:, 1, mi, dirn * 16:(dirn + 1) * 16]
                    for (r0, c0, src) in ((0, 0, ar), (0, 64, ai), (64, 0, ai), (64, 64, ar)):
                        k.op("dve", lambda e, r0=r0, c0=c0, src=src, mi=mi: e.tensor_tensor(
                            RotAll[r0:r0 + 64, mi, :, c0:c0 + 64], I2[r0:r0 + 64, :].unsqueeze(1).broadcast_to([64, 16, 64]),
                            src[r0:r0 + 64, :].unsqueeze(2).broadcast_to([64, 16, 64]), ALU.mult), reads=[I2, self.SQ], writes=[RotAll])
            carry = k.sb("carry", [128, 2, 16], F32, st)
            gF = k.sb("gF", [128, 2, SEG], F32, st)
            gB = k.sb("gB", [128, 2, SEG], BF16, st)
            sg = [k.sb(f"sg{i}", [128, 512], F32, st) for i in range(2)]
            so = [k.sb(f"so{i}", [128, 2, 512], BF16, st) for i in range(2)]
            hsd = self.dr[f"hsT{si}"].ap().rearrange("(c p) t -> p c t", p=128)
            sbst = self.dr[f"sbst{si}"]
            sbdep = [Dep(f"sbst{q}") for q in range(nseg)]
            ui = [0]

            def load_u(q):
                u = uT[ui[0] % 2]
                ui[0] += 1
                k.dma("pool", u[:], hsd[:, :, q * SEG:(q + 1) * SEG], writes=[u], owner=u)
                return u

            smd = [Dep(f"sm{g}") for g in range(16)]
            sbd = [[Dep(f"sb{d2}_{g}") for g in range(16)] for d2 in range(2)]

            def scan(dirn, u, first):
                S = Sbf[dirn]
                sd = sbd[dirn]
                c_off = 1 if dirn == 0 else 0
                ccol = 0 if dirn == 0 else NJ
                for gp in range(8):
                    bk = self.nextbank()
                    for j in range(2):
                        g = gp * 2 + j
                        tile_, hh, e_ = g // 8, (g % 8) // 4, g % 4
                        for t2 in range(8):
                            k.op("pe", lambda e, bk=bk, j=j, tile_=tile_, hh=hh, e_=e_, t2=t2, u=u: e.matmul(
                                self.bank(bk)[:, j * NJ:(j + 1) * NJ], lhsT=self.EL[64 * hh:64 * hh + 64, tile_, e_, dirn, t2, :],
                                rhs=u[64 * hh:64 * hh + 64, tile_, t2:SEG:8], start=(t2 == 0), stop=(t2 == 7)),
                                reads=[self.EL, u], writes=[self.bdep[bk]])
                    g2 = slice(gp * 2, gp * 2 + 2)
                    pv = self.bank(bk)[:, 0:2 * NJ].rearrange("p (a b) -> p a b", a=2)
                    k.op("act", lambda e, pv=pv, g2=g2: e.activation(Sm[:, g2, c_off:c_off + NJ], pv, AF.Copy),
                         reads=[self.bdep[bk], Sm], writes=[smd[gp * 2], smd[gp * 2 + 1]])
                if first:
                    k.op("dve", lambda e: e.memset(Sm[:, :, ccol:ccol + 1], 0.0), reads=[Sm], writes=smd)
                else:
                    k.op("dve", lambda e: e.tensor_copy(Sm[:, :, ccol:ccol + 1], carry[:, dirn, :].unsqueeze(2)), reads=[carry, Sm], writes=smd)
                for g in range(16):
                    if g % 2 == 0:
                        k.op("act", lambda e, g=g: e.activation(S[:, g, :], Sm[:, g, :], AF.Copy), reads=[smd[g]], writes=[sd[g]])
                    else:
                        k.op("pool", lambda e, g=g: e.tensor_copy(S[:, g, :], Sm[:, g, :]), reads=[smd[g]], writes=[sd[g]])
                for mi, dd in enumerate(steps):
                    n = NC1 - dd
                    if dirn == 0:
                        srcs, dsts = slice(0, n), slice(dd, NC1)
                    else:
                        srcs, dsts = slice(dd, NC1), slice(0, n)
                    per = 2 if n <= 256 else 1
                    for g0 in range(0, 16, per):
                        bk = self.nextbank()
                        for j in range(per):
                            g = g0 + j
                            k.op("pe", lambda e, bk=bk, j=j, g=g, mi=mi, n=n, srcs=srcs: e.matmul(
                                self.bank(bk)[:, j * 256:j * 256 + n], lhsT=RotAll[:, mi, g, :], rhs=S[:, g, srcs], start=True, stop=True),
                                reads=[RotAll, sd[g]], writes=[self.bdep[bk]])
                        for j in range(per):
                            g = g0 + j
                            k.op("dve", lambda e, bk=bk, j=j, g=g, n=n, dsts=dsts: e.tensor_tensor(
                                Sm[:, g, dsts], self.bank(bk)[:, j * 256:j * 256 + n], Sm[:, g, dsts], ALU.add),
                                reads=[self.bdep[bk], smd[g]], writes=[smd[g]])
                            ceng = "act" if g % 2 == 0 else "pool"
                            if ceng == "act":
                                k.op("act", lambda e, g=g: e.activation(S[:, g, :], Sm[:, g, :], AF.Copy), reads=[smd[g]], writes=[sd[g]])
                            else:
                                k.op("pool", lambda e, g=g: e.tensor_copy(S[:, g, :], Sm[:, g, :]), reads=[smd[g]], writes=[sd[g]])
                ocol = NJ if dirn == 0 else 0
                k.op("dve", lambda e: e.tensor_copy(carry[:, dirn, :].unsqueeze(2), Sm[:, :, ocol:ocol + 1]), reads=smd, writes=[carry])

            gen_rot(1)
            for q in range(nseg - 1, -1, -1):
                u = load_u(q)
                scan(1, u, q == nseg - 1)
                k.dma("sp", sbst.ap()[q], Sbf[1][:], reads=sbd[1], writes=[sbdep[q]], owner=Sbf[1])
            gen_rot(0)
            for q in range(nseg):
                u = load_u(q)
                scan(0, u, q == 0)
                k.dma("sp", Sbf[1][:], sbst.ap()[q], reads=[sbdep[q]], writes=sbd[1] + [Sbf[1]], owner=Sbf[1])
                for h4 in range(4):
                    tile_, hh = h4 // 2, h4 % 2
                    pr = slice(64 * hh, 64 * hh + 64)
                    for tp in range(4):
                        bk = self.nextbank()
                        for j in range(2):
                            t = tp * 2 + j
                            oc = slice(j * NJ, (j + 1) * NJ)
                            for t2 in range(8):
                                k.op("pe", lambda e, bk=bk, oc=oc, t=t, t2=t2, tile_=tile_, pr=pr, u=u: e.matmul(
                                    self.bank(bk)[pr, oc], lhsT=self.KcT[pr, tile_, t - t2 + 7, :], rhs=u[pr, tile_, t2:SEG:8],
                                    start=(t2 == 0), stop=False), reads=[self.KcT, u], writes=[self.bdep[bk]])
                            for e_ in range(4):
                                g = h4 * 4 + e_
                                k.op("pe", lambda e, bk=bk, oc=oc, t=t, g=g, pr=pr: e.matmul(
                                    self.bank(bk)[pr, oc], lhsT=self.WL[:, g, t, :], rhs=Sbf[0][:, g, 0:NJ],
                                    start=False, stop=False), reads=[self.WL, sbd[0][g]], writes=[self.bdep[bk]])
                                k.op("pe", lambda e, bk=bk, oc=oc, t=t, g=g, pr=pr, e_=e_: e.matmul(
                                    self.bank(bk)[pr, oc], lhsT=self.WL[:, 16 + g, t, :], rhs=Sbf[1][:, g, 1:NC1],
                                    start=False, stop=(e_ == 3)), reads=[self.WL, sbd[1][g]], writes=[self.bdep[bk]])
                        k.op("act", lambda e, bk=bk, tp=tp, tile_=tile_, pr=pr: e.activation(
                            gF[pr, tile_, :].rearrange("p (j t) -> p t j", t=8)[:, tp * 2:tp * 2 + 2, :],
                            self.bank(bk)[pr, 0:2 * NJ].rearrange("p (a b) -> p a b", a=2), AF.Gelu_apprx_tanh),
                            reads=[self.bdep[bk]], writes=[gF])
                if l == 0 and si == 0 and q == 0:
                    self.dbg("Sf", Sbf[0], [128, 16, NCP], BF16, sbd[0]); self.dbg("Sb", Sbf[1], [128, 16, NCP], BF16, sbd[1])
                    self.dbg("gF", gF, [128, 2, SEG]); self.dbg("uT", u, [128, 2, SEG], BF16)
                k.op("pool", lambda e: e.tensor_copy(gB[:], gF[:]), reads=[gF], writes=[gB])
                GB = min(512, SEG)
                for b in range(SEG // GB):
                    ts = slice(b * GB, (b + 1) * GB)
                    sob = so[b % 2]
                    for cc in range(2):
                        bk = self.nextbank()
                        for kc in range(2):
                            k.op("pe", lambda e, bk=bk, cc=cc, kc=kc, ts=ts: e.matmul(
                                self.bank(bk)[:, 0:GB], lhsT=self.wglu[:, kc, cc * 128:(cc + 1) * 128], rhs=gB[:, kc, ts],
                                start=(kc == 0), stop=(kc == 1)), reads=[self.wglu, gB], writes=[self.bdep[bk]])
                        sgt = sg[cc]
                        k.op("act", lambda e, bk=bk, sgt=sgt: e.activation(sgt[:, 0:GB], self.bank(bk)[:, 0:GB], AF.Sigmoid), reads=[self.bdep[bk]], writes=[sgt])
                        k.op("dve", lambda e, cc=cc, ts=ts, sgt=sgt, sob=sob: e.tensor_tensor(sob[:, cc, 0:GB], gF[:, cc, ts], sgt[:, 0:GB], ALU.mult),
                             reads=[gF, sgt], writes=[sob])
                    t0 = q * SEG + b * GB
                    k.dma("sp", self.dr[f"s5T{si}"].ap().rearrange("(c p) t -> p c t", p=128)[:, :, t0:t0 + GB], sob[:, :, 0:GB], reads=[sob], owner=sob)
            k.flush()

    def fourier(self, l, si, st):
        k = self.k
        s = self.seqs[si]
        L, N1, N2 = s["L"], s["N1"], s["N2"]
        FA = k.sb("FA", [N1, 2 * N1], BF16, st)
        FC1 = k.sb("FC1", [N2, 2 * N2], BF16, st)
        FC2 = k.sb("FC2", [N2, 2 * N2], BF16, st)
        TW = k.sb("TW", [N2, 2, N1], F32, st)
        k.dma("pool", FA[:], self.dr[f"c{si}_FA"].ap(), writes=[FA], owner=FA)
        k.dma("pool", FC1[:], self.dr[f"c{si}_FC1"].ap(), writes=[FC1], owner=FC1)
        k.dma("pool", FC2[:], self.dr[f"c{si}_FC2"].ap(), writes=[FC2], owner=FC2)
        k.dma("sp", TW[:], self.dr[f"c{si}_TW"].ap(), writes=[TW], owner=TW)
        xg = [k.sb(f"xg{i}", [N1, N2, 64], BF16, st) for i in range(2)]
        AR = k.sb("AR", [N2, N1, 64], BF16, st)
        AI = k.sb("AI", [N2, N1, 64], BF16, st)
        XT = k.sb("XT", [64, 2, L], BF16, st)
        nbA = 512 // (2 * N1)
        tmp = [k.sb(f"ftmp{i}", [N2, nbA, N1], F32, st) for i in range(4)]
        nbC = 512 // (2 * N2)
        hfd = self.dr[f"hf{si}"].ap()
        for g in range(4):
            x = xg[g % 2]
            k.dma("pool", x[:], hfd[g].rearrange("(a b) c -> a b c", a=N1), writes=[x], owner=x)
            for c0 in range(0, 64, nbA):
                bk = self.nextbank()
                for j in range(nbA):
                    ch = c0 + j
                    k.op("pe", lambda e, bk=bk, j=j, ch=ch, x=x: e.matmul(
                        self.bank(bk)[0:N2, j * 2 * N1:(j + 1) * 2 * N1], lhsT=x[:, :, ch], rhs=FA[:], start=True, stop=True),
                        reads=[x, FA], writes=[self.bdep[bk]])
                pv = self.bank(bk)[0:N2, 0:nbA * 2 * N1].rearrange("p (c r k) -> p c r k", c=nbA, r=2)
                Ar, Ai = pv[:, :, 0, :], pv[:, :, 1, :]
                twr = TW[:, 0, :].unsqueeze(1).broadcast_to([N2, nbA, N1])
                twi = TW[:, 1, :].unsqueeze(1).broadcast_to([N2, nbA, N1])
                bd = self.bdep[bk]
                k.op("dve", lambda e, Ar=Ar, twr=twr: e.tensor_tensor(tmp[0][:], Ar, twr, ALU.mult), reads=[bd, TW], writes=[tmp[0]])
                k.op("dve", lambda e, Ai=Ai, twi=twi: e.tensor_tensor(tmp[1][:], Ai, twi, ALU.mult), reads=[bd, TW], writes=[tmp[1]])
                k.op("dve", lambda e, Ar=Ar, twi=twi: e.tensor_tensor(tmp[2][:], Ar, twi, ALU.mult), reads=[bd, TW], writes=[tmp[2]])
                k.op("dve", lambda e, Ai=Ai, twr=twr: e.tensor_tensor(tmp[3][:], Ai, twr, ALU.mult), reads=[bd, TW], writes=[tmp[3]])
                k.op("pool", lambda e, c0=c0: e.tensor_tensor(AR[:, :, c0:c0 + nbA].rearrange("p k c -> p c k"), tmp[0][:], tmp[1][:], ALU.subtract),
                     reads=[tmp[0], tmp[1]], writes=[AR])
                k.op("pool", lambda e, c0=c0: e.tensor_tensor(AI[:, :, c0:c0 + nbA].rearrange("p k c -> p c k"), tmp[2][:], tmp[3][:], ALU.add),
                     reads=[tmp[2], tmp[3]], writes=[AI])
            for k10 in range(0, N1, nbC):
                bk = self.nextbank()
                for j in range(nbC):
                    k1 = k10 + j
                    oc = slice(j * 2 * N2, (j + 1) * 2 * N2)
                    k.op("pe", lambda e, bk=bk, oc=oc, k1=k1: e.matmul(self.bank(bk)[0:64, oc], lhsT=AR[:, k1, :], rhs=FC1[:], start=True, stop=False),
                         reads=[AR, FC1], writes=[self.bdep[bk]])
                    k.op("pe", lambda e, bk=bk, oc=oc, k1=k1: e.matmul(self.bank(bk)[0:64, oc], lhsT=AI[:, k1, :], rhs=FC2[:], start=False, stop=True),
                         reads=[AI, FC2], writes=[self.bdep[bk]])
                pv = self.bank(bk)[0:64, 0:nbC * 2 * N2].rearrange("p (i r k) -> p r k i", i=nbC, r=2)
                for r in range(2):
                    eng = "act" if r == 0 else "dve"
                    outv = XT[:, r, :].rearrange("p (k2 k1) -> p k2 k1", k1=N1)[:, :, k10:k10 + nbC]
                    if eng == "act":
                        k.op("act", lambda e, outv=outv, pv=pv, r=r: e.activation(outv, pv[:, r, :, :], AF.Copy), reads=[self.bdep[bk]], writes=[XT])
                    else:
                        k.op("dve", lambda e, outv=outv, pv=pv, r=r: e.tensor_copy(outv, pv[:, r, :, :]), reads=[self.bdep[bk]], writes=[XT])
            for r in range(2):
                k.dma("sp", self.dr[f"fT{si}"].ap()[r, g * 64:(g + 1) * 64, :], XT[:, r, :], reads=[XT], owner=XT)

    def mla_pre(self, l, si, st):
        k = self.k
        L = self.seqs[si]["L"]
        wq = k.sb("wq", [128, 2, 768], BF16, st)
        wqs = k.sb("wqs", [128, 2, 768], BF16, st)
        wk = k.sb("wk", [128, 512], BF16, st)
        wv = k.sb("wv", [128, 512], BF16, st)
        vec = k.sb("vec", [128, 59], F32, st)
        k.dma("sp", wq[:], self.dr["b_w_uq"].ap()[l].rearrange("(kc p) n -> p kc n", p=128), writes=[wq], owner=wq)
        k.dma("sp", wqs[:], self.dr["b_w_uq_sw"].ap()[l].rearrange("(kc p) n -> p kc n", p=128), writes=[wqs], owner=wqs)
        k.dma("sp", wk[:], self.dr["b_w_uk"].ap()[l], writes=[wk], owner=wk)
        k.dma("sp", wv[:], self.dr["b_w_uv"].ap()[l], writes=[wv], owner=wv)
        k.dma("sp", vec[:], self.dr["vecs"].ap()[l], writes=[vec], owner=vec)
        cq = [k.sb(f"cq{i}", [128, 2, 512], F32, st) for i in range(2)]
        ckv = [k.sb(f"ckv{i}", [128, 512], F32, st) for i in range(2)]
        kpa = [k.sb(f"kpa{i}", [96, 512], F32, st) for i in range(2)]
        kpb = [k.sb(f"kpb{i}", [96, 512], F32, st) for i in range(2)]
        rope = [k.sb(f"rope{i}", [96, 2, 2, 512], F32, st) for i in range(2)]
        sq = k.sb("sq", [128, 3, 512], F32, st)
        rs = k.sb("rs", [128, 2, 512], F32, st)
        cqn = k.sb("cqn", [128, 2, 512], BF16, st)
        ckvn = k.sb("ckvn", [128, 512], BF16, st)
        rt = k.sb("rt", [96, 3, 512], F32, st)
        Qst = [k.sb(f"Qst{i}", [96, H, 512], BF16, st) for i in range(2)]
        Kst = [k.sb(f"Kst{i}", [96, H, 512], BF16, st) for i in range(2)]
        Vst = [k.sb(f"Vst{i}", [128, 4, 512], BF16, st) for i in range(2)]
        cqd = self.dr[f"cqT{si}"].ap().rearrange("(c p) t -> p c t", p=128)
        rp = self.dr[f"c{si}_ROPE"].ap()
        for b in range(L // 512):
            i = b % 2
            ts = slice(b * 512, (b + 1) * 512)
            k.dma("sp", cq[i][:], cqd[:, :, ts], writes=[cq[i]], owner=cq[i])
            k.dma("sp", ckv[i][:], self.dr[f"ckvT{si}"].ap()[:, ts], writes=[ckv[i]], owner=ckv[i])
            k.dma("sp", kpa[i][64:96, :], self.dr[f"kpeT{si}"].ap()[0:32, ts], writes=[kpa[i]], owner=kpa[i])
            k.dma("sp", kpb[i][64:96, :], self.dr[f"kpeT{si}"].ap()[32:64, ts], writes=[kpb[i]], owner=kpb[i])
            k.dma("sp", rope[i][64:96], rp[:, :, :, ts].rearrange("a b r t -> r a b t"), writes=[rope[i]], owner=rope[i])
            k.op("act", lambda e, i=i: e.activation(sq[:, 0:2, :], cq[i][:], AF.Square), reads=[cq[i]], writes=[sq])
            k.op("act", lambda e, i=i: e.activation(sq[:, 2, :], ckv[i][:], AF.Square), reads=[ckv[i]], writes=[sq])
            bq = self.nextbank()
            for kc in range(2):
                k.op("pe", lambda e, bq=bq, kc=kc: e.matmul(self.bank(bq), lhsT=self.ones[:], rhs=sq[:, kc, :], start=(kc == 0), stop=(kc == 1)),
                     reads=[self.ones, sq], writes=[self.bdep[bq]])
            bkv = self.nextbank()
            k.op("pe", lambda e, bkv=bkv: e.matmul(self.bank(bkv), lhsT=self.ones[:], rhs=sq[:, 2, :], start=True, stop=True),
                 reads=[self.ones, sq], writes=[self.bdep[bkv]])
            k.op("act", lambda e, bq=bq: e.activation(rs[:, 0, :], self.bank(bq), AF.Sqrt, scale=1.0 / 256, bias=EPS), reads=[self.bdep[bq]], writes=[rs])
            k.op("act", lambda e, bkv=bkv: e.activation(rs[:, 1, :], self.bank(bkv), AF.Sqrt, scale=1.0 / 128, bias=EPS), reads=[self.bdep[bkv]], writes=[rs])
            k.op("dve", lambda e: e.reciprocal(rs[:], rs[:]), reads=[rs], writes=[rs])
            for kc in range(2):
                k.op("dve", lambda e, kc=kc, i=i: e.scalar_tensor_tensor(cqn[:, kc, :], cq[i][:, kc, :], vec[:, 56 + kc:57 + kc], rs[:, 0, :], ALU.mult, ALU.mult),
                     reads=[cq[i], vec, rs], writes=[cqn])
            k.op("dve", lambda e, i=i: e.scalar_tensor_tensor(ckvn[:], ckv[i][:], vec[:, 58:59], rs[:, 1, :], ALU.mult, ALU.mult),
                 reads=[ckv[i], vec, rs], writes=[ckvn])
            R = rope[i]
            k.op("pool", lambda e, i=i, R=R: e.tensor_tensor(rt[64:96, 0, :], kpa[i][64:96, :], R[64:96, 1, 0, :], ALU.mult), reads=[kpa[i], R], writes=[rt])
            k.op("pool", lambda e, i=i, R=R: e.tensor_tensor(rt[64:96, 1, :], kpb[i][64:96, :], R[64:96, 1, 1, :], ALU.mult), reads=[kpb[i], R], writes=[rt])
            k.op("pool", lambda e: e.tensor_tensor(rt[64:96, 2, :], rt[64:96, 0, :], rt[64:96, 1, :], ALU.add), reads=[rt], writes=[rt])
            Q, K_, V = Qst[i], Kst[i], Vst[i]
            k.op("pool", lambda e, K_=K_: e.tensor_copy(K_[64:96, :, :], rt[64:96, 2, :].unsqueeze(1).broadcast_to([32, H, 512])), reads=[rt], writes=[K_])
            for h in range(H):
                ba = self.nextbank()
                bb = self.nextbank()
                for kc in range(2):
                    k.op("pe", lambda e, ba=ba, kc=kc, h=h: e.matmul(self.bank(ba)[0:96, :], lhsT=wq[:, kc, h * 96:(h + 1) * 96], rhs=cqn[:, kc, :],
                                                                    start=(kc == 0), stop=(kc == 1)), reads=[wq, cqn], writes=[self.bdep[ba]])
                for kc in range(2):
                    k.op("pe", lambda e, bb=bb, kc=kc, h=h: e.matmul(self.bank(bb)[0:96, :], lhsT=wqs[:, kc, h * 96:(h + 1) * 96], rhs=cqn[:, kc, :],
                                                                    start=(kc == 0), stop=(kc == 1)), reads=[wqs, cqn], writes=[self.bdep[bb]])
                k.op("act", lambda e, ba=ba, h=h, Q=Q: e.activation(Q[0:64, h, :], self.bank(ba)[0:64, :], AF.Copy, scale=96 ** -0.5),
                     reads=[self.bdep[ba]], writes=[Q])
                k.op("dve", lambda e, ba=ba, R=R: e.tensor_tensor(rt[64:96, 0, :], self.bank(ba)[64:96, :], R[64:96, 0, 0, :], ALU.mult),
                     reads=[self.bdep[ba], R], writes=[rt])
                k.op("dve", lambda e, bb=bb, R=R: e.tensor_tensor(rt[64:96, 1, :], self.bank(bb)[64:96, :], R[64:96, 0, 1, :], ALU.mult),
                     reads=[self.bdep[bb], R], writes=[rt])
                k.op("dve", lambda e, h=h, Q=Q: e.tensor_tensor(Q[64:96, h, :], rt[64:96, 0, :], rt[64:96, 1, :], ALU.add), reads=[rt], writes=[Q])
                bk = self.nextbank()
                k.op("pe", lambda e, bk=bk, h=h: e.matmul(self.bank(bk)[0:64, :], lhsT=wk[:, h * 64:(h + 1) * 64], rhs=ckvn[:], start=True, stop=True),
                     reads=[wk, ckvn], writes=[self.bdep[bk]])
                k.op("act", lambda e, bk=bk, h=h, K_=K_: e.activation(K_[0:64, h, :], self.bank(bk)[0:64, :], AF.Copy), reads=[self.bdep[bk]], writes=[K_])
            for sub in range(4):
                bk = self.nextbank()
                k.op("pe", lambda e, bk=bk, sub=sub: e.matmul(self.bank(bk), lhsT=ckvn[:, sub * 128:(sub + 1) * 128], rhs=wv[:], start=True, stop=True),
                     reads=[wv, ckvn], writes=[self.bdep[bk]])
                k.op("act", lambda e, bk=bk, sub=sub, V=V: e.activation(V[:, sub, :], self.bank(bk), AF.Copy), reads=[self.bdep[bk]], writes=[V])
            k.dma("sp", self.dr[f"qT{si}"].ap()[:, :, ts].rearrange("h r t -> r h t"), Q[:], reads=[Q], owner=Q)
            k.dma("sp", self.dr[f"kT{si}"].ap()[:, :, ts].rearrange("h r t -> r h t"), K_[:], reads=[K_], owner=K_)
            k.dma("sp", self.dr[f"v{si}"].ap()[ts, :].rearrange("(s p) c -> p s c", p=128), V[:], reads=[V], owner=V)

    def attn(self, l, si, st):
        k = self.k
        L = self.seqs[si]["L"]
        split = self.seqs[si].get("split", False)
        Lq = L // 4 if split else L
        QB = min(1024, Lq)
        NKT = L // 128
        NH = QB // 512
        KT = [k.sb(f"KT{i}", [96, L], BF16, st) for i in range(2)]
        VT = [k.sb(f"VT{i}", [128, NKT, 128], BF16, st) for i in range(2)]
        Qb = [k.sb(f"Qb{i}", [96, QB], BF16, st) for i in range(2)]
        Pt = [k.sb(f"Pt{i}", [128, QB], BF16, st) for i in range(3)]
        rden = k.sb("rden", [128, QB], F32, st)
        ost = [k.sb(f"ost{i}", [64, QB], BF16, st) for i in range(2)]
        for i in range(2):
            k.op("pool", lambda e, i=i: e.memset(VT[i][:, :, 64:128], 1.0), writes=[VT[i]])
        sdep = [Dep("S0"), Dep("S1")]
        odep = [Dep("O0"), Dep("O1")]
        qd = self.dr[f"qT{si}"].ap()
        kd = self.dr[f"kT{si}"].ap()
        vd = self.dr[f"v{si}"].ap()
        ad = self.dr[f"attnq{si}"].ap() if split else self.dr[f"attnT{si}"].ap()
        cnt = 0
        pcnt = 0
        if split:
            qmd = Dep("qm")
            k.dma("sp", self.dr[f"qm{si}"].ap(), lambda e, qsrc=qd: qsrc[:, :, bass.ds(k.rbase(e, "sp", Lq), Lq)], writes=[qmd], owner=qmd)
            qd = self.dr[f"qm{si}"].ap()
        for h in range(H):
            Kt, Vt = KT[h % 2], VT[h % 2]
            k.dma("sp", Kt[:], kd[h], writes=[Kt], owner=Kt)
            vsrc = vd[:, h * 64:(h + 1) * 64].rearrange("(t p) c -> p t c", p=128)
            nsp = max(1, NKT // 32)
            for sp_ in range(nsp):
                tsl = slice(sp_ * NKT // nsp, (sp_ + 1) * NKT // nsp)
                k.dma("pool", Vt[:, tsl, 0:64], vsrc[:, tsl, :], writes=[Vt], owner=Vt)
            for qb in range(Lq // QB):
                Q = Qb[cnt % 2]
                O = self.pst[2 + cnt % 2]
                od = odep[cnt % 2]
                osb = ost[cnt % 2]
                cnt += 1
                k.dma("sp", Q[:], qd[h, :, qb * QB:(qb + 1) * QB], reads=([qmd] if split else []), writes=[Q], owner=Q)

                def smm(kt, Q=Q, Kt=Kt):
                    S = self.pst[kt % 2]
                    for hf in range(NH):
                        k.op("pe", lambda e, S=S, hf=hf, kt=kt: e.matmul(S[:, hf * 512:(hf + 1) * 512], lhsT=Kt[:, kt * 128:(kt + 1) * 128],
                                                                          rhs=Q[:, hf * 512:(hf + 1) * 512], start=True, stop=True),
                             reads=[Kt, Q], writes=[sdep[kt % 2]])
                smm(0)
                for kt in range(NKT):
                    if kt + 1 < NKT:
                        smm(kt + 1)
                    S = self.pst[kt % 2]
                    P = Pt[pcnt % 3]
                    pcnt += 1
                    k.op("act", lambda e, S=S, P=P: e.activation(P[:], S[:, 0:QB], AF.Exp), reads=[sdep[kt % 2]], writes=[P])
                    for hf in range(NH):
                        k.op("pe", lambda e, O=O, hf=hf, kt=kt, P=P, Vt=Vt: e.matmul(O[:, hf * 512:(hf + 1) * 512], lhsT=Vt[:, kt, :],
                                                                                    rhs=P[:, hf * 512:(hf + 1) * 512], start=(kt == 0), stop=(kt == NKT - 1)),
                             reads=[Vt, P], writes=[od])
                k.op("dve", lambda e, O=O: e.reciprocal(rden[64:128, :], O[64:128, 0:QB]), reads=[od], writes=[rden])
                k.op("dve", lambda e, O=O, osb=osb: e.tensor_tensor(osb[:], O[0:64, 0:QB], rden[64:128, :], ALU.mult), reads=[od, rden], writes=[osb])
                k.dma("sp", ad[h * 64:(h + 1) * 64, qb * QB:(qb + 1) * QB], osb[:], reads=[osb], owner=osb)

    def p3_setup(self, l, st):
        k = self.k
        self.vec3 = k.sb("vec3", [128, 59], F32, st)
        k.dma("sp", self.vec3[:], self.dr["vecs"].ap()[l], writes=[self.vec3], owner=self.vec3)
        self.cs64 = k.sb("cs64", [128, 2, 128], BF16, st)
        k.dma("pool", self.cs64[:], self.dr["c_CS64"].ap(), writes=[self.cs64], owner=self.cs64)
        self.wring = [k.sb(f"wr{i}", [128, 4096], BF16, st) for i in range(4)]
        self.wri = 0
        self.p3x = [k.sb(f"p3x{i}", [128, 8, 512], F32, st) for i in range(2)]
        self.p3in = [k.sb(f"p3in{i}", [128, 12, 512], BF16, st) for i in range(2)]
        self.gs = [k.sb(f"gs{i}", [128, 512], F32, st) for i in range(6)]
        self.mt = [k.sb(f"mt{i}", [128, 512], F32, st) for i in range(3)]
        self.mg = k.sb("mg", [128, 8, 512], BF16, st)
        self.z = k.sb("z", [128, 8, 512], F32, st)
        self.x1 = k.sb("x1", [128, 8, 512], F32, st)
        self.x1b = k.sb("x1b", [128, 8, 512], BF16, st)
        self.xbf = k.sb("xbf", [128, 8, 512], BF16, st)
        self.hb = k.sb("hb", [128, NFF, 512], BF16, st)
        self.lsq = [k.sb(f"lsq{i}", [128, 512], F32, st) for i in range(2)]
        self.lst = k.sb("lst", [128, 4, 512], F32, st)
        self.p3i = 0

    def wload(self, src_ap, shape_str, **kw):
        k = self.k
        w = self.wring[self.wri % 4]
        self.wri += 1
        n = 1
        for d_ in src_ap.shape[1:]:
            n *= d_
        view = w[:, 0:n].rearrange(shape_str, **kw) if shape_str else w[:, 0:n]
        k.dma("sp", view, src_ap, writes=[w], owner=w)
        return w, view

    def layernorm(self, zin, gcol, bcol, out32, out16):
        k = self.k
        vec = self.vec3
        b1 = self.nextbank()
        for kc in range(8):
            k.op("pe", lambda e, kc=kc: e.matmul(self.bank(b1), lhsT=self.ones[:], rhs=zin[:, kc, :], start=(kc == 0), stop=(kc == 7)),
                 reads=[self.ones, zin], writes=[self.bdep[b1]])
        b2 = self.nextbank()
        for kc in range(8):
            sq = self.lsq[kc % 2]
            k.op("act", lambda e, kc=kc, sq=sq: e.activation(sq[:], zin[:, kc, :], AF.Square), reads=[zin], writes=[sq])
            k.op("pe", lambda e, kc=kc, sq=sq: e.matmul(self.bank(b2), lhsT=self.ones[:], rhs=sq[:], start=(kc == 0), stop=(kc == 7)),
                 reads=[self.ones, sq], writes=[self.bdep[b2]])
        lst = self.lst
        mean, msq, var, rstd = lst[:, 0, :], lst[:, 1, :], lst[:, 2, :], lst[:, 3, :]
        k.op("dve", lambda e: e.tensor_scalar(mean, self.bank(b1), 1.0 / D, None, ALU.mult), reads=[self.bdep[b1]], writes=[lst])
        k.op("dve", lambda e: e.tensor_tensor(msq, mean, mean, ALU.mult), reads=[lst], writes=[lst])
        k.op("dve", lambda e: e.scalar_tensor_tensor(var, self.bank(b2), 1.0 / D, msq, ALU.mult, ALU.subtract), reads=[self.bdep[b2], lst], writes=[lst])
        k.op("act", lambda e: e.activation(rstd, var, AF.Sqrt, bias=EPS), reads=[lst], writes=[lst])
        k.op("dve", lambda e: e.reciprocal(rstd, rstd), reads=[lst], writes=[lst])
        for kc in range(8):
            o32 = out32[:, kc, :]
            k.op("dve", lambda e, kc=kc, o32=o32: e.tensor_tensor(o32, zin[:, kc, :], mean, ALU.subtract), reads=[zin, lst], writes=[out32])
            k.op("dve", lambda e, kc=kc, o32=o32: e.tensor_tensor(o32, o32, rstd, ALU.mult), reads=[out32, lst], writes=[out32])
            k.op("dve", lambda e, kc=kc, o32=o32: e.tensor_scalar(o32, o32, vec[:, gcol + kc:gcol + kc + 1], vec[:, bcol + kc:bcol + kc + 1], ALU.mult, ALU.add),
                 reads=[out32, vec], writes=[out32])
            if out16 is not None:
                k.op("pool", lambda e, kc=kc, o32=o32: e.tensor_copy(out16[:, kc, :], o32), reads=[out32], writes=[out16])

    def p3(self, l, si, st):
        k = self.k
        L = self.seqs[si]["L"]
        if not self.seqs[si].get("split"):
            xs = self.xsrc(l, si).ap().rearrange("(kc p) t -> p kc t", p=128)
            xd = self.xdst(l, si).ap().rearrange("(kc p) t -> p kc t", p=128)
        fT = self.dr[f"fT{si}"].ap()
        vec = self.vec3
        wb = lambda n: self.dr["b_" + n].ap()[l]
        fscale = 1.0 / math.sqrt(L * 64.0)
        split = self.seqs[si].get("split", False)
        Lq = L // 4 if split else L
        if split:
            last = (l == self.depth - 1)
            xs = self.dr[f"xT{si}"].ap().rearrange("(kc p) t -> p kc t", p=128)
            xqv = self.dr[f"xq{si}"].ap().rearrange("c (kc p) t -> c p kc t", p=128)
            xd = self.dr[f"yq{si}"].ap().rearrange("(kc p) t -> p kc t", p=128) if last else None
        s5d = self.dr[f"s5T{si}"].ap().rearrange("(c p) t -> p c t", p=128)
        for b in range(Lq // 512):
            i = self.p3i % 2
            self.p3i += 1
            ts = slice(b * 512, (b + 1) * 512)
            x, pin = self.p3x[i], self.p3in[i]
            xbf = self.xbf
            if split:
                if b == 0:
                    md = Dep("mine")
                    dynq = lambda e: bass.ds(k.rbase(e, "sp", Lq), Lq)
                    k.dma("sp", self.dr[f"fm{si}"].ap(), lambda e: fT[:, :, dynq(e)], writes=[md], owner=md)
                    k.dma("sp", self.dr[f"s5m{si}"].ap(), lambda e: self.dr[f"s5T{si}"].ap()[:, dynq(e)], writes=[md], owner=md)
                    if l == 0:
                        k.dma("sp", self.dr[f"xm{si}"].ap(), lambda e: self.dr[f"xT{si}"].ap()[:, dynq(e)], writes=[md], owner=md)
                    self._md = md
                md = self._md
                if l == 0:
                    k.dma("sp", x[:], self.dr[f"xm{si}"].ap().rearrange("(kc p) t -> p kc t", p=128)[:, :, ts], reads=[md], writes=[x], owner=x)
                else:
                    for h_ in range(2):
                        k.dma("sp", x[:, :, h_ * 256:(h_ + 1) * 256], xqv[2 * b + h_], writes=[x], owner=x)
                k.dma("sp", pin[:, 0:4, :], self.dr[f"fm{si}"].ap()[:, :, ts].rearrange("r (c p) t -> p (r c) t", p=128), reads=[md], writes=[pin], owner=pin)
                k.dma("sp", pin[:, 4:6, :], self.dr[f"s5m{si}"].ap().rearrange("(c p) t -> p c t", p=128)[:, :, ts], reads=[md], writes=[pin], owner=pin)
                k.dma("sp", pin[:, 6:10, :], self.dr[f"attnq{si}"].ap().rearrange("(c p) t -> p c t", p=128)[:, :, ts], writes=[pin], owner=pin)
            else:
                k.dma("pool", x[:], xs[:, :, ts], writes=[x], owner=x)
                k.dma("pool", pin[:, 0:4, :], fT[:, :, ts].rearrange("r (c p) t -> p (r c) t", p=128), writes=[pin], owner=pin)
                k.dma("pool", pin[:, 4:6, :], s5d[:, :, ts], writes=[pin], owner=pin)
                k.dma("pool", pin[:, 6:10, :], self.dr[f"attnT{si}"].ap().rearrange("(c p) t -> p c t", p=128)[:, :, ts], writes=[pin], owner=pin)
            k.op("act", lambda e, x=x: e.activation(xbf[:, 0:4, :], x[:, 0:4, :], AF.Copy), reads=[x], writes=[xbf])
            k.op("pool", lambda e, x=x: e.tensor_copy(xbf[:, 4:8, :], x[:, 4:8, :]), reads=[x], writes=[xbf])
            for cc in range(2):
                bk = self.nextbank()
                k.op("pe", lambda e, bk=bk, cc=cc, pin=pin: e.matmul(self.bank(bk), lhsT=self.cs64[:, 0, :], rhs=pin[:, cc, :], start=True, stop=False),
                     reads=[self.cs64, pin], writes=[self.bdep[bk]])
                k.op("pe", lambda e, bk=bk, cc=cc, pin=pin: e.matmul(self.bank(bk), lhsT=self.cs64[:, 1, :], rhs=pin[:, 2 + cc, :], start=False, stop=True),
                     reads=[self.cs64, pin], writes=[self.bdep[bk]])
                k.op("act", lambda e, bk=bk, cc=cc, pin=pin: e.activation(pin[:, 10 + cc, :], self.bank(bk), AF.Copy, scale=fscale),
                     reads=[self.bdep[bk]], writes=[pin])
            brk = [(0, (10, 11)), (2, (4, 5)), (4, (6, 7, 8, 9))]
            for G in range(2):
                gw = []
                for br in range(3):
                    c0 = 928 + br * 1024 + G * 512
                    gw.append(self.wload(wb("w_in").rearrange("(kc p) n -> p kc n", p=128)[:, :, c0:c0 + 512], "p (kc n) -> p kc n", kc=8))
                bw = self.wring[self.wri % 4]
                self.wri += 1
                bwv = bw[:, 0:4096].rearrange("p (kc n) -> p kc n", kc=8)
                cs_ = slice(G * 512, (G + 1) * 512)
                k.dma("sp", bwv[:, 0:2, :], wb("w_br_f").rearrange("(kc p) n -> p kc n", p=128)[:, :, cs_], writes=[bw], owner=bw)
                k.dma("sp", bwv[:, 2:4, :], wb("w_br_s").rearrange("(kc p) n -> p kc n", p=128)[:, :, cs_], writes=[bw], owner=bw)
                k.dma("sp", bwv[:, 4:8, :], wb("w_br_a").rearrange("(kc p) n -> p kc n", p=128)[:, :, cs_], writes=[bw], owner=bw)
                for c4 in range(4):
                    cc = G * 4 + c4
                    ws = slice(c4 * 128, (c4 + 1) * 128)
                    gsl = []
                    for br in range(3):
                        bk = self.nextbank()
                        wt, wv_ = gw[br]
                        for kc in range(8):
                            k.op("pe", lambda e, bk=bk, kc=kc, wv_=wv_, ws=ws: e.matmul(self.bank(bk), lhsT=wv_[:, kc, ws], rhs=xbf[:, kc, :],
                                                                                      start=(kc == 0), stop=(kc == 7)),
                                 reads=[wt, xbf], writes=[self.bdep[bk]])
                        g = self.gs[(cc * 3 + br) % 6]
                        col = 32 + br * 8 + cc
                        k.op("act", lambda e, bk=bk, g=g, col=col: e.activation(g[:], self.bank(bk), AF.Sigmoid, bias=vec[:, col:col + 1]),
                             reads=[self.bdep[bk], vec], writes=[g])
                        gsl.append(g)
                    for br in range(3):
                        bk = self.nextbank()
                        k0, srcs = brk[br]
                        n = len(srcs)
                        for j, sidx in enumerate(srcs):
                            k.op("pe", lambda e, bk=bk, j=j, sidx=sidx, k0=k0, n=n, ws=ws, bwv=bwv, pin=pin: e.matmul(
                                self.bank(bk), lhsT=bwv[:, k0 + j, ws], rhs=pin[:, sidx, :], start=(j == 0), stop=(j == n - 1)),
                                reads=[bw, pin], writes=[self.bdep[bk]])
                        m = self.mt[br]
                        g = gsl[br]
                        k.op("dve", lambda e, bk=bk, m=m, g=g: e.tensor_tensor(m[:], self.bank(bk), g[:], ALU.mult), reads=[self.bdep[bk], g], writes=[m])
                    k.op("pool", lambda e: e.tensor_tensor(self.mt[0][:], self.mt[0][:], self.mt[1][:], ALU.add), reads=[self.mt[0], self.mt[1]], writes=[self.mt[0]])
                    k.op("pool", lambda e, cc=cc: e.tensor_tensor(self.mg[:, cc, :], self.mt[0][:], self.mt[2][:], ALU.add), reads=[self.mt[0], self.mt[2]], writes=[self.mg])
            if split and not last:
                self._p3_rest(l, si, b, ts, x, [xqv[2 * b], xqv[2 * b + 1]])
            else:
                self._p3_rest(l, si, b, ts, x, xd)

    def _p3_rest(self, l, si, b, ts, x, xd):
        k = self.k
        vec = self.vec3
        wb = lambda n: self.dr["b_" + n].ap()[l]
        for G in range(2):
            wt, wv_ = self.wload(wb("w_o").rearrange("(kc p) n -> p kc n", p=128)[:, :, G * 512:(G + 1) * 512], "p (kc n) -> p kc n", kc=8)
            for c4 in range(4):
                cc = G * 4 + c4
                bk = self.nextbank()
                for kc in range(8):
                    k.op("pe", lambda e, bk=bk, kc=kc, wv_=wv_, c4=c4: e.matmul(self.bank(bk), lhsT=wv_[:, kc, c4 * 128:(c4 + 1) * 128], rhs=self.mg[:, kc, :],
                                                                              start=(kc == 0), stop=(kc == 7)), reads=[wt, self.mg], writes=[self.bdep[bk]])
                k.op("dve", lambda e, bk=bk, cc=cc: e.scalar_tensor_tensor(self.z[:, cc, :], x[:, cc, :], ALPHA, self.bank(bk), ALU.mult, ALU.add),
                     reads=[x, self.bdep[bk]], writes=[self.z])
        self.layernorm(self.z, 0, 8, self.x1, self.x1b)
        wgu = wb("w_gu").rearrange("(kc p) (two f) -> p kc two f", p=128, two=2)
        for f0 in range(0, NFF, 2):
            wt = self.wring[self.wri % 4]
            self.wri += 1
            wv_ = wt[:, 0:4096].rearrange("p (kc two f) -> p kc two f", kc=8, two=2)
            for two in range(2):
                k.dma("sp", wv_[:, :, two, :], wgu[:, :, two, f0 * 128:(f0 + 2) * 128], writes=[wt], owner=wt)
            for j in range(2):
                f = f0 + j
                bg = self.nextbank()
                bu = self.nextbank()
                for two, bk in ((0, bg), (1, bu)):
                    for kc in range(8):
                        k.op("pe", lambda e, bk=bk, kc=kc, wv_=wv_, two=two, j=j: e.matmul(self.bank(bk), lhsT=wv_[:, kc, two, j * 128:(j + 1) * 128],
                                                                                         rhs=self.x1b[:, kc, :], start=(kc == 0), stop=(kc == 7)),
                             reads=[wt, self.x1b], writes=[self.bdep[bk]])
                sgt = self.gs[f % 6]
                k.op("act", lambda e, bg=bg, sgt=sgt: e.activation(sgt[:], self.bank(bg), AF.Silu), reads=[self.bdep[bg]], writes=[sgt])
                k.op("dve", lambda e, bu=bu, sgt=sgt, f=f: e.tensor_tensor(self.hb[:, f, :], self.bank(bu), sgt[:], ALU.mult),
                     reads=[self.bdep[bu], sgt], writes=[self.hb])
        wdn = wb("w_down").rearrange("(kc p) n -> p kc n", p=128)
        for cc in range(8):
            wt, wv_ = self.wload(wdn[:, :, cc * 128:(cc + 1) * 128], "p (kc n) -> p kc n", kc=NFF)
            bk = self.nextbank()
            for kc in range(NFF):
                k.op("pe", lambda e, bk=bk, kc=kc, wv_=wv_: e.matmul(self.bank(bk), lhsT=wv_[:, kc, :], rhs=self.hb[:, kc, :],
                                                                    start=(kc == 0), stop=(kc == NFF - 1)), reads=[wt, self.hb], writes=[self.bdep[bk]])
            k.op("dve", lambda e, bk=bk, cc=cc: e.scalar_tensor_tensor(self.z[:, cc, :], self.x1[:, cc, :], ALPHA, self.bank(bk), ALU.mult, ALU.add),
                 reads=[self.x1, self.bdep[bk]], writes=[self.z])
        self.layernorm(self.z, 16, 24, self.x1, None)
        if isinstance(xd, list):
            for h_ in range(2):
                k.dma("pool", xd[h_], self.x1[:, :, h_ * 256:(h_ + 1) * 256], reads=[self.x1], owner=self.x1)
        else:
            k.dma("pool", xd[:, :, ts], self.x1[:], reads=[self.x1], owner=self.x1)


FULL_CFG = dict(depth=4, SEG=2048, seqs=[dict(L=4096, N1=64, N2=64), dict(L=16384, N1=128, N2=128, split=True)])


def build_program(cfg):
    p = Prog(cfg)
    nc = p.k.finish()
    return p, nc


def make_in_maps(cfg, inputs, n_cores, x_per_core):
    depth = cfg["depth"]
    par = _stage_params(inputs, depth)
    base = dict(par)
    for n, v in _global_consts().items():
        base["c_" + n] = v
    for si, s in enumerate(cfg["seqs"]):
        for n, v in _consts_for_seq(s["L"], s["N1"], s["N2"]).items():
            base[f"c{si}_{n}"] = v
    maps = []
    for c in range(n_cores):
        m = dict(base)
        for si in range(len(cfg["seqs"])):
            m[f"xT{si}"] = np.ascontiguousarray(np.asarray(x_per_core[c][si], dtype=np.float32).T)
        maps.append(m)
    return maps


def kernel(**inputs):
    cfg = FULL_CFG
    xp = np.asarray(inputs["x_prompt"], dtype=np.float32)
    xs = np.asarray(inputs["x_sample"], dtype=np.float32)
    n = 8
    x_per_core = [[xp[c], xs[c // 4]] for c in range(n)]
    _, nc = build_program(cfg)
    maps = make_in_maps(cfg, inputs, n, x_per_core)
    res = run_bass_kernel_spmd(nc, maps, core_ids=list(range(n)))
    y_prompt = np.stack([np.ascontiguousarray(res.results[c]["yT0"].T) for c in range(n)], axis=0).astype(np.float32)
    y_sample = np.stack([np.concatenate([res.results[4 * s + r]["yq1"].T for r in range(4)], axis=0) for s in range(2)], axis=0).astype(np.float32)
    return (y_prompt, y_sample)
```
